# Optimizing a Trainium2 kernel written in Bass

```python
import math
import jax, jax.numpy as jnp
from jax import lax
import numpy as np

D_MODEL = 1024
BATCH = 32
SEQ = 256
DEPTH = 1
DEC_BATCH = 4
DEC_SEQ = 4096
PAST_LEN = 512

GRID_W = 64
H_DIFF = 4
DH_DIFF = 64
H_GQA = 8
H_KV = 2
DH_GQA = 64
G_GQA = H_GQA // H_KV
D_FF = 2816
QBLK = 128
ROPE_THETA = 10000.0
EPS = 1e-6
N_MOD = 9

DQ_DIFF = H_DIFF * 2 * DH_DIFF
DQ_GQA = H_GQA * DH_GQA
DKV_GQA = H_KV * DH_GQA
D_IN = 3 * DQ_DIFF + DQ_GQA + 2 * DKV_GQA
D_MIX = DQ_DIFF + DQ_GQA
SPLITS = (DQ_DIFF, 2 * DQ_DIFF, 3 * DQ_DIFF, 3 * DQ_DIFF + DQ_GQA, 3 * DQ_DIFF + DQ_GQA + DKV_GQA)

kernel_name = "hybrid_diffattn_gqa_prefix_dit_step"


def lambda_init(layer_idx):
    return 0.8 - 0.6 * math.exp(-0.3 * layer_idx)


def rmsnorm(x, g):
    xf = x.astype(jnp.float32)
    y = xf * lax.rsqrt(jnp.mean(xf * xf, axis=-1, keepdims=True) + EPS)
    return (y * g.astype(jnp.float32)).astype(x.dtype)


def ada_mod(cvec, w, b):
    m = jax.nn.silu(cvec) @ w + b
    return m.reshape(*cvec.shape[:-1], N_MOD, D_MODEL)


def modulate(x, g, shift, scale):
    return rmsnorm(x, g) * (1 + scale[..., None, :]) + shift[..., None, :]


def swiglu(h, w_gu, w_down):
    g, u = jnp.split(h @ w_gu, 2, axis=-1)
    return (jax.nn.silu(g) * u) @ w_down


def ffn_sublayer(x, mod, i, g, w_gu, w_down):
    shift, scale, gate = mod[..., 3 * i, :], mod[..., 3 * i + 1, :], mod[..., 3 * i + 2, :]
    return x + 0.5 * gate[..., None, :] * swiglu(modulate(x, g, shift, scale), w_gu, w_down)


def axial_rope(n_tokens, head_dim):
    rows = n_tokens // GRID_W
    row = jnp.repeat(jnp.arange(rows), GRID_W)
    col = jnp.tile(jnp.arange(GRID_W), rows)
    half = head_dim // 2
    inv = ROPE_THETA ** (-jnp.arange(0, half, 2, dtype=jnp.float32) / half)
    ang = jnp.stack([row[:, None] * inv, col[:, None] * inv], axis=1)
    return jnp.cos(ang), jnp.sin(ang)


def apply_rope(x, cos, sin):
    B, T, H, d = x.shape
    xs = x.astype(jnp.float32).reshape(B, T, H, 2, 2, d // 4)
    x1, x2 = xs[..., 0, :], xs[..., 1, :]
    c = cos[None, :, None]
    s = sin[None, :, None]
    out = jnp.stack([x1 * c - x2 * s, x1 * s + x2 * c], axis=-2)
    return out.reshape(B, T, H, d).astype(x.dtype)


def sweep_query_blocks(fn, qs):
    T = qs[0].shape[1]
    nb = T // QBLK
    blocks = tuple(jnp.moveaxis(q.reshape(q.shape[0], nb, QBLK, *q.shape[2:]), 1, 0) for q in qs)
    out = lax.map(lambda qb: fn(*qb), blocks)
    return jnp.moveaxis(out, 0, 1).reshape(out.shape[1], T, *out.shape[3:])


def diff_attn_block(q, k, v, lam):
    s = jnp.einsum('bqhmd,bkhmd->bhmqk', q, k).astype(jnp.float32) * (DH_DIFF ** -0.5)
    p = jax.nn.softmax(s, axis=-1)
    a = p[:, :, 0] - lam * p[:, :, 1]
    return jnp.einsum('bhqk,bkhe->bqhe', a.astype(v.dtype), v)


def gqa_block(q, k, v):
    s = jnp.einsum('bqngd,bknd->bngqk', q, k).astype(jnp.float32) * (DH_GQA ** -0.5)
    p = jax.nn.softmax(s, axis=-1)
    return jnp.einsum('bngqk,bknd->bqngd', p.astype(v.dtype), v)


def mixer_projections(h, w_in, q_norm, k_norm):
    B, T, _ = h.shape
    dq, dk, dv, gq, gk, gv = jnp.split(h @ w_in, list(SPLITS), axis=-1)
    dq = dq.reshape(B, T, H_DIFF, 2, DH_DIFF)
    dk = dk.reshape(B, T, H_DIFF, 2, DH_DIFF)
    dv = dv.reshape(B, T, H_DIFF, 2 * DH_DIFF)
    gq = rmsnorm(gq.reshape(B, T, H_GQA, DH_GQA), q_norm)
    gk = rmsnorm(gk.reshape(B, T, H_KV, DH_GQA), k_norm)
    gv = gv.reshape(B, T, H_KV, DH_GQA)
    return dq, dk, dv, gq, gk, gv


def rope_pairs(x, cos, sin):
    B, T, H, M, d = x.shape
    return apply_rope(x.reshape(B, T, H * M, d), cos, sin).reshape(B, T, H, M, d)


def diff_lambda(lq1, lk1, lq2, lk2, lam_init):
    f = jnp.float32
    return (jnp.exp(jnp.sum(lq1.astype(f) * lk1.astype(f)))
            - jnp.exp(jnp.sum(lq2.astype(f) * lk2.astype(f))) + lam_init)


def mixer_output(o_diff, o_gqa, subln_g, lam_init, w_out):
    B, T = o_diff.shape[:2]
    o_diff = rmsnorm(o_diff, subln_g) * (1.0 - lam_init)
    o = jnp.concatenate([o_diff.reshape(B, T, DQ_DIFF), o_gqa.reshape(B, T, DQ_GQA)], axis=-1)
    return o @ w_out


def setup_inputs(seed: int = 0) -> dict:
    key = jax.random.key(seed)
    ks = jax.random.split(key, 32)
    f = jnp.float32

    def nrm(k, shape, scale=1.0):
        return jax.random.normal(k, shape, f) * scale

    def gain(k, shape):
        return 1.0 + 0.05 * jax.random.normal(k, shape, f)

    return {
        "x_prompt": nrm(ks[0], (BATCH, SEQ, D_MODEL)),
        "x_sample": nrm(ks[1], (DEC_BATCH, DEC_SEQ, D_MODEL)),
        "c": nrm(ks[2], (DEC_BATCH, D_MODEL)),
        "cache_diff_k": nrm(ks[3], (DEC_BATCH, DEPTH, PAST_LEN, H_DIFF, 2, DH_DIFF)),
        "cache_diff_v": nrm(ks[4], (DEC_BATCH, DEPTH, PAST_LEN, H_DIFF, 2 * DH_DIFF)),
        "cache_gqa_k": nrm(ks[5], (DEC_BATCH, DEPTH, PAST_LEN, H_KV, DH_GQA)),
        "cache_gqa_v": nrm(ks[6], (DEC_BATCH, DEPTH, PAST_LEN, H_KV, DH_GQA)),
        "c_ctx": nrm(ks[7], (D_MODEL,)),
        "w_ada": nrm(ks[8], (DEPTH, D_MODEL, N_MOD * D_MODEL), 0.5 * D_MODEL ** -0.5),
        "b_ada": nrm(ks[9], (DEPTH, N_MOD * D_MODEL), 0.02),
        "norm_ff1": gain(ks[10], (DEPTH, D_MODEL)),
        "w_ff1_gu": nrm(ks[11], (DEPTH, D_MODEL, 2 * D_FF), D_MODEL ** -0.5),
        "w_ff1_down": nrm(ks[12], (DEPTH, D_FF, D_MODEL), D_FF ** -0.5),
        "norm_mix": gain(ks[13], (DEPTH, D_MODEL)),
        "w_in": nrm(ks[14], (DEPTH, D_MODEL, D_IN), D_MODEL ** -0.5),
        "q_norm": gain(ks[15], (DEPTH, DH_GQA)),
        "k_norm": gain(ks[16], (DEPTH, DH_GQA)),
        "lambda_q1": nrm(ks[17], (DEPTH, DH_DIFF), 0.1),
        "lambda_k1": nrm(ks[18], (DEPTH, DH_DIFF), 0.1),
        "lambda_q2": nrm(ks[19], (DEPTH, DH_DIFF), 0.1),
        "lambda_k2": nrm(ks[20], (DEPTH, DH_DIFF), 0.1),
        "subln": gain(ks[21], (DEPTH, 2 * DH_DIFF)),
        "w_out": nrm(ks[22], (DEPTH, D_MIX, D_MODEL), D_MIX ** -0.5),
        "norm_ff2": gain(ks[23], (DEPTH, D_MODEL)),
        "w_ff2_gu": nrm(ks[24], (DEPTH, D_MODEL, 2 * D_FF), D_MODEL ** -0.5),
        "w_ff2_down": nrm(ks[25], (DEPTH, D_FF, D_MODEL), D_FF ** -0.5),
        "final_norm": gain(ks[26], (D_MODEL,)),
    }


def reference(x_prompt, x_sample, c, cache_diff_k, cache_diff_v, cache_gqa_k, cache_gqa_v,
              c_ctx, w_ada, b_ada, norm_ff1, w_ff1_gu, w_ff1_down, norm_mix, w_in,
              q_norm, k_norm, lambda_q1, lambda_k1, lambda_q2, lambda_k2, subln, w_out,
              norm_ff2, w_ff2_gu, w_ff2_down, final_norm):
    xc = x_prompt
    Bc, Tc = xc.shape[:2]
    new_dk, new_dv, new_gk, new_gv = [], [], [], []
    for l in range(DEPTH):
        lam_init = lambda_init(l)
        lam = diff_lambda(lambda_q1[l], lambda_k1[l], lambda_q2[l], lambda_k2[l], lam_init)
        mod = ada_mod(c_ctx, w_ada[l], b_ada[l])
        xc = ffn_sublayer(xc, mod, 0, norm_ff1[l], w_ff1_gu[l], w_ff1_down[l])
        h = modulate(xc, norm_mix[l], mod[..., 3, :], mod[..., 4, :])
        dq, dk, dv, gq, gk, gv = mixer_projections(h, w_in[l], q_norm[l], k_norm[l])
        o_diff = sweep_query_blocks(lambda q, dk=dk, dv=dv: diff_attn_block(q, dk, dv, lam), (dq,))
        o_gqa = sweep_query_blocks(lambda q, gk=gk, gv=gv: gqa_block(q, gk, gv),
                                   (gq.reshape(Bc, Tc, H_KV, G_GQA, DH_GQA),))
        xc = xc + mod[..., 5, :][..., None, :] * mixer_output(o_diff, o_gqa, subln[l], lam_init, w_out[l])
        xc = ffn_sublayer(xc, mod, 2, norm_ff2[l], w_ff2_gu[l], w_ff2_down[l])
        new_dk.append(dk)
        new_dv.append(dv)
        new_gk.append(gk)
        new_gv.append(gv)
    y_prompt = rmsnorm(xc, final_norm)
    new_diff_k = jnp.stack(new_dk, axis=1)
    new_diff_v = jnp.stack(new_dv, axis=1)
    new_gqa_k = jnp.stack(new_gk, axis=1)
    new_gqa_v = jnp.stack(new_gv, axis=1)

    xs = x_sample
    Bs, Ts = xs.shape[:2]
    cos_d, sin_d = axial_rope(Ts, DH_DIFF)
    cos_g, sin_g = axial_rope(Ts, DH_GQA)
    for l in range(DEPTH):
        lam_init = lambda_init(l)
        lam = diff_lambda(lambda_q1[l], lambda_k1[l], lambda_q2[l], lambda_k2[l], lam_init)
        mod = ada_mod(c, w_ada[l], b_ada[l])
        xs = ffn_sublayer(xs, mod, 0, norm_ff1[l], w_ff1_gu[l], w_ff1_down[l])
        h = modulate(xs, norm_mix[l], mod[..., 3, :], mod[..., 4, :])
        dq, dk, dv, gq, gk, gv = mixer_projections(h, w_in[l], q_norm[l], k_norm[l])
        dq = rope_pairs(dq, cos_d, sin_d)
        dk = rope_pairs(dk, cos_d, sin_d)
        gq = apply_rope(gq, cos_g, sin_g)
        gk = apply_rope(gk, cos_g, sin_g)
        dk_all = jnp.concatenate([cache_diff_k[:, l], dk], axis=1)
        dv_all = jnp.concatenate([cache_diff_v[:, l], dv], axis=1)
        gk_all = jnp.concatenate([cache_gqa_k[:, l], gk], axis=1)
        gv_all = jnp.concatenate([cache_gqa_v[:, l], gv], axis=1)
        o_diff = sweep_query_blocks(lambda q, k=dk_all, v=dv_all: diff_attn_block(q, k, v, lam), (dq,))
        o_gqa = sweep_query_blocks(lambda q, k=gk_all, v=gv_all: gqa_block(q, k, v),
                                   (gq.reshape(Bs, Ts, H_KV, G_GQA, DH_GQA),))
        xs = xs + mod[..., 5, :][..., None, :] * mixer_output(o_diff, o_gqa, subln[l], lam_init, w_out[l])
        xs = ffn_sublayer(xs, mod, 2, norm_ff2[l], w_ff2_gu[l], w_ff2_down[l])
    y_sample = rmsnorm(xs, final_norm)

    return (y_prompt, y_sample, new_diff_k, new_diff_v, new_gqa_k, new_gqa_v)
```

```python
import math
import contextlib
import numpy as np
import ml_dtypes
import concourse.bass as bass
import concourse.mybir as mybir
from concourse.bass_utils import run_bass_kernel_spmd

F32 = mybir.dt.float32
BF16 = mybir.dt.bfloat16
AF = mybir.ActivationFunctionType
ALU = mybir.AluOpType
AX = mybir.AxisListType

D = 1024
DFF = 2816
NJ = 22
NJB = 11
DIN = 2304
EPS = 1e-6
LAM_INIT = 0.8 - 0.6 * math.exp(0.0)
ST = 512
NT = 4
NKT = 36
NKEYS = NKT * 128
PROMPT_TOK = 1024
SAMPLE_TOK = 4096
OWN_TOK = 2048
ENGS = ("pe", "act", "dve", "pool", "sp")
KVSTOP = [0]
STQ = ["act"]


class Op:
    __slots__ = ("eng", "fn", "deps", "dma", "dma_val", "dma_prev", "signal", "sigval", "order")

    def __init__(self, eng, fn, order):
        self.eng = eng
        self.fn = fn
        self.deps = []
        self.dma = None
        self.dma_val = 0
        self.dma_prev = 0
        self.signal = False
        self.sigval = 0
        self.order = order


def _okey(o):
    return ("d", o.dma) if o.dma is not None else ("e", o.eng)


class Prog:
    def __init__(self):
        self.ops = []
        self.last_w = {}
        self.readers = {}
        self.dma_cnt = {}

    def op(self, eng, fn, reads=(), writes=(), dma=None):
        o = Op(eng, fn, len(self.ops))
        cand = {}

        def add(d):
            if d is None:
                return
            k = _okey(d)
            c = cand.get(k)
            if c is None or d.order > c.order:
                cand[k] = d

        for k in reads:
            add(self.last_w.get(k))
        for k in writes:
            add(self.last_w.get(k))
            for r in self.readers.get(k, {}).values():
                add(r)
        if dma is not None:
            o.dma = dma
            o.dma_prev = self.dma_cnt.get(dma, 0)
            o.dma_val = o.dma_prev + 16
            self.dma_cnt[dma] = o.dma_val
        for k in writes:
            self.last_w[k] = o
            self.readers[k] = {}
        ok = _okey(o)
        for k in reads:
            self.readers.setdefault(k, {})[ok] = o
        o.deps = list(cand.values())
        self.ops.append(o)
        return o

    def finalize(self):
        for o in self.ops:
            for d in o.deps:
                if d.dma is None:
                    if d.eng == "pe" and o.eng == "pe" and o.dma is None:
                        continue
                    d.signal = True
        cnt = {e: 0 for e in ENGS}
        for o in self.ops:
            if o.dma is None and o.signal:
                cnt[o.eng] += 1
                o.sigval = cnt[o.eng]

    def emit(self, block, sems, dma_sems):
        per = {e: [o for o in self.ops if o.eng == e] for e in ENGS}
        reg = {"pe": block.tensor, "act": block.scalar, "dve": block.vector,
               "pool": block.gpsimd, "sp": block.sync}

        def make(ename):
            def body(eh):
                known = {}
                mine = set()
                for o in per[ename]:
                    need = {}
                    for d in o.deps:
                        if d.dma is not None:
                            key, val = ("d", d.dma), d.dma_val
                        else:
                            if d.eng == "pe" and ename == "pe" and o.dma is None:
                                continue
                            key, val = ("e", d.eng), d.sigval
                        if need.get(key, 0) < val:
                            need[key] = val
                    if o.dma is not None and o.dma_prev > 0:
                        key = ("d", o.dma)
                        if need.get(key, 0) < o.dma_prev:
                            need[key] = o.dma_prev
                    for key, val in need.items():
                        if known.get(key, 0) >= val:
                            continue
                        known[key] = val
                        s = dma_sems[key[1]] if key[0] == "d" else sems[key[1]]
                        eh.wait_ge(s, val)
                    ins = o.fn(eh)
                    if o.dma is not None:
                        ins.then_inc(dma_sems[o.dma], 16)
                        mine.add(o.dma)
                    elif o.signal:
                        ins.then_inc(sems[ename], 1)
                for name in sorted(mine):
                    total = self.dma_cnt[name]
                    if known.get(("d", name), 0) < total:
                        eh.wait_ge(dma_sems[name], total)
            return body

        for e in ENGS:
            if per[e]:
                reg[e](make(e))


def build_nc(do_prompt=True, do_sample=True, n_prompt_st=2, n_s1=8, n_s2=4, dbg=None, stop_after=99):
    nc = bass.Bass("TRN2", target_bir_lowering=False)
    P = Prog()
    es = contextlib.ExitStack()

    def din(name, shape, dt=F32):
        return nc.dram_tensor(name, list(shape), dt, kind="ExternalInput").ap()

    def dout(name, shape, dt=F32):
        return nc.dram_tensor(name, list(shape), dt, kind="ExternalOutput").ap()

    def dscr(name, shape, dt=BF16):
        return nc.dram_tensor(name, list(shape), dt, kind="Internal").ap()

    xp = din("xp", [PROMPT_TOK, D])
    xs = din("xs", [SAMPLE_TOK, D])
    sp_in = din("sparams", [40, 128])
    cdk = din("cdk", [512, 512])
    cdv = din("cdv", [512, 512])
    cgk = din("cgk", [512, 128])
    cgv = din("cgv", [512, 128])
    w_ada = din("w_ada", [D, 9 * D])
    b_ada = din("b_ada", [1, 9 * D])
    w_gu = [din("w_ff1_gu", [D, 2 * DFF]), din("w_ff2_gu", [D, 2 * DFF])]
    w_dn = [din("w_ff1_down", [DFF, D]), din("w_ff2_down", [DFF, D])]
    w_in = din("w_in", [D, DIN])
    w_out = din("w_out", [D, D])
    qk_norm = din("qk_norm", [1, 128])
    lam_p = din("lam_p", [1, 256])
    subln = din("subln", [1, 128])
    fnorm = din("final_norm", [1, D])
    rope_c = din("rope_c", [SAMPLE_TOK, 64])
    rope_s = din("rope_s", [SAMPLE_TOK, 64])
    ident_f = din("ident_f", [128, 128])
    ident_b = din("ident_b", [128, 128], BF16)

    y_p = dout("y_p", [PROMPT_TOK, D])
    y_s = dout("y_s", [OWN_TOK, D])
    o_dk = dout("o_dk", [PROMPT_TOK, 512])
    o_dv = dout("o_dv", [PROMPT_TOK, 512])
    o_gk = dout("o_gk", [PROMPT_TOK, 128])
    o_gv = dout("o_gv", [PROMPT_TOK, 128])

    s_gu = [dscr(f"s_gu{i}", [NJB, 128, 8, 512]) for i in (1, 2)]
    s_dn = [dscr(f"s_dn{i}", [DFF, D]) for i in (1, 2)]
    s_in = dscr("s_in", [5, 128, 8, 512])
    s_out = dscr("s_out", [2, 128, 8, 512])
    s_kt = dscr("s_kt", [5, 128, NKEYS])
    s_vd = dscr("s_vd", [4, 128, NKT, 128])
    s_vg = dscr("s_vg", [2, 128, NKT, 128])
    s_x1 = dscr("s_x1", [OWN_TOK, D], F32)
    s_mod = dscr("s_mod", [2, 9 * D], F32)
    dbg_out = dout("dbg", dbg) if dbg is not None else None

    def sb(name, shape, dt=F32):
        return es.enter_context(nc.sbuf_tensor(name, list(shape), dt))

    XB = [sb("X0", [128, NT, D]), sb("X1", [128, NT, D])]
    xcur = [0]
    XN = sb("XN", [128, NT, D])
    HT = sb("HT", [128, 8, ST], BF16)
    ARENA = sb("ARENA", [128, 13312], BF16)
    HID = ARENA[:, 0:NJ * ST].rearrange("p (j n) -> p j n", n=ST)
    SG = sb("SG", [128, 2, ST])
    AR = sb("AR", [128, 4, 8 * 512], BF16)
    BR = sb("BR", [128, 3, 2 * 512], BF16)
    QB = sb("QB", [128, NT, 512], BF16)
    QT = sb("QT", [128, 16, ST], BF16)
    KT = sb("KT", [128, 5, ST], BF16)
    VD = sb("VD", [128, NT, 4, 128], BF16)
    VG = sb("VG", [128, NT, 2, 128], BF16)
    ONESB = sb("ONESB", [128, 128], BF16)
    ONESF = sb("ONESF", [128, 128])
    SGC = sb("SGC", [128, 1])
    EPSC = sb("EPSC", [128, 1])
    ACCS = sb("ACCS", [128, 2, 512])
    OT = sb("OT", [128, 8, ST], BF16)
    GATE = sb("GATE", [128, 3, D])
    FGB = sb("FGB", [128, D])
    SGB = sb("SGB", [128, 128])
    QKNB = sb("QKNB", [128, 2, 64])
    LPB = sb("LPB", [128, 4, 64])
    IDF = sb("IDF", [128, 128])
    IDB = sb("IDB", [128, 128], BF16)
    SPR = sb("SPR", [40, 128])
    SPT = sb("SPT", [128, 40])
    MR = sb("MR", [72, 128])
    MODT = sb("MODT", [128, 9, 8])
    SC = sb("SC", [128, 8, 2])
    GS = sb("GS", [128, 3, 8])
    SM = sb("SM", [128, 64])
    SS = sb("SS", [128, 64])
    AS = sb("AS", [128, 64])
    NEGH = sb("NEGH", [128, 64])
    JUNK = sb("JUNK", [128, D], BF16)
    RC = sb("RC", [128, NT, 64])
    RS = sb("RS", [128, NT, 64])
    RT = sb("RT", [128, 2, 512])
    ON = sb("ON", [128, 3, 512])
    MB = sb("MB", [2, 2, 256])
    BAB = sb("BAB", [2, 2, 256])

    R0 = XN[:, 0:2, :].rearrange("p a (b n) -> p (a b) n", n=512)
    R1 = XN[:, 2:4, :].rearrange("p a (b n) -> p (a b) n", n=512)
    R0K = [("XN", 0), ("XN", 1)]
    R1K = [("XN", 2), ("XN", 3)]
    XNK = R0K + R1K

    KVS_K = [ARENA[:, i * 2304:(i + 1) * 2304] for i in range(2)]
    KVS_V = [ARENA[:, 4608 + i * 2304: 4608 + (i + 1) * 2304] for i in range(2)]
    PT2 = ARENA[:, 9216:9216 + 4096].rearrange("p (b j n) -> p b j n", j=2, n=512)
    ARENAK = [("HID", j) for j in range(NJ)]

    PSA = es.enter_context(nc.psum_tensor("PSA", [128, 8, 512], F32))
    PSB = [PSA[:, i, :] for i in range(8)]
    PSH = [PSA[:, i, :].bitcast(BF16) for i in range(8)]

    sems = {e: es.enter_context(nc.semaphore(f"sem_{e}")) for e in ENGS}
    dma_names = (["pc%d" % i for i in range(8)] + ["a%d" % i for i in range(4)] + ["b%d" % i for i in range(3)]
                 + ["wa0", "wa1", "xl", "xl0", "xl1", "m0", "m1", "m2", "m3", "st0", "st1", "st2", "st3", "st4", "st5", "st6", "st7", "st8", "st9", "st10", "st11",
                    "kk0", "kk1", "kv0", "kv1", "kvw", "x1w", "tbl", "md"])
    dma_sems = {n: es.enter_context(nc.semaphore("dsem_" + n)) for n in dma_names}

    def mm(e, out, lhsT, rhs, start=True, stop=True, **kw):
        return e.matmul(out, lhsT, rhs, start=start, stop=stop, **kw)

    def PK(i):
        return ("PS", i)

    pc_i = [0]

    def precast(out_ap, in_ap, wkey):
        name = "pc%d" % (pc_i[0] % 8)
        pc_i[0] += 1
        P.op("pool", lambda e: e.dma_start(out=out_ap, in_=in_ap), writes=[wkey], dma=name)

    def precast_gu(i):
        for jb in range(NJB):
            for part in range(2):
                c0 = part * DFF + jb * 256
                precast(s_gu[i][jb, :, :, part * 256:(part + 1) * 256],
                        w_gu[i][:, c0:c0 + 256].rearrange("(kc p) c -> p kc c", p=128), ("s_gu", i, jb, part))

    def precast_dn(i):
        for h in range(2):
            precast(s_dn[i][h * 1408:(h + 1) * 1408, :], w_dn[i][h * 1408:(h + 1) * 1408, :], ("s_dn", i, h))

    def precast_mix():
        for b in range(5):
            n = 512 if b < 4 else 256
            precast(s_in[b, :, :, 0:n], w_in[:, b * 512:b * 512 + n].rearrange("(kc p) c -> p kc c", p=128), ("s_in", b))
        for b in range(2):
            precast(s_out[b], w_out[:, b * 512:(b + 1) * 512].rearrange("(kc p) c -> p kc c", p=128), ("s_out", b))

    m_i = [0]

    def ld(out_ap, in_ap, wkeys, rkeys=(), q="sp", dma=None):
        if dma is None:
            dma = "m%d" % (m_i[0] % 4)
            m_i[0] += 1
        return P.op(q, lambda e: e.dma_start(out=out_ap, in_=in_ap), reads=list(rkeys), writes=list(wkeys), dma=dma)

    ld(IDF[:, :], ident_f[:, :], ["IDF"])
    ld(IDB[:, :], ident_b[:, :], ["IDB"])
    ld(SPR[:, :], sp_in[:, :], ["SPR"])
    ld(LPB[:, :, :].rearrange("p a d -> p (a d)"), lam_p.rearrange("a d -> (a d)").partition_broadcast(128), ["LPB"])
    ld(QKNB[:, :, :].rearrange("p a d -> p (a d)"), qk_norm.rearrange("a d -> (a d)").partition_broadcast(128), ["QKNB"])
    ld(SGB[:, :], subln.rearrange("a d -> (a d)").partition_broadcast(128), ["SGB"])
    ld(FGB[:, :], fnorm.rearrange("a d -> (a d)").partition_broadcast(128), ["FGB"])

    precast_gu(0)
    precast_dn(0)

    P.op("pool", lambda e: e.memset(NEGH[:, :], -0.5), writes=["NEGH"])
    P.op("pool", lambda e: e.memset(ONESB[:, :], 1.0), writes=["ONESB"])
    P.op("dve", lambda e: e.memset(QT[:, :, :], 0.0), writes=[("QT", c) for c in range(8)])
    P.op("pool", lambda e: e.memset(ONESF[:, :], 1.0), writes=["ONESF"])
    P.op("pool", lambda e: e.memset(EPSC[:, :], EPS), writes=["EPSC"])
    ld(SGC[:, :], subln.rearrange("a d -> d a"), ["SGC"])
    P.op("dve", lambda e: e.tensor_scalar(SGC[:, :], SGC[:, :], 1.0 - LAM_INIT, None, ALU.mult), reads=["SGC"], writes=["SGC"])

    P.op("pe", lambda e: e.transpose(PSB[0][:, 0:40], SPR[:, :], IDF[0:40, 0:40]),
         reads=["SPR", "IDF"], writes=[PK(0)])
    P.op("dve", lambda e: e.tensor_copy(SPT[:, :], PSB[0][:, 0:40]), reads=[PK(0)], writes=["SPT"])
    P.op("act", lambda e: e.activation(SC[:, :, :].rearrange("p kc g -> p g kc"),
                                       SPT[:, 0:16].rearrange("p (g kc) -> p g kc", g=2), AF.Silu),
         reads=["SPT"], writes=["SC"])

    P.op("dve", lambda e: e.tensor_tensor(out=RT[:, 0, 0:128].rearrange("p (a d) -> p a d", a=2),
                                          in0=LPB[:, :, :].rearrange("p (a b) d -> p a b d", b=2)[:, :, 0, :],
                                          in1=LPB[:, :, :].rearrange("p (a b) d -> p a b d", b=2)[:, :, 1, :], op=ALU.mult),
         reads=["LPB"], writes=[("RT", 0)])
    P.op("dve", lambda e: e.tensor_reduce(out=SM[:, 2:4], in_=RT[:, 0, 0:128].rearrange("p (a d) -> p a d", a=2),
                                          axis=AX.X, op=ALU.add), reads=[("RT", 0)], writes=["SM_l"])
    P.op("act", lambda e: e.activation(SM[:, 4:6], SM[:, 2:4], AF.Exp), reads=["SM_l"], writes=["SM_e"])
    P.op("dve", lambda e: e.tensor_tensor(out=SM[:, 1:2], in0=SM[:, 5:6], in1=SM[:, 4:5], op=ALU.subtract),
         reads=["SM_e"], writes=["SM_d"])
    P.op("dve", lambda e: e.tensor_scalar(SM[:, 0:1], SM[:, 1:2], -LAM_INIT, None, ALU.add),
         reads=["SM_d"], writes=["NLAM"])
    P.op("dve", lambda e: e.tensor_scalar(SGB[:, :], SGB[:, :], 1.0 - LAM_INIT, None, ALU.mult),
         reads=["SGB"], writes=["SGB"])
    NLAM = SM[:, 0:1]

    WAV = [XN[:, 0:2, :].rearrange("p a d -> p (a d)").rearrange("p (kc n) -> p kc n", n=256),
           XN[:, 2:4, :].rearrange("p a d -> p (a d)").rearrange("p (kc n) -> p kc n", n=256)]
    WAK = [R0K, R1K]
    for blk in range(36):
        slot = blk % 2
        c0 = blk * 256
        ld(WAV[slot], w_ada[:, c0:c0 + 256].rearrange("(kc p) c -> p kc c", p=128), WAK[slot], dma="wa%d" % slot)
        ld(BAB[:, slot, 0:256], b_ada[:, c0:c0 + 256].rearrange("a d -> (a d)").partition_broadcast(2), [("BAB", slot)])
        for kc in range(8):
            P.op("pe", (lambda e, slot=slot, kc=kc: mm(e, PSB[2 + slot][0:2, 0:256], SC[:, kc, :], WAV[slot][:, kc, :],
                                                     start=(kc == 0), stop=(kc == 7))),
                 reads=WAK[slot] + ["SC"], writes=[PK(2 + slot)])
        P.op("dve", (lambda e, slot=slot: e.tensor_tensor(out=MB[:, slot, 0:256], in0=PSB[2 + slot][0:2, 0:256],
                                                          in1=BAB[:, slot, 0:256], op=ALU.add)),
             reads=[PK(2 + slot), ("BAB", slot)], writes=[("MB", slot)])
        ld(s_mod[:, c0:c0 + 256], MB[:, slot, 0:256], ["s_mod"], rkeys=[("MB", slot)], q="act", dma="md")

    def load_group(g):
        for i, m in enumerate((2, 5, 8)):
            ld(GATE[:, i, :], s_mod[g, m * D:(m + 1) * D].partition_broadcast(128), [("GATE", i)], rkeys=["s_mod"])
            if m != 5:
                P.op("dve", (lambda e, i=i: e.tensor_scalar(GATE[:, i, :], GATE[:, i, :], 0.5, None, ALU.mult)),
                     reads=[("GATE", i)], writes=[("GATE", i)])
        ld(MR[:, :], s_mod[g, :].rearrange("(r p) -> r p", p=128), ["MR"], rkeys=["s_mod"])
        P.op("pe", lambda e: e.transpose(PSB[0][:, 0:72], MR[:, :], IDF[0:72, 0:72]), reads=["MR", "IDF"], writes=[PK(0)])
        P.op("dve", lambda e: e.tensor_copy(MODT[:, :, :].rearrange("p m c -> p (m c)"), PSB[0][:, 0:72]),
             reads=[PK(0)], writes=["MODT"])
        for i in range(3):
            P.op("dve", (lambda e, i=i: e.scalar_tensor_tensor(out=GS[:, i, :], in0=MODT[:, 3 * i + 1, :], scalar=1.0,
                                                                in1=SPT[:, 16 + 8 * i:24 + 8 * i], op0=ALU.add, op1=ALU.mult)),
                 reads=["MODT", "SPT"], writes=["GS"])

    a_i = [0]
    b_i = [0]

    def load_A(src_ap, rkeys, n=512):
        slot = a_i[0] % 4
        a_i[0] += 1
        P.op("sp", lambda e: e.dma_start(out=AR[:, slot, :].rearrange("p (k n) -> p k n", n=512)[:, :, 0:n], in_=src_ap[:, :, 0:n]),
             reads=list(rkeys), writes=[("A", slot)], dma="a%d" % slot)
        return AR[:, slot, :].rearrange("p (k n) -> p k n", n=512), ("A", slot)

    def load_B(i, jb, fh):
        slot = b_i[0] % 3
        b_i[0] += 1
        src = s_dn[i][jb * 256:(jb + 1) * 256, fh * 512:(fh + 1) * 512].rearrange("(jj p) f -> p jj f", p=128)
        P.op("sp", lambda e: e.dma_start(out=BR[:, slot, :].rearrange("p (jj f) -> p jj f", jj=2), in_=src),
             reads=[("s_dn", i, 0), ("s_dn", i, 1)], writes=[("B", slot)], dma="b%d" % slot)
        return BR[:, slot, :].rearrange("p (jj f) -> p jj f", jj=2), ("B", slot)

    st_i = [0]

    def store(out_ap, in_ap, rkeys, wkeys=()):
        name = "st%d" % (st_i[0] % 12)
        st_i[0] += 1
        P.op(STQ[0], lambda e: e.dma_start(out=out_ap, in_=in_ap), reads=list(rkeys), writes=list(wkeys), dma=name)

    def rstd_from_ss(n, scale, outcols):
        P.op("dve", lambda e: e.tensor_scalar(SS[:, 32:32 + n], SS[:, 0:n], scale, EPS, ALU.mult, ALU.add),
             reads=["SS"], writes=["VE"])
        P.op("pool", lambda e: e.tensor_tensor(out=SM[:, outcols:outcols + n], in0=SS[:, 32:32 + n], in1=NEGH[:, 0:n], op=ALU.pow),
             reads=["VE", "NEGH"], writes=["RSTD"])

    def norm_stage(i):
        Xc = XB[xcur[0]]
        xk = xcur[0]
        for t in range(NT):
            P.op("act", (lambda e, t=t: e.activation(JUNK[:, :], Xc[:, t, :], AF.Square, accum_out=SS[:, t:t + 1])),
                 reads=[("X", xk, t)], writes=["SS"])
        rstd_from_ss(NT, 1.0 / D, 8)
        for t in range(NT):
            P.op("act", (lambda e, t=t: e.activation(XN[:, t, :], Xc[:, t, :], AF.Copy, scale=SM[:, 8 + t:9 + t])),
                 reads=[("X", xk, t), "RSTD"], writes=[("XN", t)])
        for c in range(8):
            bank = c % 2
            for t in range(NT):
                P.op("pe", (lambda e, t=t, c=c, bank=bank: e.transpose(PSB[bank][:, t * 128:(t + 1) * 128],
                                                                      XN[:, t, c * 128:(c + 1) * 128], IDF[:, :])),
                     reads=[("XN", t), "IDF"], writes=[PK(bank)])
            P.op("dve", (lambda e, c=c, bank=bank, i=i: e.tensor_scalar(HT[:, c, :], PSB[bank], GS[:, i, c:c + 1],
                                                                        MODT[:, 3 * i, c:c + 1], ALU.mult, ALU.add)),
                 reads=[PK(bank), "GS", "MODT"], writes=[("HT", c)])

    def resid(gate_i, fh):
        Xc = XB[xcur[0]]
        xk = xcur[0]
        for t in range(NT):
            P.op("dve", (lambda e, t=t: e.tensor_tensor(out=RT[:, t % 2, :], in0=PSB[4 + t],
                                                        in1=GATE[:, gate_i, fh * 512:(fh + 1) * 512], op=ALU.mult)),
                 reads=[PK(4 + t), ("GATE", gate_i)], writes=[("RT", t % 2)])
            P.op("dve", (lambda e, t=t: e.tensor_tensor(out=Xc[:, t, fh * 512:(fh + 1) * 512], in0=Xc[:, t, fh * 512:(fh + 1) * 512],
                                                        in1=RT[:, t % 2, :], op=ALU.add)),
                 reads=[("RT", t % 2), ("X", xk, t)], writes=[("X", xk, t)])

    def ffn_stage(i, gate_i):
        for jb in range(NJB):
            A, akey = load_A(s_gu[i][jb], [("s_gu", i, jb, 0), ("s_gu", i, jb, 1)])
            for jj in range(2):
                j = 2 * jb + jj
                pg, pu = (j % 2) * 2, (j % 2) * 2 + 1
                for kc in range(8):
                    P.op("pe", (lambda e, kc=kc, jj=jj, pg=pg, A=A: mm(e, PSB[pg], A[:, kc, jj * 128:(jj + 1) * 128], HT[:, kc, :],
                                                                       start=(kc == 0), stop=(kc == 7))),
                         reads=[akey, ("HT", kc)], writes=[PK(pg)])
                for kc in range(8):
                    P.op("pe", (lambda e, kc=kc, jj=jj, pu=pu, A=A: mm(e, PSB[pu], A[:, kc, 256 + jj * 128:256 + (jj + 1) * 128], HT[:, kc, :],
                                                                       start=(kc == 0), stop=(kc == 7))),
                         reads=[akey, ("HT", kc)], writes=[PK(pu)])
                sgs = j % 2
                P.op("act", (lambda e, sgs=sgs, pg=pg: e.activation(SG[:, sgs, :], PSB[pg], AF.Silu)),
                     reads=[PK(pg)], writes=[("SG", sgs)])
                P.op("dve", (lambda e, sgs=sgs, pu=pu, j=j: e.tensor_tensor(out=HID[:, j, :], in0=SG[:, sgs, :], in1=PSB[pu], op=ALU.mult)),
                     reads=[("SG", sgs), PK(pu)], writes=[("HID", j)])
        for fh in range(2):
            for jb in range(NJB):
                B, bkey = load_B(i, jb, fh)
                for jj in range(2):
                    j = 2 * jb + jj
                    for t in range(NT):
                        P.op("pe", (lambda e, t=t, j=j, jj=jj, B=B: mm(e, PSB[4 + t], HID[:, j, t * 128:(t + 1) * 128], B[:, jj, :],
                                                                       start=(j == 0), stop=(j == NJ - 1))),
                             reads=[bkey, ("HID", j)], writes=[PK(4 + t)])
            resid(gate_i, fh)

    def transposes_bf(dst, dst_key, src_fn, src_keys, nch, c0=0, split=False):
        for c in range(nch):
            bank = c % 2
            for t in range(NT):
                P.op("pe", (lambda e, t=t, c=c, bank=bank: e.transpose(PSH[bank][:, t * 128:(t + 1) * 128], src_fn(t, c), IDB[:, :])),
                     reads=list(src_keys) + ["IDB"], writes=[PK(bank)])
            if split:
                for hf in range(2):
                    P.op("act", (lambda e, c=c, bank=bank, hf=hf: e.activation(dst[64 * hf:64 * hf + 64, 2 * (c0 + c) + hf, :],
                                                                              PSH[bank][64 * hf:64 * hf + 64, 0:ST], AF.Copy)),
                         reads=[PK(bank)], writes=[(dst_key, c0 + c)])
            else:
                P.op("act", (lambda e, c=c, bank=bank: e.activation(dst[:, c0 + c, :], PSH[bank][:, 0:ST], AF.Copy)),
                     reads=[PK(bank)], writes=[(dst_key, c0 + c)])

    def win_block(b):
        n = 512 if b < 4 else 256
        A, akey = load_A(s_in[b], [("s_in", b)], n)
        for t in range(NT):
            for kc in range(8):
                P.op("pe", (lambda e, t=t, kc=kc, A=A, n=n: mm(e, PSB[4 + t][:, 0:n], HT[:, kc, t * 128:(t + 1) * 128], A[:, kc, 0:n],
                                                               start=(kc == 0), stop=(kc == 7))),
                     reads=[akey, ("HT", kc)], writes=[PK(4 + t)])

    def rope(src_fn, src_keys, nh, dst_fn, dst_keys, t, addview=None):
        P.op("dve", (lambda e: e.tensor_tensor(out=RT[:, 0, 0:nh * 64].rearrange("p (h d) -> p h d", d=64),
                                               in0=src_fn().rearrange("p (h d) -> p h d", d=64),
                                               in1=RC[:, t, :].unsqueeze(1).broadcast_to([128, nh, 64]), op=ALU.mult)),
             reads=list(src_keys) + ["RC"], writes=[("RT", 0)])
        for b in range(2):
            P.op("dve", (lambda e, b=b: e.tensor_tensor(
                out=RT[:, 1, 0:nh * 64].rearrange("p (h a b i) -> p h a b i", a=2, b=2, i=16)[:, :, :, b, :],
                in0=src_fn().rearrange("p (h a b i) -> p h a b i", a=2, b=2, i=16)[:, :, :, 1 - b, :],
                in1=RS[:, t, :].rearrange("p (a b i) -> p a b i", a=2, b=2)[:, :, b, :].unsqueeze(1).broadcast_to([128, nh, 2, 16]),
                op=ALU.mult)),
                 reads=list(src_keys) + ["RS"], writes=[("RT", 1)])
        av = addview if addview is not None else (lambda a: a.rearrange("p (h d) -> p h d", d=64))
        P.op("dve", (lambda e: e.tensor_tensor(out=dst_fn(), in0=av(RT[:, 0, 0:nh * 64]),
                                                in1=av(RT[:, 1, 0:nh * 64]), op=ALU.add)),
             reads=[("RT", 0), ("RT", 1)], writes=list(dst_keys))

    def headnorm(R, Rk, col0, nh, gi):
        for t in range(NT):
            P.op("act", (lambda e, t=t: e.activation(RT[:, 0, 0:nh * 64], R[:, t, col0:col0 + nh * 64], AF.Square)),
                 reads=list(Rk), writes=[("RT", 0)])
            P.op("dve", (lambda e, t=t: e.tensor_reduce(out=SS[:, t * nh:(t + 1) * nh],
                                                        in_=RT[:, 0, 0:nh * 64].rearrange("p (h d) -> p h d", d=64),
                                                        axis=AX.X, op=ALU.add)),
                 reads=[("RT", 0)], writes=["SS"])
        rstd_from_ss(NT * nh, 1.0 / 64, 16)
        for t in range(NT):
            P.op("dve", (lambda e, t=t: e.tensor_tensor(
                out=R[:, t, col0:col0 + nh * 64].rearrange("p (h d) -> p h d", d=64),
                in0=R[:, t, col0:col0 + nh * 64].rearrange("p (h d) -> p h d", d=64),
                in1=SM[:, 16 + t * nh:16 + (t + 1) * nh].unsqueeze(2).broadcast_to([128, nh, 64]), op=ALU.mult)),
                 reads=list(Rk) + ["RSTD"], writes=list(Rk))
            P.op("dve", (lambda e, t=t: e.tensor_tensor(
                out=R[:, t, col0:col0 + nh * 64].rearrange("p (h d) -> p h d", d=64),
                in0=R[:, t, col0:col0 + nh * 64].rearrange("p (h d) -> p h d", d=64),
                in1=QKNB[:, gi, :].unsqueeze(1).broadcast_to([128, nh, 64]), op=ALU.mult)),
                 reads=list(Rk) + ["QKNB"], writes=list(Rk))

    def gq_dst(t):
        return QB[:, t, :].rearrange("p (g n d) -> p n g d", g=4, n=2)

    def mix_q(sample):
        win_block(0)
        for t in range(NT):
            if sample:
                rope(lambda t=t: PSB[4 + t], [PK(4 + t)], 8, lambda t=t: QB[:, t, :].rearrange("p (h d) -> p h d", d=64), ["QB"], t)
            else:
                P.op("act", (lambda e, t=t: e.activation(QB[:, t, :], PSB[4 + t], AF.Copy)), reads=[PK(4 + t)], writes=["QB"])
        transposes_bf(QT, "QT", lambda t, c: QB[:, t, c * 128:(c + 1) * 128], ["QB"], 4, 0, split=True)
        win_block(3)
        for t in range(NT):
            P.op("dve", (lambda e, t=t: e.tensor_copy(R1[:, t, :], PSB[4 + t])), reads=[PK(4 + t)], writes=R1K)
        headnorm(R1, R1K, 0, 8, 0)
        for t in range(NT):
            if sample:
                rope(lambda t=t: R1[:, t, :], R1K, 8, lambda t=t: gq_dst(t), ["QB"], t,
                     addview=lambda a: a.rearrange("p (n g d) -> p n g d", n=2, g=4))
            else:
                P.op("dve", (lambda e, t=t: e.tensor_copy(gq_dst(t), R1[:, t, :].rearrange("p (n g d) -> p n g d", n=2, g=4))),
                     reads=R1K, writes=["QB"])
        transposes_bf(QT, "QT", lambda t, c: QB[:, t, c * 128:(c + 1) * 128], ["QB"], 4, 4, split=True)

    def mix_kv(sample, tok0, kt0):
        win_block(1)
        for t in range(NT):
            if sample:
                rope(lambda t=t: PSB[4 + t], [PK(4 + t)], 8, lambda t=t: QB[:, t, :].rearrange("p (h d) -> p h d", d=64), ["QB"], t)
            else:
                P.op("act", (lambda e, t=t: e.activation(R0[:, t, :], PSB[4 + t], AF.Copy)), reads=[PK(4 + t)], writes=R0K)
                P.op("dve", (lambda e, t=t: e.tensor_copy(QB[:, t, :], R0[:, t, :])), reads=R0K, writes=["QB"])
        if not sample and KVSTOP[0] != 11:
            store(o_dk[tok0:tok0 + ST, :].rearrange("(t p) f -> p t f", p=128), R0, R0K)
        if KVSTOP[0] == 11:
            return
        transposes_bf(KT, "KT", lambda t, c: QB[:, t, c * 128:(c + 1) * 128], ["QB"], 4, 0)
        if KVSTOP[0] == 1:
            return
        win_block(2)
        for t in range(NT):
            if not sample:
                P.op("act", (lambda e, t=t: e.activation(R1[:, t, :], PSB[4 + t], AF.Copy)), reads=[PK(4 + t)], writes=R1K)
                P.op("dve", (lambda e, t=t: e.tensor_copy(VD[:, t, :, :], R1[:, t, :].rearrange("p (h d) -> p h d", d=128))),
                     reads=R1K, writes=["VD"])
            else:
                P.op("dve", (lambda e, t=t: e.tensor_copy(VD[:, t, :, :], PSB[4 + t].rearrange("p (h d) -> p h d", d=128))),
                     reads=[PK(4 + t)], writes=["VD"])
        if not sample:
            store(o_dv[tok0:tok0 + ST, :].rearrange("(t p) f -> p t f", p=128), R1, R1K)
        if KVSTOP[0] == 2:
            return
        win_block(4)
        for t in range(NT):
            P.op("dve", (lambda e, t=t: e.tensor_copy(R0[:, t, 0:256], PSB[4 + t][:, 0:256])), reads=[PK(4 + t)], writes=R0K)
        headnorm(R0, R0K, 0, 2, 1)
        for t in range(NT):
            if sample:
                rope(lambda t=t: R0[:, t, 0:128], R0K, 2, lambda t=t: QB[:, t, 0:128].rearrange("p (h d) -> p h d", d=64), ["QB"], t)
            else:
                P.op("act", (lambda e, t=t: e.activation(QB[:, t, 0:128], R0[:, t, 0:128], AF.Copy)), reads=R0K, writes=["QB"])
            P.op("dve", (lambda e, t=t: e.tensor_copy(VG[:, t, :, :].rearrange("p n (r d) -> p n r d", r=2),
                                                      R0[:, t, 128:256].rearrange("p (h d) -> p h d", d=64).unsqueeze(2).broadcast_to([128, 2, 2, 64]))),
                 reads=R0K, writes=["VG"])
        if not sample:
            store(o_gk[tok0:tok0 + ST, :].rearrange("(t p) f -> p t f", p=128), R0[:, :, 0:128], R0K)
            store(o_gv[tok0:tok0 + ST, :].rearrange("(t p) f -> p t f", p=128), R0[:, :, 128:256], R0K)
        transposes_bf(KT, "KT", lambda t, c: QB[:, t, 0:128], ["QB"], 1, 4)
        if sample:
            store_kv(kt0)

    def store_kv(kt0):
        store(s_kt[:, :, kt0 * 128:kt0 * 128 + ST].rearrange("c p n -> p c n"), KT[:, :, :], [("KT", c) for c in range(5)], ["s_kt"])
        for h in range(4):
            store(s_vd[h, :, kt0:kt0 + NT, :], VD[:, :, h, :], ["VD"], ["s_vd"])
        for n in range(2):
            store(s_vg[n, :, kt0:kt0 + NT, :], VG[:, :, n, :], ["VG"], ["s_vg"])

    s_i = [0]
    p_i = [0]

    def attn_unit(kind, u, subs, q0, nq, ktiles):
        nk = len(ktiles)
        info = {}
        LA = 2

        def emit_qk(ki):
            kfn, kkeys, v_ap, vkeys = ktiles[ki]
            r = s_i[0] % 3
            s_i[0] += 1
            chunks = []
            for j in range(2):
                if kind == "diff":
                    r0, chunk = 64 * subs[j], u
                else:
                    r0, chunk = 64 * u, 4 + subs[j]
                chunks.append(chunk)
                P.op("pe", (lambda e, j=j, r0=r0, chunk=chunk: mm(e, PSB[2 * r + j][:, 0:nq], kfn(),
                                                                  QT[:, 2 * chunk + r0 // 64, q0:q0 + nq])),
                     reads=list(kkeys) + [("QT", chunk)], writes=[PK(2 * r + j)])
            slot = p_i[0] % 4
            p_i[0] += 1
            P.op("act", (lambda e: e.activation(PT2[:, slot, :, 0:nq], PSA[:, 2 * r:2 * r + 2, 0:nq], AF.Exp, scale=0.125)),
                 reads=[PK(2 * r), PK(2 * r + 1)],
                 writes=[("PT", slot)] + ([("HID", jj) for jj in range(18, NJ)] if ki == 0 else []))
            if ki == 0:
                P.op("dve", (lambda e: e.tensor_copy(ACCS[:, :, 0:nq], PT2[:, slot, :, 0:nq])), reads=[("PT", slot)], writes=["ACCS", "ACCS2"])
            else:
                cs = (nq * 3) // 4
                P.op("dve", (lambda e: e.tensor_tensor(out=ACCS[:, :, 0:cs], in0=ACCS[:, :, 0:cs], in1=PT2[:, slot, :, 0:cs], op=ALU.add)),
                     reads=[("PT", slot), "ACCS"], writes=["ACCS"])
                P.op("pool", (lambda e: e.tensor_tensor(out=ACCS[:, :, cs:nq], in0=ACCS[:, :, cs:nq], in1=PT2[:, slot, :, cs:nq], op=ALU.add)),
                     reads=[("PT", slot), "ACCS2"], writes=["ACCS2"])
            info[ki] = slot

        def emit_pv(ki):
            kfn, kkeys, v_ap, vkeys = ktiles[ki]
            slot = info[ki]
            for j in range(2):
                P.op("pe", (lambda e, j=j: mm(e, PSB[6 + j][:, 0:nq], v_ap, PT2[:, slot, j, 0:nq], start=(ki == 0), stop=(ki == nk - 1))),
                     reads=[("PT", slot)] + list(vkeys), writes=[PK(6 + j)])

        for kk in range(nk + LA):
            if kk < nk:
                emit_qk(kk)
            if kk - LA >= 0:
                emit_pv(kk - LA)
        for j in range(2):
            P.op("pe", (lambda e, j=j: mm(e, PSB[j][:, 0:nq], ONESF[:, :], ACCS[:, j, 0:nq])), reads=["ACCS", "ACCS2", "ONESF"], writes=[PK(j)])

        for j in range(2):
            P.op("dve", (lambda e, j=j: e.reciprocal(ON[:, 2, 0:nq], PSB[j][:, 0:nq])), reads=[PK(j)], writes=[("ON", 2)])
            if kind == "diff":
                P.op("dve", (lambda e, j=j: e.tensor_tensor(out=ON[:, j, 0:nq], in0=PSB[6 + j][:, 0:nq], in1=ON[:, 2, 0:nq], op=ALU.mult)),
                     reads=[PK(6 + j), ("ON", 2)], writes=[("ON", j)])
            else:
                hq = 4 * u + subs[j]
                r0, chunk = 64 * (hq % 2), 4 + hq // 2
                P.op("dve", (lambda e, j=j, r0=r0, chunk=chunk: e.tensor_tensor(
                    out=OT[r0:r0 + 64, chunk, q0:q0 + nq], in0=PSB[6 + j][r0:r0 + 64, 0:nq], in1=ON[r0:r0 + 64, 2, 0:nq], op=ALU.mult)),
                     reads=[PK(6 + j), ("ON", 2)], writes=[("OT", chunk)])
        if kind == "gqa":
            return
        P.op("dve", lambda e: e.scalar_tensor_tensor(out=ON[:, 0, 0:nq], in0=ON[:, 1, 0:nq], scalar=NLAM, in1=ON[:, 0, 0:nq],
                                                     op0=ALU.mult, op1=ALU.add),
             reads=[("ON", 0), ("ON", 1), "NLAM"], writes=[("ON", 0)])
        P.op("act", lambda e: e.activation(ON[:, 2, 0:nq], ON[:, 0, 0:nq], AF.Square), reads=[("ON", 0)], writes=[("ON", 2)])
        P.op("pe", lambda e: mm(e, PSB[2][:, 0:nq], ONESF[:, :], ON[:, 2, 0:nq]), reads=[("ON", 2), "ONESF"], writes=[PK(2)])
        P.op("act", lambda e: e.activation(ON[:, 1, 0:nq], PSB[2][:, 0:nq], AF.Ln, scale=1.0 / 128, bias=EPSC[:, 0:1]),
             reads=[PK(2), "EPSC"], writes=[("ON", 1)])
        P.op("act", lambda e: e.activation(ON[:, 1, 0:nq], ON[:, 1, 0:nq], AF.Exp, scale=-0.5), reads=[("ON", 1)], writes=[("ON", 1)])
        P.op("dve", lambda e: e.scalar_tensor_tensor(out=OT[:, u, q0:q0 + nq], in0=ON[:, 0, 0:nq], scalar=SGC[:, 0:1], in1=ON[:, 1, 0:nq],
                                                     op0=ALU.mult, op1=ALU.mult),
             reads=[("ON", 0), ("ON", 1), "SGC"], writes=[("OT", u)])

    def attn_prompt():
        for bb in range(2):
            for h in range(4):
                kts = []
                for t in (2 * bb, 2 * bb + 1):
                    kts.append(((lambda t=t, h=h: KT[:, h, t * 128:(t + 1) * 128]), [("KT", h)],
                                VD[:, t, h, :], ["VD"]))
                attn_unit("diff", h, (0, 1), bb * 256, 256, kts)
            for n in range(2):
                for gp in range(2):
                    kts = []
                    for t in (2 * bb, 2 * bb + 1):
                        kts.append(((lambda t=t: KT[:, 4, t * 128:(t + 1) * 128]), [("KT", 4)],
                                    VG[:, t, n, :], ["VG"]))
                    attn_unit("gqa", n, (2 * gp, 2 * gp + 1), bb * 256, 256, kts)

    kv_i = [0]

    def attn_sample():
        units = [("diff", h, (0, 1)) for h in range(4)] + [("gqa", n, (2 * gp, 2 * gp + 1)) for n in range(2) for gp in range(2)]
        for kind, u, subs in units:
            kts = []
            for half in range(2):
                buf = kv_i[0] % 2
                kv_i[0] += 1
                chunk = u if kind == "diff" else 4
                P.op("sp", (lambda e, buf=buf, chunk=chunk, half=half: e.dma_start(
                    out=KVS_K[buf], in_=s_kt[chunk, :, half * 2304:(half + 1) * 2304])),
                     reads=["s_kt"], writes=[("KVK", buf)] + [("HID", j) for j in ((0, 1, 2, 3, 4) if buf == 0 else (4, 5, 6, 7, 8))],
                     dma="kk%d" % buf)
                vs = s_vd if kind == "diff" else s_vg
                vsrc = vs[u, :, half * 18:(half + 1) * 18, :].rearrange("p k e -> p (k e)")
                P.op("sp", (lambda e, buf=buf, vsrc=vsrc: e.dma_start(out=KVS_V[buf], in_=vsrc)),
                     reads=["s_vd", "s_vg"], writes=[("KVV", buf)] + [("HID", j) for j in ((9, 10, 11, 12, 13) if buf == 0 else (13, 14, 15, 16, 17))],
                     dma="kv%d" % buf)
                for k in range(18):
                    kts.append(((lambda buf=buf, k=k: KVS_K[buf][:, k * 128:(k + 1) * 128]), [("KVK", buf)],
                                KVS_V[buf][:, k * 128:(k + 1) * 128], [("KVV", buf)]))
            attn_unit(kind, u, subs, 0, ST, kts)

    def out_proj():
        for fh in range(2):
            A, akey = load_A(s_out[fh], [("s_out", fh)])
            for t in range(NT):
                for kc in range(8):
                    P.op("pe", (lambda e, t=t, kc=kc, A=A: mm(e, PSB[4 + t], OT[:, kc, t * 128:(t + 1) * 128], A[:, kc, :],
                                                              start=(kc == 0), stop=(kc == 7))),
                         reads=[akey, ("OT", kc)], writes=[PK(4 + t)])
            resid(1, fh)

    def final_norm(y, tok0):
        Xc = XB[xcur[0]]
        xk = xcur[0]
        for t in range(NT):
            P.op("act", (lambda e, t=t: e.activation(JUNK[:, :], Xc[:, t, :], AF.Square, accum_out=SS[:, t:t + 1])),
                 reads=[("X", xk, t)], writes=["SS"])
        rstd_from_ss(NT, 1.0 / D, 8)
        for t in range(NT):
            P.op("dve", (lambda e, t=t: e.scalar_tensor_tensor(out=XN[:, t, :], in0=Xc[:, t, :], scalar=SM[:, 8 + t:9 + t],
                                                                in1=FGB[:, :], op0=ALU.mult, op1=ALU.mult)),
                 reads=[("X", xk, t), "RSTD", "FGB"], writes=[("XN", t)])
        store(y[tok0:tok0 + ST, :].rearrange("(t p) f -> p t f", p=128), XN[:, :, :], XNK)

    def load_x(src, tok0, buf):
        Xc = XB[buf]
        P.op("sp", lambda e: e.dma_start(out=Xc[:, :, :], in_=src[tok0:tok0 + ST, :].rearrange("(t p) f -> p t f", p=128)),
             reads=["s_x1"] if src is s_x1 else [], writes=[("X", buf, t) for t in range(NT)], dma="xl%d" % buf)

    def load_rope(tok0):
        P.op("sp", lambda e: e.dma_start(out=RC[:, :, :], in_=rope_c[tok0:tok0 + ST, :].rearrange("(t p) f -> p t f", p=128)),
             writes=["RC"], dma="tbl")
        P.op("sp", lambda e: e.dma_start(out=RS[:, :, :], in_=rope_s[tok0:tok0 + ST, :].rearrange("(t p) f -> p t f", p=128)),
             writes=["RS"], dma="tbl")

    first = [True]

    def maybe(fn):
        if first[0]:
            fn()

    tiles = []
    if do_prompt:
        tiles += [("prompt", si, xp, si * ST) for si in range(n_prompt_st)]
    if do_sample:
        tiles += [("s1", si, xs, si * ST) for si in range(n_s1)]
        tiles += [("s2", qi, s_x1, qi * ST) for qi in range(n_s2)]

    def prefetch(k):
        if k < len(tiles):
            ph, idx, src, off = tiles[k]
            load_x(src, off, k % 2)

    def sample_cache_prep():
        load_group(1)
        P.op("sp", lambda e: e.dma_start(out=R0, in_=cdk.rearrange("(t p) f -> p t f", p=128)), writes=R0K, dma="xl")
        for t in range(NT):
            P.op("act", (lambda e, t=t: e.activation(QB[:, t, :], R0[:, t, :], AF.Copy)), reads=R0K, writes=["QB"])
        transposes_bf(KT, "KT", lambda t, c: QB[:, t, c * 128:(c + 1) * 128], ["QB"], 4, 0)
        P.op("sp", lambda e: e.dma_start(out=R1, in_=cdv.rearrange("(t p) f -> p t f", p=128)), writes=R1K, dma="xl")
        for t in range(NT):
            P.op("dve", (lambda e, t=t: e.tensor_copy(VD[:, t, :, :], R1[:, t, :].rearrange("p (h d) -> p h d", d=128))),
                 reads=R1K, writes=["VD"])
        P.op("sp", lambda e: e.dma_start(out=R0[:, :, 0:128], in_=cgk.rearrange("(t p) f -> p t f", p=128)), writes=R0K, dma="xl")
        P.op("sp", lambda e: e.dma_start(out=R0[:, :, 128:256], in_=cgv.rearrange("(t p) f -> p t f", p=128)), writes=R0K, dma="xl")
        for t in range(NT):
            P.op("act", (lambda e, t=t: e.activation(QB[:, t, 0:128], R0[:, t, 0:128], AF.Copy)), reads=R0K, writes=["QB"])
            P.op("dve", (lambda e, t=t: e.tensor_copy(VG[:, t, :, :].rearrange("p n (r d) -> p n r d", r=2),
                                                      R0[:, t, 128:256].rearrange("p (h d) -> p h d", d=64).unsqueeze(2).broadcast_to([128, 2, 2, 64]))),
                 reads=R0K, writes=["VG"])
        transposes_bf(KT, "KT", lambda t, c: QB[:, t, 0:128], ["QB"], 1, 4)
        store_kv(0)

    prefetch(0)
    seen_phase = set()
    for k, (ph, idx, src, off) in enumerate(tiles):
        xcur[0] = k % 2
        if ph == "prompt" and "prompt" not in seen_phase:
            load_group(0)
        if ph == "s1" and "s1" not in seen_phase:
            if first[0]:
                precast_mix()
                precast_gu(1)
                precast_dn(1)
                first[0] = False
            sample_cache_prep()
        if ph == "s2" and "s1" not in seen_phase and "s2" not in seen_phase:
            load_group(1)
        seen_phase.add(ph)
        if ph == "prompt":
            tok0 = off
            steps = [lambda: norm_stage(0), lambda: (prefetch(k + 1), maybe(precast_mix)), lambda: ffn_stage(0, 0),
                     lambda: norm_stage(1), lambda: maybe(lambda: (precast_gu(1), precast_dn(1))), lambda: mix_q(False),
                     lambda: mix_kv(False, tok0, 0), attn_prompt, out_proj, lambda: norm_stage(2), lambda: ffn_stage(1, 2),
                     lambda: final_norm(y_p, tok0)]
            for kk, fn in enumerate(steps):
                if kk < stop_after:
                    fn()
            first[0] = False
        elif ph == "s1":
            load_rope(off)
            norm_stage(0)
            prefetch(k + 1)
            ffn_stage(0, 0)
            if idx >= 4:
                Xc = XB[xcur[0]]
                store(s_x1[(idx - 4) * ST:(idx - 3) * ST, :].rearrange("(t p) f -> p t f", p=128), Xc[:, :, :],
                      [("X", xcur[0], t) for t in range(NT)], ["s_x1"])
            norm_stage(1)
            mix_kv(True, off, 4 + idx * NT)
        else:
            load_rope(OWN_TOK + off)
            norm_stage(1)
            prefetch(k + 1)
            mix_q(True)
            attn_sample()
            out_proj()
            norm_stage(2)
            ffn_stage(1, 2)
            final_norm(y_s, off)
    if first[0] and not tiles:
        precast_mix()
        precast_gu(1)
        precast_dn(1)

    P.finalize()
    with nc.Block() as block:
        P.emit(block, sems, dma_sems)
    es.close()
    return nc


def _rope_tables():
    rows = SAMPLE_TOK // 64
    row = np.repeat(np.arange(rows), 64)
    col = np.tile(np.arange(64), rows)
    half = 32
    inv = (10000.0 ** (-np.arange(0, half, 2, dtype=np.float32) / half)).astype(np.float32)
    ang = np.stack([row[:, None].astype(np.float32) * inv, col[:, None].astype(np.float32) * inv], axis=1)
    c = np.cos(ang).astype(np.float32)
    s = np.sin(ang).astype(np.float32)
    C2 = np.stack([c, c], axis=2).reshape(SAMPLE_TOK, 64)
    S2 = np.stack([-s, s], axis=2).reshape(SAMPLE_TOK, 64)
    return np.ascontiguousarray(C2), np.ascontiguousarray(S2)


def make_in_maps(inp):
    f = lambda a: np.ascontiguousarray(np.asarray(a, dtype=np.float32))
    C2, S2 = _rope_tables()
    ident_f = np.eye(128, dtype=np.float32)
    ident_b = np.eye(128, dtype=np.float32).astype(ml_dtypes.bfloat16)
    shared = {
        "w_ada": f(inp["w_ada"][0]), "b_ada": f(inp["b_ada"]).reshape(1, 9 * D),
        "w_ff1_gu": f(inp["w_ff1_gu"][0]), "w_ff2_gu": f(inp["w_ff2_gu"][0]),
        "w_ff1_down": f(inp["w_ff1_down"][0]), "w_ff2_down": f(inp["w_ff2_down"][0]),
        "w_in": f(inp["w_in"][0]), "w_out": f(inp["w_out"][0]),
        "qk_norm": np.concatenate([f(inp["q_norm"][0]), f(inp["k_norm"][0])]).reshape(1, 128),
        "lam_p": np.concatenate([f(inp["lambda_q1"][0]), f(inp["lambda_k1"][0]),
                                 f(inp["lambda_q2"][0]), f(inp["lambda_k2"][0])]).reshape(1, 256),
        "subln": f(inp["subln"]).reshape(1, 128), "final_norm": f(inp["final_norm"]).reshape(1, D),
        "ident_f": ident_f, "ident_b": ident_b,
    }
    norms = np.concatenate([f(inp["norm_ff1"][0]), f(inp["norm_mix"][0]), f(inp["norm_ff2"][0])]).reshape(24, 128)
    xp_all = f(inp["x_prompt"])
    xs_all = f(inp["x_sample"])
    maps = []
    for core in range(8):
        b, half = core // 2, core % 2
        order = np.concatenate([np.arange((1 - half) * OWN_TOK, (2 - half) * OWN_TOK),
                                np.arange(half * OWN_TOK, (half + 1) * OWN_TOK)])
        cv = np.stack([f(inp["c_ctx"]), f(inp["c"])[b]]).reshape(16, 128)
        m = dict(shared)
        m["xp"] = np.ascontiguousarray(xp_all[4 * core:4 * core + 4].reshape(PROMPT_TOK, D))
        m["xs"] = np.ascontiguousarray(xs_all[b][order])
        m["sparams"] = np.ascontiguousarray(np.concatenate([cv, norms], axis=0))
        m["cdk"] = f(inp["cache_diff_k"][b, 0]).reshape(512, 512)
        m["cdv"] = f(inp["cache_diff_v"][b, 0]).reshape(512, 512)
        m["cgk"] = f(inp["cache_gqa_k"][b, 0]).reshape(512, 128)
        m["cgv"] = f(inp["cache_gqa_v"][b, 0]).reshape(512, 128)
        m["rope_c"] = np.ascontiguousarray(C2[order])
        m["rope_s"] = np.ascontiguousarray(S2[order])
        maps.append(m)
    return maps


def kernel(**inp):
    nc = build_nc()
    maps = make_in_maps(inp)
    res = run_bass_kernel_spmd(nc, maps, core_ids=list(range(8)))
    R = res.results
    y_p = np.concatenate([r["y_p"].reshape(4, 256, D) for r in R], axis=0)
    y_s = np.zeros((4, SAMPLE_TOK, D), np.float32)
    for core, r in enumerate(R):
        b, half = core // 2, core % 2
        y_s[b, half * OWN_TOK:(half + 1) * OWN_TOK] = r["y_s"]
    ndk = np.concatenate([r["o_dk"].reshape(4, 1, 256, 4, 2, 64) for r in R], axis=0)
    ndv = np.concatenate([r["o_dv"].reshape(4, 1, 256, 4, 128) for r in R], axis=0)
    ngk = np.concatenate([r["o_gk"].reshape(4, 1, 256, 2, 64) for r in R], axis=0)
    ngv = np.concatenate([r["o_gv"].reshape(4, 1, 256, 2, 64) for r in R], axis=0)
    return (y_p.astype(np.float32), y_s, ndk.astype(np.float32), ndv.astype(np.float32),
            ngk.astype(np.float32), ngv.astype(np.float32))
```

```python
import math
import contextlib
import numpy as np
import ml_dtypes
import concourse.bass as bass
import concourse.mybir as mybir
from concourse.bass_utils import run_bass_kernel_spmd

F32 = mybir.dt.float32
BF16 = mybir.dt.bfloat16
AF = mybir.ActivationFunctionType
ALU = mybir.AluOpType
AX = mybir.AxisListType

D = 1024
DFF = 2816
NJ = 22
NJB = 11
DIN = 2304
EPS = 1e-6
LAM_INIT = 0.8 - 0.6 * math.exp(0.0)
ST = 512
NT = 4
NKT = 36
NKEYS = NKT * 128
PROMPT_TOK = 1024
SAMPLE_TOK = 4096
OWN_TOK = 2048
ENGS = ("pe", "act", "dve", "pool", "sp")
KVSTOP = [0]
STQ = ["act"]


class Op:
    __slots__ = ("eng", "fn", "deps", "dma", "dma_val", "dma_prev", "signal", "sigval", "order")

    def __init__(self, eng, fn, order):
        self.eng = eng
        self.fn = fn
        self.deps = []
        self.dma = None
        self.dma_val = 0
        self.dma_prev = 0
        self.signal = False
        self.sigval = 0
        self.order = order


def _okey(o):
    return ("d", o.dma) if o.dma is not None else ("e", o.eng)


class Prog:
    def __init__(self):
        self.ops = []
        self.last_w = {}
        self.readers = {}
        self.dma_cnt = {}

    def op(self, eng, fn, reads=(), writes=(), dma=None):
        o = Op(eng, fn, len(self.ops))
        cand = {}

        def add(d):
            if d is None:
                return
            k = _okey(d)
            c = cand.get(k)
            if c is None or d.order > c.order:
                cand[k] = d

        for k in reads:
            add(self.last_w.get(k))
        for k in writes:
            add(self.last_w.get(k))
            for r in self.readers.get(k, {}).values():
                add(r)
        if dma is not None:
            o.dma = dma
            o.dma_prev = self.dma_cnt.get(dma, 0)
            o.dma_val = o.dma_prev + 16
            self.dma_cnt[dma] = o.dma_val
        for k in writes:
            self.last_w[k] = o
            self.readers[k] = {}
        ok = _okey(o)
        for k in reads:
            self.readers.setdefault(k, {})[ok] = o
        o.deps = list(cand.values())
        self.ops.append(o)
        return o

    def finalize(self):
        for o in self.ops:
            for d in o.deps:
                if d.dma is None:
                    if d.eng == "pe" and o.eng == "pe" and o.dma is None:
                        continue
                    d.signal = True
        cnt = {e: 0 for e in ENGS}
        for o in self.ops:
            if o.dma is None and o.signal:
                cnt[o.eng] += 1
                o.sigval = cnt[o.eng]

    def emit(self, block, sems, dma_sems):
        per = {e: [o for o in self.ops if o.eng == e] for e in ENGS}
        reg = {"pe": block.tensor, "act": block.scalar, "dve": block.vector,
               "pool": block.gpsimd, "sp": block.sync}

        def make(ename):
            def body(eh):
                known = {}
                mine = set()
                for o in per[ename]:
                    need = {}
                    for d in o.deps:
                        if d.dma is not None:
                            key, val = ("d", d.dma), d.dma_val
                        else:
                            if d.eng == "pe" and ename == "pe" and o.dma is None:
                                continue
                            key, val = ("e", d.eng), d.sigval
                        if need.get(key, 0) < val:
                            need[key] = val
                    if o.dma is not None and o.dma_prev > 0:
                        key = ("d", o.dma)
                        if need.get(key, 0) < o.dma_prev:
                            need[key] = o.dma_prev
                    for key, val in need.items():
                        if known.get(key, 0) >= val:
                            continue
                        known[key] = val
                        s = dma_sems[key[1]] if key[0] == "d" else sems[key[1]]
                        eh.wait_ge(s, val)
                    ins = o.fn(eh)
                    if o.dma is not None:
                        ins.then_inc(dma_sems[o.dma], 16)
                        mine.add(o.dma)
                    elif o.signal:
                        ins.then_inc(sems[ename], 1)
                for name in sorted(mine):
                    total = self.dma_cnt[name]
                    if known.get(("d", name), 0) < total:
                        eh.wait_ge(dma_sems[name], total)
            return body

        for e in ENGS:
            if per[e]:
                reg[e](make(e))


def build_nc(do_prompt=True, do_sample=True, n_prompt_st=2, n_s1=8, n_s2=4, dbg=None, stop_after=99):
    nc = bass.Bass("TRN2", target_bir_lowering=False)
    P = Prog()
    es = contextlib.ExitStack()

    def din(name, shape, dt=F32):
        return nc.dram_tensor(name, list(shape), dt, kind="ExternalInput").ap()

    def dout(name, shape, dt=F32):
        return nc.dram_tensor(name, list(shape), dt, kind="ExternalOutput").ap()

    def dscr(name, shape, dt=BF16):
        return nc.dram_tensor(name, list(shape), dt, kind="Internal").ap()

    xp = din("xp", [PROMPT_TOK, D])
    xs = din("xs", [SAMPLE_TOK, D])
    sp_in = din("sparams", [40, 128])
    cdk = din("cdk", [512, 512])
    cdv = din("cdv", [512, 512])
    cgk = din("cgk", [512, 128])
    cgv = din("cgv", [512, 128])
    w_ada = din("w_ada", [D, 9 * D])
    b_ada = din("b_ada", [1, 9 * D])
    w_gu = [din("w_ff1_gu", [D, 2 * DFF]), din("w_ff2_gu", [D, 2 * DFF])]
    w_dn = [din("w_ff1_down", [DFF, D]), din("w_ff2_down", [DFF, D])]
    w_in = din("w_in", [D, DIN])
    w_out = din("w_out", [D, D])
    qk_norm = din("qk_norm", [1, 128])
    lam_p = din("lam_p", [1, 256])
    subln = din("subln", [1, 128])
    fnorm = din("final_norm", [1, D])
    rope_c = din("rope_c", [SAMPLE_TOK, 64])
    rope_s = din("rope_s", [SAMPLE_TOK, 64])
    ident_f = din("ident_f", [128, 128])
    ident_b = din("ident_b", [128, 128], BF16)

    y_p = dout("y_p", [PROMPT_TOK, D])
    y_s = dout("y_s", [OWN_TOK, D])
    o_dk = dout("o_dk", [PROMPT_TOK, 512])
    o_dv = dout("o_dv", [PROMPT_TOK, 512])
    o_gk = dout("o_gk", [PROMPT_TOK, 128])
    o_gv = dout("o_gv", [PROMPT_TOK, 128])

    s_gu = [dscr(f"s_gu{i}", [NJB, 128, 8, 512]) for i in (1, 2)]
    s_dn = [dscr(f"s_dn{i}", [DFF, D]) for i in (1, 2)]
    s_in = dscr("s_in", [5, 128, 8, 512])
    s_out = dscr("s_out", [2, 128, 8, 512])
    s_kt = dscr("s_kt", [5, 128, NKEYS])
    s_vd = dscr("s_vd", [4, 128, NKT, 128])
    s_vg = dscr("s_vg", [2, 128, NKT, 128])
    s_x1 = dscr("s_x1", [OWN_TOK, D], F32)
    s_mod = dscr("s_mod", [2, 9 * D], F32)
    dbg_out = dout("dbg", dbg) if dbg is not None else None

    def sb(name, shape, dt=F32):
        return es.enter_context(nc.sbuf_tensor(name, list(shape), dt))

    XB = [sb("X0", [128, NT, D]), sb("X1", [128, NT, D])]
    xcur = [0]
    XN = sb("XN", [128, NT, D])
    HT = sb("HT", [128, 8, ST], BF16)
    ARENA = sb("ARENA", [128, 13312], BF16)
    HID = ARENA[:, 0:NJ * ST].rearrange("p (j n) -> p j n", n=ST)
    SG = sb("SG", [128, 2, ST])
    AR = sb("AR", [128, 4, 8 * 512], BF16)
    BR = sb("BR", [128, 3, 2 * 512], BF16)
    QB = sb("QB", [128, NT, 512], BF16)
    QT = sb("QT", [128, 16, ST], BF16)
    KT = sb("KT", [128, 5, ST], BF16)
    VD = sb("VD", [128, 4, NT, 128], BF16)
    VG = sb("VG", [128, 2, NT, 128], BF16)
    ONESB = sb("ONESB", [128, 128], BF16)
    ONESF = sb("ONESF", [128, 128])
    SGC = sb("SGC", [128, 1])
    EPSC = sb("EPSC", [128, 1])
    ACCS = sb("ACCS", [128, 2, 512])
    OT = sb("OT", [128, 8, ST], BF16)
    GATE = sb("GATE", [128, 3, D])
    FGB = sb("FGB", [128, D])
    SGB = sb("SGB", [128, 128])
    QKNB = sb("QKNB", [128, 2, 64])
    LPB = sb("LPB", [128, 4, 64])
    IDF = sb("IDF", [128, 128])
    IDB = sb("IDB", [128, 128], BF16)
    SPR = sb("SPR", [40, 128])
    SPT = sb("SPT", [128, 40])
    MR = sb("MR", [72, 128])
    MODT = sb("MODT", [128, 9, 8])
    SC = sb("SC", [128, 8, 2])
    GS = sb("GS", [128, 3, 8])
    SM = sb("SM", [128, 64])
    SS = sb("SS", [128, 64])
    AS = sb("AS", [128, 64])
    NEGH = sb("NEGH", [128, 64])
    JUNK = sb("JUNK", [128, D], BF16)
    RC = sb("RC", [128, NT, 64])
    RS = sb("RS", [128, NT, 64])
    RT = sb("RT", [128, 2, 512])
    ON = sb("ON", [128, 3, 512])

    R0 = XN[:, 0:2, :].rearrange("p a (b n) -> p (a b) n", n=512)
    R1 = XN[:, 2:4, :].rearrange("p a (b n) -> p (a b) n", n=512)
    R0K = [("XN", 0), ("XN", 1)]
    R1K = [("XN", 2), ("XN", 3)]
    XNK = R0K + R1K

    KVS_K = [ARENA[:, i * 2304:(i + 1) * 2304] for i in range(2)]
    KVS_V = [ARENA[:, 4608 + i * 2304: 4608 + (i + 1) * 2304] for i in range(2)]
    PT2 = ARENA[:, 9216:9216 + 4096].rearrange("p (b j n) -> p b j n", j=2, n=512)
    ARENAK = [("HID", j) for j in range(NJ)]

    PSA = es.enter_context(nc.psum_tensor("PSA", [128, 8, 512], F32))
    PSB = [PSA[:, i, :] for i in range(8)]
    PSH = [PSA[:, i, :].bitcast(BF16) for i in range(8)]

    sems = {e: es.enter_context(nc.semaphore(f"sem_{e}")) for e in ENGS}
    dma_names = (["pc%d" % i for i in range(8)] + ["a%d" % i for i in range(4)] + ["b%d" % i for i in range(3)]
                 + ["wa0", "wa1", "wa2", "wa3", "wa4", "md0", "md1", "md2", "md3", "xl", "xl0", "xl1", "m0", "m1", "m2", "m3", "st0", "st1", "st2", "st3", "st4", "st5", "st6", "st7", "st8", "st9", "st10", "st11",
                    "kk0", "kk1", "kv0", "kv1", "kvw", "x1w", "tbl", "md"])
    dma_sems = {n: es.enter_context(nc.semaphore("dsem_" + n)) for n in dma_names}

    def mm(e, out, lhsT, rhs, start=True, stop=True, **kw):
        return e.matmul(out, lhsT, rhs, start=start, stop=stop, **kw)

    def PK(i):
        return ("PS", i)

    pc_i = [0]

    def precast(out_ap, in_ap, wkey):
        name = "pc%d" % (pc_i[0] % 8)
        pc_i[0] += 1
        P.op("pool", lambda e: e.dma_start(out=out_ap, in_=in_ap), writes=[wkey], dma=name)

    def precast_gu(i):
        for jb in range(NJB):
            for part in range(2):
                c0 = part * DFF + jb * 256
                precast(s_gu[i][jb, :, :, part * 256:(part + 1) * 256],
                        w_gu[i][:, c0:c0 + 256].rearrange("(kc p) c -> p kc c", p=128), ("s_gu", i, jb, part))

    def precast_dn(i):
        for h in range(2):
            precast(s_dn[i][h * 1408:(h + 1) * 1408, :], w_dn[i][h * 1408:(h + 1) * 1408, :], ("s_dn", i, h))

    def precast_mix():
        for b in range(5):
            n = 512 if b < 4 else 256
            precast(s_in[b, :, :, 0:n], w_in[:, b * 512:b * 512 + n].rearrange("(kc p) c -> p kc c", p=128), ("s_in", b))
        for b in range(2):
            precast(s_out[b], w_out[:, b * 512:(b + 1) * 512].rearrange("(kc p) c -> p kc c", p=128), ("s_out", b))

    m_i = [0]

    def ld(out_ap, in_ap, wkeys, rkeys=(), q="sp", dma=None):
        if dma is None:
            dma = "m%d" % (m_i[0] % 4)
            m_i[0] += 1
        return P.op(q, lambda e: e.dma_start(out=out_ap, in_=in_ap), reads=list(rkeys), writes=list(wkeys), dma=dma)

    ld(IDF[:, :], ident_f[:, :], ["IDF"])
    ld(IDB[:, :], ident_b[:, :], ["IDB"])
    ld(SPR[:, :], sp_in[:, :], ["SPR"])
    ld(LPB[:, :, :].rearrange("p a d -> p (a d)"), lam_p.rearrange("a d -> (a d)").partition_broadcast(128), ["LPB"])
    ld(QKNB[:, :, :].rearrange("p a d -> p (a d)"), qk_norm.rearrange("a d -> (a d)").partition_broadcast(128), ["QKNB"])
    ld(SGB[:, :], subln.rearrange("a d -> (a d)").partition_broadcast(128), ["SGB"])
    ld(FGB[:, :], fnorm.rearrange("a d -> (a d)").partition_broadcast(128), ["FGB"])

    precast_gu(0)
    precast_dn(0)

    P.op("pool", lambda e: e.memset(NEGH[:, :], -0.5), writes=["NEGH"])
    P.op("pool", lambda e: e.memset(ONESB[:, :], 1.0), writes=["ONESB"])
    P.op("dve", lambda e: e.memset(QT[:, :, :], 0.0), writes=[("QT", c) for c in range(8)])
    P.op("pool", lambda e: e.memset(ONESF[:, :], 1.0), writes=["ONESF"])
    P.op("pool", lambda e: e.memset(EPSC[:, :], EPS), writes=["EPSC"])
    ld(SGC[:, :], subln.rearrange("a d -> d a"), ["SGC"])
    P.op("dve", lambda e: e.tensor_scalar(SGC[:, :], SGC[:, :], 1.0 - LAM_INIT, None, ALU.mult), reads=["SGC"], writes=["SGC"])

    P.op("pe", lambda e: e.transpose(PSB[0][:, 0:40], SPR[:, :], IDF[0:40, 0:40]),
         reads=["SPR", "IDF"], writes=[PK(0)])
    P.op("dve", lambda e: e.tensor_copy(SPT[:, :], PSB[0][:, 0:40]), reads=[PK(0)], writes=["SPT"])
    P.op("act", lambda e: e.activation(SC[:, :, :].rearrange("p kc g -> p g kc"),
                                       SPT[:, 0:16].rearrange("p (g kc) -> p g kc", g=2), AF.Silu),
         reads=["SPT"], writes=["SC"])

    P.op("dve", lambda e: e.tensor_tensor(out=RT[:, 0, 0:128].rearrange("p (a d) -> p a d", a=2),
                                          in0=LPB[:, :, :].rearrange("p (a b) d -> p a b d", b=2)[:, :, 0, :],
                                          in1=LPB[:, :, :].rearrange("p (a b) d -> p a b d", b=2)[:, :, 1, :], op=ALU.mult),
         reads=["LPB"], writes=[("RT", 0)])
    P.op("dve", lambda e: e.tensor_reduce(out=SM[:, 2:4], in_=RT[:, 0, 0:128].rearrange("p (a d) -> p a d", a=2),
                                          axis=AX.X, op=ALU.add), reads=[("RT", 0)], writes=["SM_l"])
    P.op("act", lambda e: e.activation(SM[:, 4:6], SM[:, 2:4], AF.Exp), reads=["SM_l"], writes=["SM_e"])
    P.op("dve", lambda e: e.tensor_tensor(out=SM[:, 1:2], in0=SM[:, 5:6], in1=SM[:, 4:5], op=ALU.subtract),
         reads=["SM_e"], writes=["SM_d"])
    P.op("dve", lambda e: e.tensor_scalar(SM[:, 0:1], SM[:, 1:2], -LAM_INIT, None, ALU.add),
         reads=["SM_d"], writes=["NLAM"])
    P.op("dve", lambda e: e.tensor_scalar(SGB[:, :], SGB[:, :], 1.0 - LAM_INIT, None, ALU.mult),
         reads=["SGB"], writes=["SGB"])
    NLAM = SM[:, 0:1]

    ARF = ARENA[:, 0:12288].bitcast(F32)
    WAV = [XN[:, 0:2, :].rearrange("p a d -> p (a d)").rearrange("p (kc n) -> p kc n", n=256),
           XN[:, 2:4, :].rearrange("p a d -> p (a d)").rearrange("p (kc n) -> p kc n", n=256)]
    WAV += [ARF[:, i * 2048:(i + 1) * 2048].rearrange("p (kc n) -> p kc n", n=256) for i in range(3)]
    WAK = [R0K, R1K, [("HID", j) for j in range(0, 8)], [("HID", j) for j in range(8, 16)], [("HID", j) for j in range(16, 22)] + ["ARX"]]
    NWS = 4
    MB = ON[0:2, :, :].rearrange("p a n -> p (a n)")[:, 0:1024].rearrange("p (s n) -> p s n", n=256)
    BAB = ACCS[0:2, :, :].rearrange("p a n -> p (a n)").rearrange("p (s n) -> p s n", n=256)
    for blk in range(36):
        slot = blk % NWS
        c0 = blk * 256
        ld(WAV[slot], w_ada[:, c0:c0 + 256].rearrange("(kc p) c -> p kc c", p=128), WAK[slot], dma="wa%d" % slot)
        ld(BAB[:, slot, 0:256], b_ada[:, c0:c0 + 256].rearrange("a d -> (a d)").partition_broadcast(2), [("BAB", slot)])
        for kc in range(8):
            P.op("pe", (lambda e, slot=slot, kc=kc: mm(e, PSB[2 + slot][0:2, 0:256], SC[:, kc, :], WAV[slot][:, kc, :],
                                                     start=(kc == 0), stop=(kc == 7))),
                 reads=WAK[slot] + ["SC"], writes=[PK(2 + slot)])
        P.op("dve", (lambda e, slot=slot: e.tensor_tensor(out=MB[:, slot, 0:256], in0=PSB[2 + slot][0:2, 0:256],
                                                          in1=BAB[:, slot, 0:256], op=ALU.add)),
             reads=[PK(2 + slot), ("BAB", slot)], writes=[("MB", slot)])
        ld(s_mod[:, c0:c0 + 256], MB[:, slot, 0:256], ["s_mod"], rkeys=[("MB", slot)], q="act", dma="md%d" % (blk % 4))

    def load_group(g):
        for i, m in enumerate((2, 5, 8)):
            ld(GATE[:, i, :], s_mod[g, m * D:(m + 1) * D].partition_broadcast(128), [("GATE", i)], rkeys=["s_mod"])
            if m != 5:
                P.op("dve", (lambda e, i=i: e.tensor_scalar(GATE[:, i, :], GATE[:, i, :], 0.5, None, ALU.mult)),
                     reads=[("GATE", i)], writes=[("GATE", i)])
        ld(MR[:, :], s_mod[g, :].rearrange("(r p) -> r p", p=128), ["MR"], rkeys=["s_mod"])
        P.op("pe", lambda e: e.transpose(PSB[0][:, 0:72], MR[:, :], IDF[0:72, 0:72]), reads=["MR", "IDF"], writes=[PK(0)])
        P.op("dve", lambda e: e.tensor_copy(MODT[:, :, :].rearrange("p m c -> p (m c)"), PSB[0][:, 0:72]),
             reads=[PK(0)], writes=["MODT"])
        for i in range(3):
            P.op("dve", (lambda e, i=i: e.scalar_tensor_tensor(out=GS[:, i, :], in0=MODT[:, 3 * i + 1, :], scalar=1.0,
                                                                in1=SPT[:, 16 + 8 * i:24 + 8 * i], op0=ALU.add, op1=ALU.mult)),
                 reads=["MODT", "SPT"], writes=["GS"])

    a_i = [0]
    b_i = [0]

    def load_A(src_ap, rkeys, n=512):
        slot = a_i[0] % 4
        a_i[0] += 1
        P.op("sp", lambda e: e.dma_start(out=AR[:, slot, :].rearrange("p (k n) -> p k n", n=512)[:, :, 0:n], in_=src_ap[:, :, 0:n]),
             reads=list(rkeys), writes=[("A", slot)], dma="a%d" % slot)
        return AR[:, slot, :].rearrange("p (k n) -> p k n", n=512), ("A", slot)

    def load_B(i, jb, fh):
        slot = b_i[0] % 3
        b_i[0] += 1
        src = s_dn[i][jb * 256:(jb + 1) * 256, fh * 512:(fh + 1) * 512].rearrange("(jj p) f -> p jj f", p=128)
        P.op("sp", lambda e: e.dma_start(out=BR[:, slot, :].rearrange("p (jj f) -> p jj f", jj=2), in_=src),
             reads=[("s_dn", i, 0), ("s_dn", i, 1)], writes=[("B", slot)], dma="b%d" % slot)
        return BR[:, slot, :].rearrange("p (jj f) -> p jj f", jj=2), ("B", slot)

    st_i = [0]

    def store(out_ap, in_ap, rkeys, wkeys=(), q=None):
        name = "st%d" % (st_i[0] % 12)
        st_i[0] += 1
        P.op(q or STQ[0], lambda e: e.dma_start(out=out_ap, in_=in_ap), reads=list(rkeys), writes=list(wkeys), dma=name)

    def rstd_from_ss(n, scale, outcols):
        P.op("dve", lambda e: e.tensor_scalar(SS[:, 32:32 + n], SS[:, 0:n], scale, EPS, ALU.mult, ALU.add),
             reads=["SS"], writes=["VE"])
        P.op("pool", lambda e: e.tensor_tensor(out=SM[:, outcols:outcols + n], in0=SS[:, 32:32 + n], in1=NEGH[:, 0:n], op=ALU.pow),
             reads=["VE", "NEGH"], writes=["RSTD"])

    def norm_stage(i):
        Xc = XB[xcur[0]]
        xk = xcur[0]
        for t in range(NT):
            P.op("act", (lambda e, t=t: e.activation(JUNK[:, :], Xc[:, t, :], AF.Square, accum_out=SS[:, t:t + 1])),
                 reads=[("X", xk, t)], writes=["SS"])
        rstd_from_ss(NT, 1.0 / D, 8)
        for t in range(NT):
            P.op("act", (lambda e, t=t: e.activation(XN[:, t, :], Xc[:, t, :], AF.Copy, scale=SM[:, 8 + t:9 + t])),
                 reads=[("X", xk, t), "RSTD"], writes=[("XN", t)])
        for c in range(8):
            bank = c % 2
            for t in range(NT):
                P.op("pe", (lambda e, t=t, c=c, bank=bank: e.transpose(PSB[bank][:, t * 128:(t + 1) * 128],
                                                                      XN[:, t, c * 128:(c + 1) * 128], IDF[:, :])),
                     reads=[("XN", t), "IDF"], writes=[PK(bank)])
            P.op("dve", (lambda e, c=c, bank=bank, i=i: e.tensor_scalar(HT[:, c, :], PSB[bank], GS[:, i, c:c + 1],
                                                                        MODT[:, 3 * i, c:c + 1], ALU.mult, ALU.add)),
                 reads=[PK(bank), "GS", "MODT"], writes=[("HT", c)])

    def resid(gate_i, fh):
        Xc = XB[xcur[0]]
        xk = xcur[0]
        for t in range(NT):
            P.op("dve", (lambda e, t=t: e.tensor_tensor(out=RT[:, t % 2, :], in0=PSB[4 + t],
                                                        in1=GATE[:, gate_i, fh * 512:(fh + 1) * 512], op=ALU.mult)),
                 reads=[PK(4 + t), ("GATE", gate_i)], writes=[("RT", t % 2)])
            P.op("dve", (lambda e, t=t: e.tensor_tensor(out=Xc[:, t, fh * 512:(fh + 1) * 512], in0=Xc[:, t, fh * 512:(fh + 1) * 512],
                                                        in1=RT[:, t % 2, :], op=ALU.add)),
                 reads=[("RT", t % 2), ("X", xk, t)], writes=[("X", xk, t)])

    def ffn_stage(i, gate_i):
        for jb in range(NJB):
            A, akey = load_A(s_gu[i][jb], [("s_gu", i, jb, 0), ("s_gu", i, jb, 1)])
            for jj in range(2):
                j = 2 * jb + jj
                pg, pu = (j % 2) * 2, (j % 2) * 2 + 1
                for kc in range(8):
                    P.op("pe", (lambda e, kc=kc, jj=jj, pg=pg, A=A: mm(e, PSB[pg], A[:, kc, jj * 128:(jj + 1) * 128], HT[:, kc, :],
                                                                       start=(kc == 0), stop=(kc == 7))),
                         reads=[akey, ("HT", kc)], writes=[PK(pg)])
                for kc in range(8):
                    P.op("pe", (lambda e, kc=kc, jj=jj, pu=pu, A=A: mm(e, PSB[pu], A[:, kc, 256 + jj * 128:256 + (jj + 1) * 128], HT[:, kc, :],
                                                                       start=(kc == 0), stop=(kc == 7))),
                         reads=[akey, ("HT", kc)], writes=[PK(pu)])
                sgs = j % 2
                P.op("act", (lambda e, sgs=sgs, pg=pg: e.activation(SG[:, sgs, :], PSB[pg], AF.Silu)),
                     reads=[PK(pg)], writes=[("SG", sgs)])
                P.op("dve", (lambda e, sgs=sgs, pu=pu, j=j: e.tensor_tensor(out=HID[:, j, :], in0=SG[:, sgs, :], in1=PSB[pu], op=ALU.mult)),
                     reads=[("SG", sgs), PK(pu)], writes=[("HID", j)])
        for fh in range(2):
            for jb in range(NJB):
                B, bkey = load_B(i, jb, fh)
                for jj in range(2):
                    j = 2 * jb + jj
                    for t in range(NT):
                        P.op("pe", (lambda e, t=t, j=j, jj=jj, B=B: mm(e, PSB[4 + t], HID[:, j, t * 128:(t + 1) * 128], B[:, jj, :],
                                                                       start=(j == 0), stop=(j == NJ - 1))),
                             reads=[bkey, ("HID", j)], writes=[PK(4 + t)])
            resid(gate_i, fh)

    def transposes_bf(dst, dst_key, src_fn, src_keys, nch, c0=0, split=False):
        for c in range(nch):
            bank = c % 2
            for t in range(NT):
                P.op("pe", (lambda e, t=t, c=c, bank=bank: e.transpose(PSH[bank][:, t * 128:(t + 1) * 128], src_fn(t, c), IDB[:, :])),
                     reads=list(src_keys) + ["IDB"], writes=[PK(bank)])
            if split:
                for hf in range(2):
                    P.op("act", (lambda e, c=c, bank=bank, hf=hf: e.activation(dst[64 * hf:64 * hf + 64, 2 * (c0 + c) + hf, :],
                                                                              PSH[bank][64 * hf:64 * hf + 64, 0:ST], AF.Copy)),
                         reads=[PK(bank)], writes=[(dst_key, c0 + c)])
            else:
                P.op("act", (lambda e, c=c, bank=bank: e.activation(dst[:, c0 + c, :], PSH[bank][:, 0:ST], AF.Copy)),
                     reads=[PK(bank)], writes=[(dst_key, c0 + c)])

    def win_block(b):
        n = 512 if b < 4 else 256
        A, akey = load_A(s_in[b], [("s_in", b)], n)
        for t in range(NT):
            for kc in range(8):
                P.op("pe", (lambda e, t=t, kc=kc, A=A, n=n: mm(e, PSB[4 + t][:, 0:n], HT[:, kc, t * 128:(t + 1) * 128], A[:, kc, 0:n],
                                                               start=(kc == 0), stop=(kc == 7))),
                     reads=[akey, ("HT", kc)], writes=[PK(4 + t)])

    def rope(src_fn, src_keys, nh, dst_fn, dst_keys, t, addview=None):
        P.op("dve", (lambda e: e.tensor_tensor(out=RT[:, 0, 0:nh * 64].rearrange("p (h d) -> p h d", d=64),
                                               in0=src_fn().rearrange("p (h d) -> p h d", d=64),
                                               in1=RC[:, t, :].unsqueeze(1).broadcast_to([128, nh, 64]), op=ALU.mult)),
             reads=list(src_keys) + ["RC"], writes=[("RT", 0)])
        for b in range(2):
            P.op("dve", (lambda e, b=b: e.tensor_tensor(
                out=RT[:, 1, 0:nh * 64].rearrange("p (h a b i) -> p h a b i", a=2, b=2, i=16)[:, :, :, b, :],
                in0=src_fn().rearrange("p (h a b i) -> p h a b i", a=2, b=2, i=16)[:, :, :, 1 - b, :],
                in1=RS[:, t, :].rearrange("p (a b i) -> p a b i", a=2, b=2)[:, :, b, :].unsqueeze(1).broadcast_to([128, nh, 2, 16]),
                op=ALU.mult)),
                 reads=list(src_keys) + ["RS"], writes=[("RT", 1)])
        av = addview if addview is not None else (lambda a: a.rearrange("p (h d) -> p h d", d=64))
        P.op("dve", (lambda e: e.tensor_tensor(out=dst_fn(), in0=av(RT[:, 0, 0:nh * 64]),
                                                in1=av(RT[:, 1, 0:nh * 64]), op=ALU.add)),
             reads=[("RT", 0), ("RT", 1)], writes=list(dst_keys))

    def headnorm(R, Rk, col0, nh, gi):
        for t in range(NT):
            P.op("act", (lambda e, t=t: e.activation(RT[:, 0, 0:nh * 64], R[:, t, col0:col0 + nh * 64], AF.Square)),
                 reads=list(Rk), writes=[("RT", 0)])
            P.op("dve", (lambda e, t=t: e.tensor_reduce(out=SS[:, t * nh:(t + 1) * nh],
                                                        in_=RT[:, 0, 0:nh * 64].rearrange("p (h d) -> p h d", d=64),
                                                        axis=AX.X, op=ALU.add)),
                 reads=[("RT", 0)], writes=["SS"])
        rstd_from_ss(NT * nh, 1.0 / 64, 16)
        for t in range(NT):
            P.op("dve", (lambda e, t=t: e.tensor_tensor(
                out=R[:, t, col0:col0 + nh * 64].rearrange("p (h d) -> p h d", d=64),
                in0=R[:, t, col0:col0 + nh * 64].rearrange("p (h d) -> p h d", d=64),
                in1=SM[:, 16 + t * nh:16 + (t + 1) * nh].unsqueeze(2).broadcast_to([128, nh, 64]), op=ALU.mult)),
                 reads=list(Rk) + ["RSTD"], writes=list(Rk))
            P.op("dve", (lambda e, t=t: e.tensor_tensor(
                out=R[:, t, col0:col0 + nh * 64].rearrange("p (h d) -> p h d", d=64),
                in0=R[:, t, col0:col0 + nh * 64].rearrange("p (h d) -> p h d", d=64),
                in1=QKNB[:, gi, :].unsqueeze(1).broadcast_to([128, nh, 64]), op=ALU.mult)),
                 reads=list(Rk) + ["QKNB"], writes=list(Rk))

    def gq_dst(t):
        return QB[:, t, :].rearrange("p (g n d) -> p n g d", g=4, n=2)

    def mix_q(sample):
        win_block(0)
        for t in range(NT):
            if sample:
                rope(lambda t=t: PSB[4 + t], [PK(4 + t)], 8, lambda t=t: QB[:, t, :].rearrange("p (h d) -> p h d", d=64), ["QB"], t)
            else:
                P.op("act", (lambda e, t=t: e.activation(QB[:, t, :], PSB[4 + t], AF.Copy)), reads=[PK(4 + t)], writes=["QB"])
        transposes_bf(QT, "QT", lambda t, c: QB[:, t, c * 128:(c + 1) * 128], ["QB"], 4, 0, split=True)
        win_block(3)
        for t in range(NT):
            P.op("dve", (lambda e, t=t: e.tensor_copy(R1[:, t, :], PSB[4 + t])), reads=[PK(4 + t)], writes=R1K)
        headnorm(R1, R1K, 0, 8, 0)
        for t in range(NT):
            if sample:
                rope(lambda t=t: R1[:, t, :], R1K, 8, lambda t=t: gq_dst(t), ["QB"], t,
                     addview=lambda a: a.rearrange("p (n g d) -> p n g d", n=2, g=4))
            else:
                P.op("dve", (lambda e, t=t: e.tensor_copy(gq_dst(t), R1[:, t, :].rearrange("p (n g d) -> p n g d", n=2, g=4))),
                     reads=R1K, writes=["QB"])
        transposes_bf(QT, "QT", lambda t, c: QB[:, t, c * 128:(c + 1) * 128], ["QB"], 4, 4, split=True)

    def mix_kv(sample, tok0, kt0):
        win_block(1)
        for t in range(NT):
            if sample:
                rope(lambda t=t: PSB[4 + t], [PK(4 + t)], 8, lambda t=t: QB[:, t, :].rearrange("p (h d) -> p h d", d=64), ["QB"], t)
            else:
                P.op("act", (lambda e, t=t: e.activation(R0[:, t, :], PSB[4 + t], AF.Copy)), reads=[PK(4 + t)], writes=R0K)
                P.op("dve", (lambda e, t=t: e.tensor_copy(QB[:, t, :], R0[:, t, :])), reads=R0K, writes=["QB"])
        if not sample and KVSTOP[0] != 11:
            store(o_dk[tok0:tok0 + ST, :].rearrange("(t p) f -> p t f", p=128), R0, R0K)
        if KVSTOP[0] == 11:
            return
        transposes_bf(KT, "KT", lambda t, c: QB[:, t, c * 128:(c + 1) * 128], ["QB"], 4, 0)
        if KVSTOP[0] == 1:
            return
        win_block(2)
        for t in range(NT):
            if not sample:
                P.op("act", (lambda e, t=t: e.activation(R1[:, t, :], PSB[4 + t], AF.Copy)), reads=[PK(4 + t)], writes=R1K)
                P.op("dve", (lambda e, t=t: e.tensor_copy(VD[:, :, t, :], R1[:, t, :].rearrange("p (h d) -> p h d", d=128))),
                     reads=R1K, writes=["VD"])
            else:
                P.op("dve", (lambda e, t=t: e.tensor_copy(VD[:, :, t, :], PSB[4 + t].rearrange("p (h d) -> p h d", d=128))),
                     reads=[PK(4 + t)], writes=["VD"])
        if not sample:
            store(o_dv[tok0:tok0 + ST, :].rearrange("(t p) f -> p t f", p=128), R1, R1K)
        if KVSTOP[0] == 2:
            return
        win_block(4)
        for t in range(NT):
            P.op("dve", (lambda e, t=t: e.tensor_copy(R0[:, t, 0:256], PSB[4 + t][:, 0:256])), reads=[PK(4 + t)], writes=R0K)
        headnorm(R0, R0K, 0, 2, 1)
        for t in range(NT):
            if sample:
                rope(lambda t=t: R0[:, t, 0:128], R0K, 2, lambda t=t: QB[:, t, 0:128].rearrange("p (h d) -> p h d", d=64), ["QB"], t)
            else:
                P.op("act", (lambda e, t=t: e.activation(QB[:, t, 0:128], R0[:, t, 0:128], AF.Copy)), reads=R0K, writes=["QB"])
            P.op("dve", (lambda e, t=t: e.tensor_copy(VG[:, :, t, :].rearrange("p n (r d) -> p n r d", r=2),
                                                      R0[:, t, 128:256].rearrange("p (h d) -> p h d", d=64).unsqueeze(2).broadcast_to([128, 2, 2, 64]))),
                 reads=R0K, writes=["VG"])
        if not sample:
            store(o_gk[tok0:tok0 + ST, :].rearrange("(t p) f -> p t f", p=128), R0[:, :, 0:128], R0K)
            store(o_gv[tok0:tok0 + ST, :].rearrange("(t p) f -> p t f", p=128), R0[:, :, 128:256], R0K)
        transposes_bf(KT, "KT", lambda t, c: QB[:, t, 0:128], ["QB"], 1, 4)
        if sample:
            store_kv(kt0)

    def store_kv(kt0):
        store(s_kt[:, :, kt0 * 128:kt0 * 128 + ST].rearrange("c p n -> p c n"), KT[:, :, :], [("KT", c) for c in range(5)], [("s_kt", kt0)], q="sp")
        store(s_vd[:, :, kt0:kt0 + NT, :].rearrange("h p t e -> p h (t e)"), VD[:, :, :, :].rearrange("p h t e -> p h (t e)"),
              ["VD"], [("s_vd", h, kt0) for h in range(4)], q="sp")
        store(s_vg[:, :, kt0:kt0 + NT, :].rearrange("n p t e -> p n (t e)"), VG[:, :, :, :].rearrange("p n t e -> p n (t e)"),
              ["VG"], [("s_vg", n, kt0) for n in range(2)], q="sp")

    s_i = [0]
    p_i = [0]

    def attn_unit(kind, u, subs, q0, nq, ktiles):
        nk = len(ktiles)
        info = {}
        LA = 2

        def emit_qk(ki):
            kfn, kkeys, v_ap, vkeys = ktiles[ki]
            r = s_i[0] % 3
            s_i[0] += 1
            chunks = []
            for j in range(2):
                if kind == "diff":
                    r0, chunk = 64 * subs[j], u
                else:
                    r0, chunk = 64 * u, 4 + subs[j]
                chunks.append(chunk)
                P.op("pe", (lambda e, j=j, r0=r0, chunk=chunk: mm(e, PSB[2 * r + j][:, 0:nq], kfn(),
                                                                  QT[:, 2 * chunk + r0 // 64, q0:q0 + nq])),
                     reads=list(kkeys) + [("QT", chunk)], writes=[PK(2 * r + j)])
            slot = p_i[0] % 4
            p_i[0] += 1
            P.op("act", (lambda e: e.activation(PT2[:, slot, :, 0:nq], PSA[:, 2 * r:2 * r + 2, 0:nq], AF.Exp, scale=0.125)),
                 reads=[PK(2 * r), PK(2 * r + 1)],
                 writes=[("PT", slot)] + ([("HID", jj) for jj in range(18, NJ)] if ki == 0 else []))
            if ki == 0:
                P.op("dve", (lambda e: e.tensor_copy(ACCS[:, :, 0:nq], PT2[:, slot, :, 0:nq])), reads=[("PT", slot)], writes=["ACCS", "ACCS2"])
            else:
                cs = (nq * 3) // 4
                P.op("dve", (lambda e: e.tensor_tensor(out=ACCS[:, :, 0:cs], in0=ACCS[:, :, 0:cs], in1=PT2[:, slot, :, 0:cs], op=ALU.add)),
                     reads=[("PT", slot), "ACCS"], writes=["ACCS"])
                P.op("pool", (lambda e: e.tensor_tensor(out=ACCS[:, :, cs:nq], in0=ACCS[:, :, cs:nq], in1=PT2[:, slot, :, cs:nq], op=ALU.add)),
                     reads=[("PT", slot), "ACCS2"], writes=["ACCS2"])
            info[ki] = slot

        def emit_pv(ki):
            kfn, kkeys, v_ap, vkeys = ktiles[ki]
            slot = info[ki]
            for j in range(2):
                P.op("pe", (lambda e, j=j: mm(e, PSB[6 + j][:, 0:nq], v_ap, PT2[:, slot, j, 0:nq], start=(ki == 0), stop=(ki == nk - 1))),
                     reads=[("PT", slot)] + list(vkeys), writes=[PK(6 + j)])

        for kk in range(nk + LA):
            if kk < nk:
                emit_qk(kk)
            if kk - LA >= 0:
                emit_pv(kk - LA)
        for j in range(2):
            P.op("pe", (lambda e, j=j: mm(e, PSB[j][:, 0:nq], ONESF[:, :], ACCS[:, j, 0:nq])), reads=["ACCS", "ACCS2", "ONESF"], writes=[PK(j)])

        for j in range(2):
            P.op("dve", (lambda e, j=j: e.reciprocal(ON[:, 2, 0:nq], PSB[j][:, 0:nq])), reads=[PK(j)], writes=[("ON", 2)])
            if kind == "diff":
                P.op("dve", (lambda e, j=j: e.tensor_tensor(out=ON[:, j, 0:nq], in0=PSB[6 + j][:, 0:nq], in1=ON[:, 2, 0:nq], op=ALU.mult)),
                     reads=[PK(6 + j), ("ON", 2)], writes=[("ON", j)])
            else:
                hq = 4 * u + subs[j]
                r0, chunk = 64 * (hq % 2), 4 + hq // 2
                P.op("dve", (lambda e, j=j, r0=r0, chunk=chunk: e.tensor_tensor(
                    out=OT[r0:r0 + 64, chunk, q0:q0 + nq], in0=PSB[6 + j][r0:r0 + 64, 0:nq], in1=ON[r0:r0 + 64, 2, 0:nq], op=ALU.mult)),
                     reads=[PK(6 + j), ("ON", 2)], writes=[("OT", chunk)])
        if kind == "gqa":
            return
        P.op("dve", lambda e: e.scalar_tensor_tensor(out=ON[:, 0, 0:nq], in0=ON[:, 1, 0:nq], scalar=NLAM, in1=ON[:, 0, 0:nq],
                                                     op0=ALU.mult, op1=ALU.add),
             reads=[("ON", 0), ("ON", 1), "NLAM"], writes=[("ON", 0)])
        P.op("act", lambda e: e.activation(ON[:, 2, 0:nq], ON[:, 0, 0:nq], AF.Square), reads=[("ON", 0)], writes=[("ON", 2)])
        P.op("pe", lambda e: mm(e, PSB[2][:, 0:nq], ONESF[:, :], ON[:, 2, 0:nq]), reads=[("ON", 2), "ONESF"], writes=[PK(2)])
        P.op("act", lambda e: e.activation(ON[:, 1, 0:nq], PSB[2][:, 0:nq], AF.Ln, scale=1.0 / 128, bias=EPSC[:, 0:1]),
             reads=[PK(2), "EPSC"], writes=[("ON", 1)])
        P.op("act", lambda e: e.activation(ON[:, 1, 0:nq], ON[:, 1, 0:nq], AF.Exp, scale=-0.5), reads=[("ON", 1)], writes=[("ON", 1)])
        P.op("dve", lambda e: e.scalar_tensor_tensor(out=OT[:, u, q0:q0 + nq], in0=ON[:, 0, 0:nq], scalar=SGC[:, 0:1], in1=ON[:, 1, 0:nq],
                                                     op0=ALU.mult, op1=ALU.mult),
             reads=[("ON", 0), ("ON", 1), "SGC"], writes=[("OT", u)])

    def attn_prompt():
        for bb in range(2):
            for h in range(4):
                kts = []
                for t in (2 * bb, 2 * bb + 1):
                    kts.append(((lambda t=t, h=h: KT[:, h, t * 128:(t + 1) * 128]), [("KT", h)],
                                VD[:, h, t, :], ["VD"]))
                attn_unit("diff", h, (0, 1), bb * 256, 256, kts)
            for n in range(2):
                for gp in range(2):
                    kts = []
                    for t in (2 * bb, 2 * bb + 1):
                        kts.append(((lambda t=t: KT[:, 4, t * 128:(t + 1) * 128]), [("KT", 4)],
                                    VG[:, n, t, :], ["VG"]))
                    attn_unit("gqa", n, (2 * gp, 2 * gp + 1), bb * 256, 256, kts)

    kv_i = [0]

    def attn_sample():
        units = [("diff", h, (0, 1)) for h in range(4)] + [("gqa", n, (2 * gp, 2 * gp + 1)) for n in range(2) for gp in range(2)]
        for kind, u, subs in units:
            kts = []
            for half in range(2):
                buf = kv_i[0] % 2
                kv_i[0] += 1
                chunk = u if kind == "diff" else 4
                P.op("sp", (lambda e, buf=buf, chunk=chunk, half=half: e.dma_start(
                    out=KVS_K[buf], in_=s_kt[chunk, :, half * 2304:(half + 1) * 2304])),
                     reads=[("s_kt", k0) for k0 in range(0, NKT, NT)], writes=[("KVK", buf)] + [("HID", j) for j in ((0, 1, 2, 3, 4) if buf == 0 else (4, 5, 6, 7, 8))],
                     dma="kk%d" % buf)
                vs = s_vd if kind == "diff" else s_vg
                vsrc = vs[u, :, half * 18:(half + 1) * 18, :].rearrange("p k e -> p (k e)")
                P.op("sp", (lambda e, buf=buf, vsrc=vsrc: e.dma_start(out=KVS_V[buf], in_=vsrc)),
                     reads=[("s_vd", hh, k0) for hh in range(4) for k0 in range(0, NKT, NT)] + [("s_vg", nn, k0) for nn in range(2) for k0 in range(0, NKT, NT)], writes=[("KVV", buf)] + [("HID", j) for j in ((9, 10, 11, 12, 13) if buf == 0 else (13, 14, 15, 16, 17))],
                     dma="kv%d" % buf)
                for k in range(18):
                    kts.append(((lambda buf=buf, k=k: KVS_K[buf][:, k * 128:(k + 1) * 128]), [("KVK", buf)],
                                KVS_V[buf][:, k * 128:(k + 1) * 128], [("KVV", buf)]))
            attn_unit(kind, u, subs, 0, ST, kts)

    def out_proj():
        for fh in range(2):
            A, akey = load_A(s_out[fh], [("s_out", fh)])
            for t in range(NT):
                for kc in range(8):
                    P.op("pe", (lambda e, t=t, kc=kc, A=A: mm(e, PSB[4 + t], OT[:, kc, t * 128:(t + 1) * 128], A[:, kc, :],
                                                              start=(kc == 0), stop=(kc == 7))),
                         reads=[akey, ("OT", kc)], writes=[PK(4 + t)])
            resid(1, fh)

    def final_norm(y, tok0):
        Xc = XB[xcur[0]]
        xk = xcur[0]
        for t in range(NT):
            P.op("act", (lambda e, t=t: e.activation(JUNK[:, :], Xc[:, t, :], AF.Square, accum_out=SS[:, t:t + 1])),
                 reads=[("X", xk, t)], writes=["SS"])
        rstd_from_ss(NT, 1.0 / D, 8)
        for t in range(NT):
            P.op("dve", (lambda e, t=t: e.scalar_tensor_tensor(out=XN[:, t, :], in0=Xc[:, t, :], scalar=SM[:, 8 + t:9 + t],
                                                                in1=FGB[:, :], op0=ALU.mult, op1=ALU.mult)),
                 reads=[("X", xk, t), "RSTD", "FGB"], writes=[("XN", t)])
        store(y[tok0:tok0 + ST, :].rearrange("(t p) f -> p t f", p=128), XN[:, :, :], XNK)

    def load_x(src, tok0, buf):
        Xc = XB[buf]
        P.op("sp", lambda e: e.dma_start(out=Xc[:, :, :], in_=src[tok0:tok0 + ST, :].rearrange("(t p) f -> p t f", p=128)),
             reads=[("s_x1", tok0 // ST)] if src is s_x1 else [], writes=[("X", buf, t) for t in range(NT)], dma="xl%d" % buf)

    def load_rope(tok0):
        P.op("sp", lambda e: e.dma_start(out=RC[:, :, :], in_=rope_c[tok0:tok0 + ST, :].rearrange("(t p) f -> p t f", p=128)),
             writes=["RC"], dma="tbl")
        P.op("sp", lambda e: e.dma_start(out=RS[:, :, :], in_=rope_s[tok0:tok0 + ST, :].rearrange("(t p) f -> p t f", p=128)),
             writes=["RS"], dma="tbl")

    first = [True]

    def maybe(fn):
        if first[0]:
            fn()

    tiles = []
    if do_prompt:
        tiles += [("prompt", si, xp, si * ST) for si in range(n_prompt_st)]
    if do_sample:
        tiles += [("s1", si, xs, si * ST) for si in range(n_s1)]
        tiles += [("s2", qi, s_x1, qi * ST) for qi in range(n_s2)]

    def prefetch(k):
        if k < len(tiles):
            ph, idx, src, off = tiles[k]
            load_x(src, off, k % 2)

    def sample_cache_prep():
        load_group(1)
        P.op("sp", lambda e: e.dma_start(out=R0, in_=cdk.rearrange("(t p) f -> p t f", p=128)), writes=R0K, dma="xl")
        for t in range(NT):
            P.op("act", (lambda e, t=t: e.activation(QB[:, t, :], R0[:, t, :], AF.Copy)), reads=R0K, writes=["QB"])
        transposes_bf(KT, "KT", lambda t, c: QB[:, t, c * 128:(c + 1) * 128], ["QB"], 4, 0)
        P.op("sp", lambda e: e.dma_start(out=R1, in_=cdv.rearrange("(t p) f -> p t f", p=128)), writes=R1K, dma="xl")
        for t in range(NT):
            P.op("dve", (lambda e, t=t: e.tensor_copy(VD[:, :, t, :], R1[:, t, :].rearrange("p (h d) -> p h d", d=128))),
                 reads=R1K, writes=["VD"])
        P.op("sp", lambda e: e.dma_start(out=R0[:, :, 0:128], in_=cgk.rearrange("(t p) f -> p t f", p=128)), writes=R0K, dma="xl")
        P.op("sp", lambda e: e.dma_start(out=R0[:, :, 128:256], in_=cgv.rearrange("(t p) f -> p t f", p=128)), writes=R0K, dma="xl")
        for t in range(NT):
            P.op("act", (lambda e, t=t: e.activation(QB[:, t, 0:128], R0[:, t, 0:128], AF.Copy)), reads=R0K, writes=["QB"])
            P.op("dve", (lambda e, t=t: e.tensor_copy(VG[:, :, t, :].rearrange("p n (r d) -> p n r d", r=2),
                                                      R0[:, t, 128:256].rearrange("p (h d) -> p h d", d=64).unsqueeze(2).broadcast_to([128, 2, 2, 64]))),
                 reads=R0K, writes=["VG"])
        transposes_bf(KT, "KT", lambda t, c: QB[:, t, 0:128], ["QB"], 1, 4)
        store_kv(0)

    prefetch(0)
    seen_phase = set()
    for k, (ph, idx, src, off) in enumerate(tiles):
        xcur[0] = k % 2
        if ph == "prompt" and "prompt" not in seen_phase:
            load_group(0)
        if ph == "s1" and "s1" not in seen_phase:
            if first[0]:
                precast_mix()
                precast_gu(1)
                precast_dn(1)
                first[0] = False
            sample_cache_prep()
        if ph == "s2" and "s1" not in seen_phase and "s2" not in seen_phase:
            load_group(1)
        seen_phase.add(ph)
        if ph == "prompt":
            tok0 = off
            steps = [lambda: norm_stage(0), lambda: (prefetch(k + 1), maybe(precast_mix)), lambda: ffn_stage(0, 0),
                     lambda: norm_stage(1), lambda: maybe(lambda: (precast_gu(1), precast_dn(1))), lambda: mix_q(False),
                     lambda: mix_kv(False, tok0, 0), attn_prompt, out_proj, lambda: norm_stage(2), lambda: ffn_stage(1, 2),
                     lambda: final_norm(y_p, tok0)]
            for kk, fn in enumerate(steps):
                if kk < stop_after:
                    fn()
            first[0] = False
        elif ph == "s1":
            load_rope(off)
            norm_stage(0)
            prefetch(k + 1)
            ffn_stage(0, 0)
            if idx >= 4:
                Xc = XB[xcur[0]]
                store(s_x1[(idx - 4) * ST:(idx - 3) * ST, :].rearrange("(t p) f -> p t f", p=128), Xc[:, :, :],
                      [("X", xcur[0], t) for t in range(NT)], [("s_x1", idx - 4)], q="sp")
            norm_stage(1)
            mix_kv(True, off, 4 + idx * NT)
        else:
            load_rope(OWN_TOK + off)
            norm_stage(1)
            prefetch(k + 1)
            mix_q(True)
            attn_sample()
            out_proj()
            norm_stage(2)
            ffn_stage(1, 2)
            final_norm(y_s, off)
    if first[0] and not tiles:
        precast_mix()
        precast_gu(1)
        precast_dn(1)

    P.finalize()
    with nc.Block() as block:
        P.emit(block, sems, dma_sems)
    es.close()
    return nc


def _rope_tables():
    rows = SAMPLE_TOK // 64
    row = np.repeat(np.arange(rows), 64)
    col = np.tile(np.arange(64), rows)
    half = 32
    inv = (10000.0 ** (-np.arange(0, half, 2, dtype=np.float32) / half)).astype(np.float32)
    ang = np.stack([row[:, None].astype(np.float32) * inv, col[:, None].astype(np.float32) * inv], axis=1)
    c = np.cos(ang).astype(np.float32)
    s = np.sin(ang).astype(np.float32)
    C2 = np.stack([c, c], axis=2).reshape(SAMPLE_TOK, 64)
    S2 = np.stack([-s, s], axis=2).reshape(SAMPLE_TOK, 64)
    return np.ascontiguousarray(C2), np.ascontiguousarray(S2)


def make_in_maps(inp):
    f = lambda a: np.ascontiguousarray(np.asarray(a, dtype=np.float32))
    C2, S2 = _rope_tables()
    ident_f = np.eye(128, dtype=np.float32)
    ident_b = np.eye(128, dtype=np.float32).astype(ml_dtypes.bfloat16)
    shared = {
        "w_ada": f(inp["w_ada"][0]), "b_ada": f(inp["b_ada"]).reshape(1, 9 * D),
        "w_ff1_gu": f(inp["w_ff1_gu"][0]), "w_ff2_gu": f(inp["w_ff2_gu"][0]),
        "w_ff1_down": f(inp["w_ff1_down"][0]), "w_ff2_down": f(inp["w_ff2_down"][0]),
        "w_in": f(inp["w_in"][0]), "w_out": f(inp["w_out"][0]),
        "qk_norm": np.concatenate([f(inp["q_norm"][0]), f(inp["k_norm"][0])]).reshape(1, 128),
        "lam_p": np.concatenate([f(inp["lambda_q1"][0]), f(inp["lambda_k1"][0]),
                                 f(inp["lambda_q2"][0]), f(inp["lambda_k2"][0])]).reshape(1, 256),
        "subln": f(inp["subln"]).reshape(1, 128), "final_norm": f(inp["final_norm"]).reshape(1, D),
        "ident_f": ident_f, "ident_b": ident_b,
    }
    norms = np.concatenate([f(inp["norm_ff1"][0]), f(inp["norm_mix"][0]), f(inp["norm_ff2"][0])]).reshape(24, 128)
    xp_all = f(inp["x_prompt"])
    xs_all = f(inp["x_sample"])
    maps = []
    for core in range(8):
        b, half = core // 2, core % 2
        order = np.concatenate([np.arange((1 - half) * OWN_TOK, (2 - half) * OWN_TOK),
                                np.arange(half * OWN_TOK, (half + 1) * OWN_TOK)])
        cv = np.stack([f(inp["c_ctx"]), f(inp["c"])[b]]).reshape(16, 128)
        m = dict(shared)
        m["xp"] = np.ascontiguousarray(xp_all[4 * core:4 * core + 4].reshape(PROMPT_TOK, D))
        m["xs"] = np.ascontiguousarray(xs_all[b][order])
        m["sparams"] = np.ascontiguousarray(np.concatenate([cv, norms], axis=0))
        m["cdk"] = f(inp["cache_diff_k"][b, 0]).reshape(512, 512)
        m["cdv"] = f(inp["cache_diff_v"][b, 0]).reshape(512, 512)
        m["cgk"] = f(inp["cache_gqa_k"][b, 0]).reshape(512, 128)
        m["cgv"] = f(inp["cache_gqa_v"][b, 0]).reshape(512, 128)
        m["rope_c"] = np.ascontiguousarray(C2[order])
        m["rope_s"] = np.ascontiguousarray(S2[order])
        maps.append(m)
    return maps


def kernel(**inp):
    nc = build_nc()
    maps = make_in_maps(inp)
    res = run_bass_kernel_spmd(nc, maps, core_ids=list(range(8)))
    R = res.results
    y_p = np.concatenate([r["y_p"].reshape(4, 256, D) for r in R], axis=0)
    y_s = np.zeros((4, SAMPLE_TOK, D), np.float32)
    for core, r in enumerate(R):
        b, half = core // 2, core % 2
        y_s[b, half * OWN_TOK:(half + 1) * OWN_TOK] = r["y_s"]
    ndk = np.concatenate([r["o_dk"].reshape(4, 1, 256, 4, 2, 64) for r in R], axis=0)
    ndv = np.concatenate([r["o_dv"].reshape(4, 1, 256, 4, 128) for r in R], axis=0)
    ngk = np.concatenate([r["o_gk"].reshape(4, 1, 256, 2, 64) for r in R], axis=0)
    ngv = np.concatenate([r["o_gv"].reshape(4, 1, 256, 2, 64) for r in R], axis=0)
    return (y_p.astype(np.float32), y_s, ndk.astype(np.float32), ndv.astype(np.float32),
            ngk.astype(np.float32), ngv.astype(np.float32))
```

```python
import math
import contextlib
import numpy as np
import ml_dtypes
import concourse.bass as bass
import concourse.mybir as mybir
from concourse.bass_utils import run_bass_kernel_spmd

F32 = mybir.dt.float32
BF16 = mybir.dt.bfloat16
AF = mybir.ActivationFunctionType
ALU = mybir.AluOpType
AX = mybir.AxisListType

D = 1024
DFF = 2816
NJ = 22
NJB = 11
DIN = 2304
EPS = 1e-6
LAM_INIT = 0.8 - 0.6 * math.exp(0.0)
ST = 512
NT = 4
NKT = 36
NKEYS = NKT * 128
PROMPT_TOK = 1024
SAMPLE_TOK = 4096
OWN_TOK = 2048
ENGS = ("pe", "act", "dve", "pool", "sp")
KVSTOP = [0]
STQ = ["act"]


class Op:
    __slots__ = ("eng", "fn", "deps", "dma", "dma_val", "dma_prev", "signal", "sigval", "order")

    def __init__(self, eng, fn, order):
        self.eng = eng
        self.fn = fn
        self.deps = []
        self.dma = None
        self.dma_val = 0
        self.dma_prev = 0
        self.signal = False
        self.sigval = 0
        self.order = order


def _okey(o):
    return ("d", o.dma) if o.dma is not None else ("e", o.eng)


class Prog:
    def __init__(self):
        self.ops = []
        self.last_w = {}
        self.readers = {}
        self.dma_cnt = {}

    def op(self, eng, fn, reads=(), writes=(), dma=None):
        o = Op(eng, fn, len(self.ops))
        cand = {}

        def add(d):
            if d is None:
                return
            k = _okey(d)
            c = cand.get(k)
            if c is None or d.order > c.order:
                cand[k] = d

        for k in reads:
            add(self.last_w.get(k))
        for k in writes:
            add(self.last_w.get(k))
            for r in self.readers.get(k, {}).values():
                add(r)
        if dma is not None:
            o.dma = dma
            o.dma_prev = self.dma_cnt.get(dma, 0)
            o.dma_val = o.dma_prev + 16
            self.dma_cnt[dma] = o.dma_val
        for k in writes:
            self.last_w[k] = o
            self.readers[k] = {}
        ok = _okey(o)
        for k in reads:
            self.readers.setdefault(k, {})[ok] = o
        o.deps = list(cand.values())
        self.ops.append(o)
        return o

    def finalize(self):
        for o in self.ops:
            for d in o.deps:
                if d.dma is None:
                    if d.eng == "pe" and o.eng == "pe" and o.dma is None:
                        continue
                    d.signal = True
        cnt = {e: 0 for e in ENGS}
        for o in self.ops:
            if o.dma is None and o.signal:
                cnt[o.eng] += 1
                o.sigval = cnt[o.eng]

    def emit(self, block, sems, dma_sems):
        per = {e: [o for o in self.ops if o.eng == e] for e in ENGS}
        reg = {"pe": block.tensor, "act": block.scalar, "dve": block.vector,
               "pool": block.gpsimd, "sp": block.sync}

        def make(ename):
            def body(eh):
                known = {}
                mine = set()
                for o in per[ename]:
                    need = {}
                    for d in o.deps:
                        if d.dma is not None:
                            key, val = ("d", d.dma), d.dma_val
                        else:
                            if d.eng == "pe" and ename == "pe" and o.dma is None:
                                continue
                            key, val = ("e", d.eng), d.sigval
                        if need.get(key, 0) < val:
                            need[key] = val
                    if o.dma is not None and o.dma_prev > 0:
                        key = ("d", o.dma)
                        if need.get(key, 0) < o.dma_prev:
                            need[key] = o.dma_prev
                    for key, val in need.items():
                        if known.get(key, 0) >= val:
                            continue
                        known[key] = val
                        s = dma_sems[key[1]] if key[0] == "d" else sems[key[1]]
                        eh.wait_ge(s, val)
                    ins = o.fn(eh)
                    if o.dma is not None:
                        ins.then_inc(dma_sems[o.dma], 16)
                        mine.add(o.dma)
                    elif o.signal:
                        ins.then_inc(sems[ename], 1)
                for name in sorted(mine):
                    total = self.dma_cnt[name]
                    if known.get(("d", name), 0) < total:
                        eh.wait_ge(dma_sems[name], total)
            return body

        for e in ENGS:
            if per[e]:
                reg[e](make(e))


def build_nc(do_prompt=True, do_sample=True, n_prompt_st=2, n_s1=8, n_s2=4, dbg=None, stop_after=99):
    nc = bass.Bass("TRN2", target_bir_lowering=False)
    P = Prog()
    es = contextlib.ExitStack()

    def din(name, shape, dt=F32):
        return nc.dram_tensor(name, list(shape), dt, kind="ExternalInput").ap()

    def dout(name, shape, dt=F32):
        return nc.dram_tensor(name, list(shape), dt, kind="ExternalOutput").ap()

    def dscr(name, shape, dt=BF16):
        return nc.dram_tensor(name, list(shape), dt, kind="Internal").ap()

    xp = din("xp", [PROMPT_TOK, D])
    xs = din("xs", [SAMPLE_TOK, D])
    sp_in = din("sparams", [40, 128])
    cdk = din("cdk", [512, 512])
    cdv = din("cdv", [512, 512])
    cgk = din("cgk", [512, 128])
    cgv = din("cgv", [512, 128])
    w_ada = din("w_ada", [D, 9 * D])
    b_ada = din("b_ada", [1, 9 * D])
    w_gu = [din("w_ff1_gu", [D, 2 * DFF]), din("w_ff2_gu", [D, 2 * DFF])]
    w_dn = [din("w_ff1_down", [DFF, D]), din("w_ff2_down", [DFF, D])]
    w_in = din("w_in", [D, DIN])
    w_out = din("w_out", [D, D])
    qk_norm = din("qk_norm", [1, 128])
    lam_p = din("lam_p", [1, 256])
    subln = din("subln", [1, 128])
    fnorm = din("final_norm", [1, D])
    rope_c = din("rope_c", [SAMPLE_TOK, 64])
    rope_s = din("rope_s", [SAMPLE_TOK, 64])
    ident_f = din("ident_f", [128, 128])
    ident_b = din("ident_b", [128, 128], BF16)

    y_p = dout("y_p", [PROMPT_TOK, D])
    y_s = dout("y_s", [OWN_TOK, D])
    o_dk = dout("o_dk", [PROMPT_TOK, 512])
    o_dv = dout("o_dv", [PROMPT_TOK, 512])
    o_gk = dout("o_gk", [PROMPT_TOK, 128])
    o_gv = dout("o_gv", [PROMPT_TOK, 128])

    s_gu = [dscr(f"s_gu{i}", [NJB, 128, 8, 512]) for i in (1, 2)]
    s_dn = [dscr(f"s_dn{i}", [DFF, D]) for i in (1, 2)]
    s_in = dscr("s_in", [5, 128, 8, 512])
    s_out = dscr("s_out", [2, 128, 8, 512])
    s_kt = dscr("s_kt", [5, 128, NKEYS])
    s_vd = dscr("s_vd", [4, 128, NKT, 128])
    s_vg = dscr("s_vg", [2, 128, NKT, 128])
    s_x1 = dscr("s_x1", [OWN_TOK, D], F32)
    s_mod = dscr("s_mod", [2, 9 * D], F32)
    dbg_out = dout("dbg", dbg) if dbg is not None else None

    def sb(name, shape, dt=F32):
        return es.enter_context(nc.sbuf_tensor(name, list(shape), dt))

    XB = [sb("X0", [128, NT, D]), sb("X1", [128, NT, D])]
    xcur = [0]
    XN = sb("XN", [128, NT, D])
    HT = sb("HT", [128, 8, ST], BF16)
    ARENA = sb("ARENA", [128, 13312], BF16)
    HID = ARENA[:, 0:NJ * ST].rearrange("p (j n) -> p j n", n=ST)
    SG = sb("SG", [128, 2, ST])
    AR = sb("AR", [128, 4, 8 * 512], BF16)
    BR = sb("BR", [128, 3, 2 * 512], BF16)
    QB = sb("QB", [128, NT, 512], BF16)
    QT = sb("QT", [128, 16, ST], BF16)
    KT = sb("KT", [128, 5, ST], BF16)
    VD = sb("VD", [128, 4, NT, 128], BF16)
    VG = sb("VG", [128, 2, NT, 128], BF16)
    ONESB = sb("ONESB", [128, 128], BF16)
    ONESF = sb("ONESF", [128, 128])
    SGC = sb("SGC", [128, 1])
    EPSC = sb("EPSC", [128, 1])
    ACCS = sb("ACCS", [128, 2, 512])
    ACCS_B = sb("ACCS_B", [128, 2, 512])
    OT = sb("OT", [128, 8, ST], BF16)
    GATE = sb("GATE", [128, 3, D])
    FGB = sb("FGB", [128, D])
    SGB = sb("SGB", [128, 128])
    QKNB = sb("QKNB", [128, 2, 64])
    LPB = sb("LPB", [128, 4, 64])
    IDF = sb("IDF", [128, 128])
    IDB = sb("IDB", [128, 128], BF16)
    SPR = sb("SPR", [40, 128])
    SPT = sb("SPT", [128, 40])
    MR = sb("MR", [72, 128])
    MODT = sb("MODT", [128, 9, 8])
    SC = sb("SC", [128, 8, 2])
    GS = sb("GS", [128, 3, 8])
    SM = sb("SM", [128, 64])
    SS = sb("SS", [128, 64])
    AS = sb("AS", [128, 64])
    NEGH = sb("NEGH", [128, 64])
    JUNK = sb("JUNK", [128, D], BF16)
    RC = sb("RC", [128, NT, 64])
    RS = sb("RS", [128, NT, 64])
    RT = sb("RT", [128, 2, 512])
    ON = sb("ON", [128, 3, 512])

    R0 = XN[:, 0:2, :].rearrange("p a (b n) -> p (a b) n", n=512)
    R1 = XN[:, 2:4, :].rearrange("p a (b n) -> p (a b) n", n=512)
    R0K = [("XN", 0), ("XN", 1)]
    R1K = [("XN", 2), ("XN", 3)]
    XNK = R0K + R1K

    KVS_K = [ARENA[:, i * 2304:(i + 1) * 2304] for i in range(2)]
    KVS_V = [ARENA[:, 4608 + i * 2304: 4608 + (i + 1) * 2304] for i in range(2)]
    PT2 = ARENA[:, 9216:9216 + 4096].rearrange("p (b j n) -> p b j n", j=2, n=512)
    ARENAK = [("HID", j) for j in range(NJ)]

    PSA = es.enter_context(nc.psum_tensor("PSA", [128, 8, 512], F32))
    PSB = [PSA[:, i, :] for i in range(8)]
    PSH = [PSA[:, i, :].bitcast(BF16) for i in range(8)]

    sems = {e: es.enter_context(nc.semaphore(f"sem_{e}")) for e in ENGS}
    dma_names = (["pc%d" % i for i in range(8)] + ["a%d" % i for i in range(4)] + ["b%d" % i for i in range(3)]
                 + ["wa0", "wa1", "wa2", "wa3", "wa4", "md0", "md1", "md2", "md3", "xl", "xl0", "xl1", "m0", "m1", "m2", "m3", "st0", "st1", "st2", "st3", "st4", "st5", "st6", "st7", "st8", "st9", "st10", "st11",
                    "kk0", "kk1", "kv0", "kv1", "kvw", "x1w", "tbl", "md"])
    dma_sems = {n: es.enter_context(nc.semaphore("dsem_" + n)) for n in dma_names}

    def mm(e, out, lhsT, rhs, start=True, stop=True, **kw):
        return e.matmul(out, lhsT, rhs, start=start, stop=stop, **kw)

    def PK(i):
        return ("PS", i)

    pc_i = [0]

    def precast(out_ap, in_ap, wkey):
        name = "pc%d" % (pc_i[0] % 8)
        pc_i[0] += 1
        P.op("pool", lambda e: e.dma_start(out=out_ap, in_=in_ap), writes=[wkey], dma=name)

    def precast_gu(i):
        for jb in range(NJB):
            for part in range(2):
                c0 = part * DFF + jb * 256
                precast(s_gu[i][jb, :, :, part * 256:(part + 1) * 256],
                        w_gu[i][:, c0:c0 + 256].rearrange("(kc p) c -> p kc c", p=128), ("s_gu", i, jb, part))

    def precast_dn(i):
        for h in range(2):
            precast(s_dn[i][h * 1408:(h + 1) * 1408, :], w_dn[i][h * 1408:(h + 1) * 1408, :], ("s_dn", i, h))

    def precast_mix():
        for b in range(5):
            n = 512 if b < 4 else 256
            precast(s_in[b, :, :, 0:n], w_in[:, b * 512:b * 512 + n].rearrange("(kc p) c -> p kc c", p=128), ("s_in", b))
        for b in range(2):
            precast(s_out[b], w_out[:, b * 512:(b + 1) * 512].rearrange("(kc p) c -> p kc c", p=128), ("s_out", b))

    m_i = [0]

    def ld(out_ap, in_ap, wkeys, rkeys=(), q="sp", dma=None):
        if dma is None:
            dma = "m%d" % (m_i[0] % 4)
            m_i[0] += 1
        return P.op(q, lambda e: e.dma_start(out=out_ap, in_=in_ap), reads=list(rkeys), writes=list(wkeys), dma=dma)

    ld(IDF[:, :], ident_f[:, :], ["IDF"])
    ld(IDB[:, :], ident_b[:, :], ["IDB"])
    ld(SPR[:, :], sp_in[:, :], ["SPR"])
    ld(LPB[:, :, :].rearrange("p a d -> p (a d)"), lam_p.rearrange("a d -> (a d)").partition_broadcast(128), ["LPB"])
    ld(QKNB[:, :, :].rearrange("p a d -> p (a d)"), qk_norm.rearrange("a d -> (a d)").partition_broadcast(128), ["QKNB"])
    ld(SGB[:, :], subln.rearrange("a d -> (a d)").partition_broadcast(128), ["SGB"])
    ld(FGB[:, :], fnorm.rearrange("a d -> (a d)").partition_broadcast(128), ["FGB"])

    precast_gu(0)
    precast_dn(0)

    P.op("pool", lambda e: e.memset(NEGH[:, :], -0.5), writes=["NEGH"])
    P.op("pool", lambda e: e.memset(ONESB[:, :], 1.0), writes=["ONESB"])
    P.op("dve", lambda e: e.memset(QT[:, :, :], 0.0), writes=[("QT", c) for c in range(8)])
    P.op("pool", lambda e: e.memset(ONESF[:, :], 1.0), writes=["ONESF"])
    P.op("pool", lambda e: e.memset(EPSC[:, :], EPS), writes=["EPSC"])
    ld(SGC[:, :], subln.rearrange("a d -> d a"), ["SGC"])
    P.op("dve", lambda e: e.tensor_scalar(SGC[:, :], SGC[:, :], 1.0 - LAM_INIT, None, ALU.mult), reads=["SGC"], writes=["SGC"])

    P.op("pe", lambda e: e.transpose(PSB[0][:, 0:40], SPR[:, :], IDF[0:40, 0:40]),
         reads=["SPR", "IDF"], writes=[PK(0)])
    P.op("dve", lambda e: e.tensor_copy(SPT[:, :], PSB[0][:, 0:40]), reads=[PK(0)], writes=["SPT"])
    P.op("act", lambda e: e.activation(SC[:, :, :].rearrange("p kc g -> p g kc"),
                                       SPT[:, 0:16].rearrange("p (g kc) -> p g kc", g=2), AF.Silu),
         reads=["SPT"], writes=["SC"])

    P.op("dve", lambda e: e.tensor_tensor(out=RT[:, 0, 0:128].rearrange("p (a d) -> p a d", a=2),
                                          in0=LPB[:, :, :].rearrange("p (a b) d -> p a b d", b=2)[:, :, 0, :],
                                          in1=LPB[:, :, :].rearrange("p (a b) d -> p a b d", b=2)[:, :, 1, :], op=ALU.mult),
         reads=["LPB"], writes=[("RT", 0)])
    P.op("dve", lambda e: e.tensor_reduce(out=SM[:, 2:4], in_=RT[:, 0, 0:128].rearrange("p (a d) -> p a d", a=2),
                                          axis=AX.X, op=ALU.add), reads=[("RT", 0)], writes=["SM_l"])
    P.op("act", lambda e: e.activation(SM[:, 4:6], SM[:, 2:4], AF.Exp), reads=["SM_l"], writes=["SM_e"])
    P.op("dve", lambda e: e.tensor_tensor(out=SM[:, 1:2], in0=SM[:, 5:6], in1=SM[:, 4:5], op=ALU.subtract),
         reads=["SM_e"], writes=["SM_d"])
    P.op("dve", lambda e: e.tensor_scalar(SM[:, 0:1], SM[:, 1:2], -LAM_INIT, None, ALU.add),
         reads=["SM_d"], writes=["NLAM"])
    P.op("dve", lambda e: e.tensor_scalar(SGB[:, :], SGB[:, :], 1.0 - LAM_INIT, None, ALU.mult),
         reads=["SGB"], writes=["SGB"])
    NLAM = SM[:, 0:1]

    ARF = ARENA[:, 0:12288].bitcast(F32)
    WAV = [XN[:, 0:2, :].rearrange("p a d -> p (a d)").rearrange("p (kc n) -> p kc n", n=256),
           XN[:, 2:4, :].rearrange("p a d -> p (a d)").rearrange("p (kc n) -> p kc n", n=256)]
    WAV += [ARF[:, i * 2048:(i + 1) * 2048].rearrange("p (kc n) -> p kc n", n=256) for i in range(3)]
    WAK = [R0K, R1K, [("HID", j) for j in range(0, 8)], [("HID", j) for j in range(8, 16)], [("HID", j) for j in range(16, 22)] + ["ARX"]]
    NWS = 4
    MB = ON[0:2, :, :].rearrange("p a n -> p (a n)")[:, 0:1024].rearrange("p (s n) -> p s n", n=256)
    BAB = ACCS[0:2, :, :].rearrange("p a n -> p (a n)").rearrange("p (s n) -> p s n", n=256)
    for blk in range(36):
        slot = blk % NWS
        c0 = blk * 256
        ld(WAV[slot], w_ada[:, c0:c0 + 256].rearrange("(kc p) c -> p kc c", p=128), WAK[slot], dma="wa%d" % slot)
        ld(BAB[:, slot, 0:256], b_ada[:, c0:c0 + 256].rearrange("a d -> (a d)").partition_broadcast(2), [("BAB", slot)])
        for kc in range(8):
            P.op("pe", (lambda e, slot=slot, kc=kc: mm(e, PSB[2 + slot][0:2, 0:256], SC[:, kc, :], WAV[slot][:, kc, :],
                                                     start=(kc == 0), stop=(kc == 7))),
                 reads=WAK[slot] + ["SC"], writes=[PK(2 + slot)])
        P.op("dve", (lambda e, slot=slot: e.tensor_tensor(out=MB[:, slot, 0:256], in0=PSB[2 + slot][0:2, 0:256],
                                                          in1=BAB[:, slot, 0:256], op=ALU.add)),
             reads=[PK(2 + slot), ("BAB", slot)], writes=[("MB", slot)])
        ld(s_mod[:, c0:c0 + 256], MB[:, slot, 0:256], ["s_mod"], rkeys=[("MB", slot)], q="act", dma="md%d" % (blk % 4))

    def load_group(g):
        for i, m in enumerate((2, 5, 8)):
            ld(GATE[:, i, :], s_mod[g, m * D:(m + 1) * D].partition_broadcast(128), [("GATE", i)], rkeys=["s_mod"])
            if m != 5:
                P.op("dve", (lambda e, i=i: e.tensor_scalar(GATE[:, i, :], GATE[:, i, :], 0.5, None, ALU.mult)),
                     reads=[("GATE", i)], writes=[("GATE", i)])
        ld(MR[:, :], s_mod[g, :].rearrange("(r p) -> r p", p=128), ["MR"], rkeys=["s_mod"])
        P.op("pe", lambda e: e.transpose(PSB[0][:, 0:72], MR[:, :], IDF[0:72, 0:72]), reads=["MR", "IDF"], writes=[PK(0)])
        P.op("dve", lambda e: e.tensor_copy(MODT[:, :, :].rearrange("p m c -> p (m c)"), PSB[0][:, 0:72]),
             reads=[PK(0)], writes=["MODT"])
        for i in range(3):
            P.op("dve", (lambda e, i=i: e.scalar_tensor_tensor(out=GS[:, i, :], in0=MODT[:, 3 * i + 1, :], scalar=1.0,
                                                                in1=SPT[:, 16 + 8 * i:24 + 8 * i], op0=ALU.add, op1=ALU.mult)),
                 reads=["MODT", "SPT"], writes=["GS"])

    a_i = [0]
    b_i = [0]

    def load_A(src_ap, rkeys, n=512):
        slot = a_i[0] % 4
        a_i[0] += 1
        P.op("sp", lambda e: e.dma_start(out=AR[:, slot, :].rearrange("p (k n) -> p k n", n=512)[:, :, 0:n], in_=src_ap[:, :, 0:n]),
             reads=list(rkeys), writes=[("A", slot)], dma="a%d" % slot)
        return AR[:, slot, :].rearrange("p (k n) -> p k n", n=512), ("A", slot)

    def load_B(i, jb, fh):
        slot = b_i[0] % 3
        b_i[0] += 1
        src = s_dn[i][jb * 256:(jb + 1) * 256, fh * 512:(fh + 1) * 512].rearrange("(jj p) f -> p jj f", p=128)
        P.op("sp", lambda e: e.dma_start(out=BR[:, slot, :].rearrange("p (jj f) -> p jj f", jj=2), in_=src),
             reads=[("s_dn", i, 0), ("s_dn", i, 1)], writes=[("B", slot)], dma="b%d" % slot)
        return BR[:, slot, :].rearrange("p (jj f) -> p jj f", jj=2), ("B", slot)

    st_i = [0]

    def store(out_ap, in_ap, rkeys, wkeys=(), q=None):
        name = "st%d" % (st_i[0] % 12)
        st_i[0] += 1
        P.op(q or STQ[0], lambda e: e.dma_start(out=out_ap, in_=in_ap), reads=list(rkeys), writes=list(wkeys), dma=name)

    def rstd_from_ss(n, scale, outcols):
        P.op("dve", lambda e: e.tensor_scalar(SS[:, 32:32 + n], SS[:, 0:n], scale, EPS, ALU.mult, ALU.add),
             reads=["SS"], writes=["VE"])
        P.op("pool", lambda e: e.tensor_tensor(out=SM[:, outcols:outcols + n], in0=SS[:, 32:32 + n], in1=NEGH[:, 0:n], op=ALU.pow),
             reads=["VE", "NEGH"], writes=["RSTD"])

    def norm_stage(i):
        Xc = XB[xcur[0]]
        xk = xcur[0]
        for t in range(NT):
            P.op("act", (lambda e, t=t: e.activation(JUNK[:, :], Xc[:, t, :], AF.Square, accum_out=SS[:, t:t + 1])),
                 reads=[("X", xk, t)], writes=["SS"])
        rstd_from_ss(NT, 1.0 / D, 8)
        for t in range(NT):
            P.op("act", (lambda e, t=t: e.activation(XN[:, t, :], Xc[:, t, :], AF.Copy, scale=SM[:, 8 + t:9 + t])),
                 reads=[("X", xk, t), "RSTD"], writes=[("XN", t)])
        for c in range(8):
            bank = c % 2
            for t in range(NT):
                P.op("pe", (lambda e, t=t, c=c, bank=bank: e.transpose(PSB[bank][:, t * 128:(t + 1) * 128],
                                                                      XN[:, t, c * 128:(c + 1) * 128], IDF[:, :])),
                     reads=[("XN", t), "IDF"], writes=[PK(bank)])
            P.op("dve", (lambda e, c=c, bank=bank, i=i: e.tensor_scalar(HT[:, c, :], PSB[bank], GS[:, i, c:c + 1],
                                                                        MODT[:, 3 * i, c:c + 1], ALU.mult, ALU.add)),
                 reads=[PK(bank), "GS", "MODT"], writes=[("HT", c)])

    def resid(gate_i, fh):
        Xc = XB[xcur[0]]
        xk = xcur[0]
        for t in range(NT):
            P.op("dve", (lambda e, t=t: e.tensor_tensor(out=RT[:, t % 2, :], in0=PSB[4 + t],
                                                        in1=GATE[:, gate_i, fh * 512:(fh + 1) * 512], op=ALU.mult)),
                 reads=[PK(4 + t), ("GATE", gate_i)], writes=[("RT", t % 2)])
            P.op("dve", (lambda e, t=t: e.tensor_tensor(out=Xc[:, t, fh * 512:(fh + 1) * 512], in0=Xc[:, t, fh * 512:(fh + 1) * 512],
                                                        in1=RT[:, t % 2, :], op=ALU.add)),
                 reads=[("RT", t % 2), ("X", xk, t)], writes=[("X", xk, t)])

    def ffn_stage(i, gate_i):
        for jb in range(NJB):
            A, akey = load_A(s_gu[i][jb], [("s_gu", i, jb, 0), ("s_gu", i, jb, 1)])
            for jj in range(2):
                j = 2 * jb + jj
                pg, pu = (j % 2) * 2, (j % 2) * 2 + 1
                for kc in range(8):
                    P.op("pe", (lambda e, kc=kc, jj=jj, pg=pg, A=A: mm(e, PSB[pg], A[:, kc, jj * 128:(jj + 1) * 128], HT[:, kc, :],
                                                                       start=(kc == 0), stop=(kc == 7))),
                         reads=[akey, ("HT", kc)], writes=[PK(pg)])
                for kc in range(8):
                    P.op("pe", (lambda e, kc=kc, jj=jj, pu=pu, A=A: mm(e, PSB[pu], A[:, kc, 256 + jj * 128:256 + (jj + 1) * 128], HT[:, kc, :],
                                                                       start=(kc == 0), stop=(kc == 7))),
                         reads=[akey, ("HT", kc)], writes=[PK(pu)])
                sgs = j % 2
                P.op("act", (lambda e, sgs=sgs, pg=pg: e.activation(SG[:, sgs, :], PSB[pg], AF.Silu)),
                     reads=[PK(pg)], writes=[("SG", sgs)])
                P.op("dve", (lambda e, sgs=sgs, pu=pu, j=j: e.tensor_tensor(out=HID[:, j, :], in0=SG[:, sgs, :], in1=PSB[pu], op=ALU.mult)),
                     reads=[("SG", sgs), PK(pu)], writes=[("HID", j)])
        for fh in range(2):
            for jb in range(NJB):
                B, bkey = load_B(i, jb, fh)
                for jj in range(2):
                    j = 2 * jb + jj
                    for t in range(NT):
                        P.op("pe", (lambda e, t=t, j=j, jj=jj, B=B: mm(e, PSB[4 + t], HID[:, j, t * 128:(t + 1) * 128], B[:, jj, :],
                                                                       start=(j == 0), stop=(j == NJ - 1))),
                             reads=[bkey, ("HID", j)], writes=[PK(4 + t)])
            resid(gate_i, fh)

    def transposes_bf(dst, dst_key, src_fn, src_keys, nch, c0=0, split=False):
        for c in range(nch):
            bank = c % 2
            for t in range(NT):
                P.op("pe", (lambda e, t=t, c=c, bank=bank: e.transpose(PSH[bank][:, t * 128:(t + 1) * 128], src_fn(t, c), IDB[:, :])),
                     reads=list(src_keys) + ["IDB"], writes=[PK(bank)])
            if split:
                for hf in range(2):
                    P.op("act", (lambda e, c=c, bank=bank, hf=hf: e.activation(dst[64 * hf:64 * hf + 64, 2 * (c0 + c) + hf, :],
                                                                              PSH[bank][64 * hf:64 * hf + 64, 0:ST], AF.Copy)),
                         reads=[PK(bank)], writes=[(dst_key, c0 + c)])
            else:
                P.op("act", (lambda e, c=c, bank=bank: e.activation(dst[:, c0 + c, :], PSH[bank][:, 0:ST], AF.Copy)),
                     reads=[PK(bank)], writes=[(dst_key, c0 + c)])

    def win_block(b):
        n = 512 if b < 4 else 256
        A, akey = load_A(s_in[b], [("s_in", b)], n)
        for t in range(NT):
            for kc in range(8):
                P.op("pe", (lambda e, t=t, kc=kc, A=A, n=n: mm(e, PSB[4 + t][:, 0:n], HT[:, kc, t * 128:(t + 1) * 128], A[:, kc, 0:n],
                                                               start=(kc == 0), stop=(kc == 7))),
                     reads=[akey, ("HT", kc)], writes=[PK(4 + t)])

    def rope(src_fn, src_keys, nh, dst_fn, dst_keys, t, addview=None):
        P.op("dve", (lambda e: e.tensor_tensor(out=RT[:, 0, 0:nh * 64].rearrange("p (h d) -> p h d", d=64),
                                               in0=src_fn().rearrange("p (h d) -> p h d", d=64),
                                               in1=RC[:, t, :].unsqueeze(1).broadcast_to([128, nh, 64]), op=ALU.mult)),
             reads=list(src_keys) + ["RC"], writes=[("RT", 0)])
        for b in range(2):
            P.op("dve", (lambda e, b=b: e.tensor_tensor(
                out=RT[:, 1, 0:nh * 64].rearrange("p (h a b i) -> p h a b i", a=2, b=2, i=16)[:, :, :, b, :],
                in0=src_fn().rearrange("p (h a b i) -> p h a b i", a=2, b=2, i=16)[:, :, :, 1 - b, :],
                in1=RS[:, t, :].rearrange("p (a b i) -> p a b i", a=2, b=2)[:, :, b, :].unsqueeze(1).broadcast_to([128, nh, 2, 16]),
                op=ALU.mult)),
                 reads=list(src_keys) + ["RS"], writes=[("RT", 1)])
        av = addview if addview is not None else (lambda a: a.rearrange("p (h d) -> p h d", d=64))
        P.op("dve", (lambda e: e.tensor_tensor(out=dst_fn(), in0=av(RT[:, 0, 0:nh * 64]),
                                                in1=av(RT[:, 1, 0:nh * 64]), op=ALU.add)),
             reads=[("RT", 0), ("RT", 1)], writes=list(dst_keys))

    def headnorm(R, Rk, col0, nh, gi):
        for t in range(NT):
            P.op("act", (lambda e, t=t: e.activation(RT[:, 0, 0:nh * 64], R[:, t, col0:col0 + nh * 64], AF.Square)),
                 reads=list(Rk), writes=[("RT", 0)])
            P.op("dve", (lambda e, t=t: e.tensor_reduce(out=SS[:, t * nh:(t + 1) * nh],
                                                        in_=RT[:, 0, 0:nh * 64].rearrange("p (h d) -> p h d", d=64),
                                                        axis=AX.X, op=ALU.add)),
                 reads=[("RT", 0)], writes=["SS"])
        rstd_from_ss(NT * nh, 1.0 / 64, 16)
        for t in range(NT):
            P.op("dve", (lambda e, t=t: e.tensor_tensor(
                out=R[:, t, col0:col0 + nh * 64].rearrange("p (h d) -> p h d", d=64),
                in0=R[:, t, col0:col0 + nh * 64].rearrange("p (h d) -> p h d", d=64),
                in1=SM[:, 16 + t * nh:16 + (t + 1) * nh].unsqueeze(2).broadcast_to([128, nh, 64]), op=ALU.mult)),
                 reads=list(Rk) + ["RSTD"], writes=list(Rk))
            P.op("dve", (lambda e, t=t: e.tensor_tensor(
                out=R[:, t, col0:col0 + nh * 64].rearrange("p (h d) -> p h d", d=64),
                in0=R[:, t, col0:col0 + nh * 64].rearrange("p (h d) -> p h d", d=64),
                in1=QKNB[:, gi, :].unsqueeze(1).broadcast_to([128, nh, 64]), op=ALU.mult)),
                 reads=list(Rk) + ["QKNB"], writes=list(Rk))

    def gq_dst(t):
        return QB[:, t, :].rearrange("p (g n d) -> p n g d", g=4, n=2)

    def mix_q(sample):
        win_block(0)
        for t in range(NT):
            if sample:
                rope(lambda t=t: PSB[4 + t], [PK(4 + t)], 8, lambda t=t: QB[:, t, :].rearrange("p (h d) -> p h d", d=64), ["QB"], t)
            else:
                P.op("act", (lambda e, t=t: e.activation(QB[:, t, :], PSB[4 + t], AF.Copy)), reads=[PK(4 + t)], writes=["QB"])
        transposes_bf(QT, "QT", lambda t, c: QB[:, t, c * 128:(c + 1) * 128], ["QB"], 4, 0, split=True)
        win_block(3)
        for t in range(NT):
            P.op("dve", (lambda e, t=t: e.tensor_copy(R1[:, t, :], PSB[4 + t])), reads=[PK(4 + t)], writes=R1K)
        headnorm(R1, R1K, 0, 8, 0)
        for t in range(NT):
            if sample:
                rope(lambda t=t: R1[:, t, :], R1K, 8, lambda t=t: gq_dst(t), ["QB"], t,
                     addview=lambda a: a.rearrange("p (n g d) -> p n g d", n=2, g=4))
            else:
                P.op("dve", (lambda e, t=t: e.tensor_copy(gq_dst(t), R1[:, t, :].rearrange("p (n g d) -> p n g d", n=2, g=4))),
                     reads=R1K, writes=["QB"])
        transposes_bf(QT, "QT", lambda t, c: QB[:, t, c * 128:(c + 1) * 128], ["QB"], 4, 4, split=True)

    def mix_kv(sample, tok0, kt0):
        win_block(1)
        for t in range(NT):
            if sample:
                rope(lambda t=t: PSB[4 + t], [PK(4 + t)], 8, lambda t=t: QB[:, t, :].rearrange("p (h d) -> p h d", d=64), ["QB"], t)
            else:
                P.op("act", (lambda e, t=t: e.activation(R0[:, t, :], PSB[4 + t], AF.Copy)), reads=[PK(4 + t)], writes=R0K)
                P.op("dve", (lambda e, t=t: e.tensor_copy(QB[:, t, :], R0[:, t, :])), reads=R0K, writes=["QB"])
        if not sample and KVSTOP[0] != 11:
            store(o_dk[tok0:tok0 + ST, :].rearrange("(t p) f -> p t f", p=128), R0, R0K)
        if KVSTOP[0] == 11:
            return
        transposes_bf(KT, "KT", lambda t, c: QB[:, t, c * 128:(c + 1) * 128], ["QB"], 4, 0)
        if KVSTOP[0] == 1:
            return
        win_block(2)
        for t in range(NT):
            if not sample:
                P.op("act", (lambda e, t=t: e.activation(R1[:, t, :], PSB[4 + t], AF.Copy)), reads=[PK(4 + t)], writes=R1K)
                P.op("dve", (lambda e, t=t: e.tensor_copy(VD[:, :, t, :], R1[:, t, :].rearrange("p (h d) -> p h d", d=128))),
                     reads=R1K, writes=["VD"])
            else:
                P.op("dve", (lambda e, t=t: e.tensor_copy(VD[:, :, t, :], PSB[4 + t].rearrange("p (h d) -> p h d", d=128))),
                     reads=[PK(4 + t)], writes=["VD"])
        if not sample:
            store(o_dv[tok0:tok0 + ST, :].rearrange("(t p) f -> p t f", p=128), R1, R1K)
        if KVSTOP[0] == 2:
            return
        win_block(4)
        for t in range(NT):
            P.op("dve", (lambda e, t=t: e.tensor_copy(R0[:, t, 0:256], PSB[4 + t][:, 0:256])), reads=[PK(4 + t)], writes=R0K)
        headnorm(R0, R0K, 0, 2, 1)
        for t in range(NT):
            if sample:
                rope(lambda t=t: R0[:, t, 0:128], R0K, 2, lambda t=t: QB[:, t, 0:128].rearrange("p (h d) -> p h d", d=64), ["QB"], t)
            else:
                P.op("act", (lambda e, t=t: e.activation(QB[:, t, 0:128], R0[:, t, 0:128], AF.Copy)), reads=R0K, writes=["QB"])
            P.op("dve", (lambda e, t=t: e.tensor_copy(VG[:, :, t, :].rearrange("p n (r d) -> p n r d", r=2),
                                                      R0[:, t, 128:256].rearrange("p (h d) -> p h d", d=64).unsqueeze(2).broadcast_to([128, 2, 2, 64]))),
                 reads=R0K, writes=["VG"])
        if not sample:
            store(o_gk[tok0:tok0 + ST, :].rearrange("(t p) f -> p t f", p=128), R0[:, :, 0:128], R0K)
            store(o_gv[tok0:tok0 + ST, :].rearrange("(t p) f -> p t f", p=128), R0[:, :, 128:256], R0K)
        transposes_bf(KT, "KT", lambda t, c: QB[:, t, 0:128], ["QB"], 1, 4)
        if sample:
            store_kv(kt0)

    def store_kv(kt0):
        store(s_kt[:, :, kt0 * 128:kt0 * 128 + ST].rearrange("c p n -> p c n"), KT[:, :, :], [("KT", c) for c in range(5)], [("s_kt", kt0)], q="sp")
        store(s_vd[:, :, kt0:kt0 + NT, :].rearrange("h p t e -> p h (t e)"), VD[:, :, :, :].rearrange("p h t e -> p h (t e)"),
              ["VD"], [("s_vd", h, kt0) for h in range(4)], q="sp")
        store(s_vg[:, :, kt0:kt0 + NT, :].rearrange("n p t e -> p n (t e)"), VG[:, :, :, :].rearrange("p n t e -> p n (t e)"),
              ["VG"], [("s_vg", n, kt0) for n in range(2)], q="sp")

    s_i = [0]
    p_i = [0]

    pending = [None]
    ab_i = [0]

    def attn_unit(kind, u, subs, q0, nq, ktiles):
        ab = ab_i[0] % 2
        ab_i[0] += 1
        attn_main(kind, u, subs, q0, nq, ktiles, ab)
        if pending[0] is not None:
            attn_norm(*pending[0])
        pending[0] = (kind, u, subs, q0, nq, ab)

    def attn_flush():
        if pending[0] is not None:
            attn_norm(*pending[0])
        pending[0] = None

    def attn_main(kind, u, subs, q0, nq, ktiles, ab):
        nk = len(ktiles)
        info = {}
        LA = 1
        ACC = ACCS if ab == 0 else ACCS_B
        ak = "ACCS%d" % ab
        ob = 4 + 2 * ab

        def emit_qk(ki):
            kfn, kkeys, v_ap, vkeys = ktiles[ki]
            r = s_i[0] % 2
            s_i[0] += 1
            for j in range(2):
                if kind == "diff":
                    r0, chunk = 64 * subs[j], u
                else:
                    r0, chunk = 64 * u, 4 + subs[j]
                P.op("pe", (lambda e, j=j, r0=r0, chunk=chunk: mm(e, PSB[2 * r + j][:, 0:nq], kfn(),
                                                                  QT[:, 2 * chunk + r0 // 64, q0:q0 + nq])),
                     reads=list(kkeys) + [("QT", chunk)], writes=[PK(2 * r + j)])
            slot = p_i[0] % 4
            p_i[0] += 1
            P.op("act", (lambda e: e.activation(PT2[:, slot, :, 0:nq], PSA[:, 2 * r:2 * r + 2, 0:nq], AF.Exp, scale=0.125)),
                 reads=[PK(2 * r), PK(2 * r + 1)],
                 writes=[("PT", slot)] + ([("HID", jj) for jj in range(18, NJ)] if ki == 0 else []))
            if ki == 0:
                P.op("dve", (lambda e: e.tensor_copy(ACC[:, :, 0:nq], PT2[:, slot, :, 0:nq])), reads=[("PT", slot)], writes=[ak, ak + "b"])
            else:
                cs = (nq * 3) // 4
                P.op("dve", (lambda e: e.tensor_tensor(out=ACC[:, :, 0:cs], in0=ACC[:, :, 0:cs], in1=PT2[:, slot, :, 0:cs], op=ALU.add)),
                     reads=[("PT", slot), ak], writes=[ak])
                P.op("pool", (lambda e: e.tensor_tensor(out=ACC[:, :, cs:nq], in0=ACC[:, :, cs:nq], in1=PT2[:, slot, :, cs:nq], op=ALU.add)),
                     reads=[("PT", slot), ak + "b"], writes=[ak + "b"])
            info[ki] = slot

        def emit_pv(ki):
            kfn, kkeys, v_ap, vkeys = ktiles[ki]
            slot = info[ki]
            for j in range(2):
                P.op("pe", (lambda e, j=j: mm(e, PSB[ob + j][:, 0:nq], v_ap, PT2[:, slot, j, 0:nq], start=(ki == 0), stop=(ki == nk - 1))),
                     reads=[("PT", slot)] + list(vkeys), writes=[PK(ob + j)])

        for kk in range(nk + LA):
            if kk < nk:
                emit_qk(kk)
            if kk - LA >= 0:
                emit_pv(kk - LA)

    def attn_norm(kind, u, subs, q0, nq, ab):
        ACC = ACCS if ab == 0 else ACCS_B
        ak = "ACCS%d" % ab
        ob = 4 + 2 * ab
        r = s_i[0] % 2
        s_i[0] += 1
        for j in range(2):
            P.op("pe", (lambda e, j=j: mm(e, PSB[2 * r + j][:, 0:nq], ONESF[:, :], ACC[:, j, 0:nq])), reads=[ak, ak + "b", "ONESF"], writes=[PK(2 * r + j)])
        for j in range(2):
            P.op("dve", (lambda e, j=j: e.reciprocal(ON[:, 2, 0:nq], PSB[2 * r + j][:, 0:nq])), reads=[PK(2 * r + j)], writes=[("ON", 2)])
            if kind == "diff":
                P.op("dve", (lambda e, j=j: e.tensor_tensor(out=ON[:, j, 0:nq], in0=PSB[ob + j][:, 0:nq], in1=ON[:, 2, 0:nq], op=ALU.mult)),
                     reads=[PK(ob + j), ("ON", 2)], writes=[("ON", j)])
            else:
                hq = 4 * u + subs[j]
                r0, chunk = 64 * (hq % 2), 4 + hq // 2
                P.op("dve", (lambda e, j=j, r0=r0, chunk=chunk: e.tensor_tensor(
                    out=OT[r0:r0 + 64, chunk, q0:q0 + nq], in0=PSB[ob + j][r0:r0 + 64, 0:nq], in1=ON[r0:r0 + 64, 2, 0:nq], op=ALU.mult)),
                     reads=[PK(ob + j), ("ON", 2)], writes=[("OT", chunk)])
        if kind == "gqa":
            return
        P.op("dve", lambda e: e.scalar_tensor_tensor(out=ON[:, 0, 0:nq], in0=ON[:, 1, 0:nq], scalar=NLAM, in1=ON[:, 0, 0:nq],
                                                     op0=ALU.mult, op1=ALU.add),
             reads=[("ON", 0), ("ON", 1), "NLAM"], writes=[("ON", 0)])
        P.op("act", lambda e: e.activation(ON[:, 2, 0:nq], ON[:, 0, 0:nq], AF.Square), reads=[("ON", 0)], writes=[("ON", 2)])
        r2 = s_i[0] % 2
        s_i[0] += 1
        P.op("pe", lambda e: mm(e, PSB[2 * r2][:, 0:nq], ONESF[:, :], ON[:, 2, 0:nq]), reads=[("ON", 2), "ONESF"], writes=[PK(2 * r2)])
        P.op("act", lambda e: e.activation(ON[:, 1, 0:nq], PSB[2 * r2][:, 0:nq], AF.Ln, scale=1.0 / 128, bias=EPSC[:, 0:1]),
             reads=[PK(2 * r2), "EPSC"], writes=[("ON", 1)])
        P.op("act", lambda e: e.activation(ON[:, 1, 0:nq], ON[:, 1, 0:nq], AF.Exp, scale=-0.5), reads=[("ON", 1)], writes=[("ON", 1)])
        P.op("dve", lambda e: e.scalar_tensor_tensor(out=OT[:, u, q0:q0 + nq], in0=ON[:, 0, 0:nq], scalar=SGC[:, 0:1], in1=ON[:, 1, 0:nq],
                                                     op0=ALU.mult, op1=ALU.mult),
             reads=[("ON", 0), ("ON", 1), "SGC"], writes=[("OT", u)])

    def attn_prompt():
        for bb in range(2):
            for h in range(4):
                kts = []
                for t in (2 * bb, 2 * bb + 1):
                    kts.append(((lambda t=t, h=h: KT[:, h, t * 128:(t + 1) * 128]), [("KT", h)],
                                VD[:, h, t, :], ["VD"]))
                attn_unit("diff", h, (0, 1), bb * 256, 256, kts)
            for n in range(2):
                for gp in range(2):
                    kts = []
                    for t in (2 * bb, 2 * bb + 1):
                        kts.append(((lambda t=t: KT[:, 4, t * 128:(t + 1) * 128]), [("KT", 4)],
                                    VG[:, n, t, :], ["VG"]))
                    attn_unit("gqa", n, (2 * gp, 2 * gp + 1), bb * 256, 256, kts)
        attn_flush()

    kv_i = [0]

    def attn_sample():
        units = [("diff", h, (0, 1)) for h in range(4)] + [("gqa", n, (2 * gp, 2 * gp + 1)) for n in range(2) for gp in range(2)]
        for kind, u, subs in units:
            kts = []
            for half in range(2):
                buf = kv_i[0] % 2
                kv_i[0] += 1
                chunk = u if kind == "diff" else 4
                P.op("sp", (lambda e, buf=buf, chunk=chunk, half=half: e.dma_start(
                    out=KVS_K[buf], in_=s_kt[chunk, :, half * 2304:(half + 1) * 2304])),
                     reads=[("s_kt", k0) for k0 in range(0, NKT, NT)], writes=[("KVK", buf)] + [("HID", j) for j in ((0, 1, 2, 3, 4) if buf == 0 else (4, 5, 6, 7, 8))],
                     dma="kk%d" % buf)
                vs = s_vd if kind == "diff" else s_vg
                vsrc = vs[u, :, half * 18:(half + 1) * 18, :].rearrange("p k e -> p (k e)")
                P.op("sp", (lambda e, buf=buf, vsrc=vsrc: e.dma_start(out=KVS_V[buf], in_=vsrc)),
                     reads=[("s_vd", hh, k0) for hh in range(4) for k0 in range(0, NKT, NT)] + [("s_vg", nn, k0) for nn in range(2) for k0 in range(0, NKT, NT)], writes=[("KVV", buf)] + [("HID", j) for j in ((9, 10, 11, 12, 13) if buf == 0 else (13, 14, 15, 16, 17))],
                     dma="kv%d" % buf)
                for k in range(18):
                    kts.append(((lambda buf=buf, k=k: KVS_K[buf][:, k * 128:(k + 1) * 128]), [("KVK", buf)],
                                KVS_V[buf][:, k * 128:(k + 1) * 128], [("KVV", buf)]))
            attn_unit(kind, u, subs, 0, ST, kts)
        attn_flush()

    def out_proj():
        for fh in range(2):
            A, akey = load_A(s_out[fh], [("s_out", fh)])
            for t in range(NT):
                for kc in range(8):
                    P.op("pe", (lambda e, t=t, kc=kc, A=A: mm(e, PSB[4 + t], OT[:, kc, t * 128:(t + 1) * 128], A[:, kc, :],
                                                              start=(kc == 0), stop=(kc == 7))),
                         reads=[akey, ("OT", kc)], writes=[PK(4 + t)])
            resid(1, fh)

    def final_norm(y, tok0):
        Xc = XB[xcur[0]]
        xk = xcur[0]
        for t in range(NT):
            P.op("act", (lambda e, t=t: e.activation(JUNK[:, :], Xc[:, t, :], AF.Square, accum_out=SS[:, t:t + 1])),
                 reads=[("X", xk, t)], writes=["SS"])
        rstd_from_ss(NT, 1.0 / D, 8)
        for t in range(NT):
            P.op("dve", (lambda e, t=t: e.scalar_tensor_tensor(out=XN[:, t, :], in0=Xc[:, t, :], scalar=SM[:, 8 + t:9 + t],
                                                                in1=FGB[:, :], op0=ALU.mult, op1=ALU.mult)),
                 reads=[("X", xk, t), "RSTD", "FGB"], writes=[("XN", t)])
        store(y[tok0:tok0 + ST, :].rearrange("(t p) f -> p t f", p=128), XN[:, :, :], XNK)

    def load_x(src, tok0, buf):
        Xc = XB[buf]
        P.op("sp", lambda e: e.dma_start(out=Xc[:, :, :], in_=src[tok0:tok0 + ST, :].rearrange("(t p) f -> p t f", p=128)),
             reads=[("s_x1", tok0 // ST)] if src is s_x1 else [], writes=[("X", buf, t) for t in range(NT)], dma="xl%d" % buf)

    def load_rope(tok0):
        P.op("sp", lambda e: e.dma_start(out=RC[:, :, :], in_=rope_c[tok0:tok0 + ST, :].rearrange("(t p) f -> p t f", p=128)),
             writes=["RC"], dma="tbl")
        P.op("sp", lambda e: e.dma_start(out=RS[:, :, :], in_=rope_s[tok0:tok0 + ST, :].rearrange("(t p) f -> p t f", p=128)),
             writes=["RS"], dma="tbl")

    first = [True]

    def maybe(fn):
        if first[0]:
            fn()

    tiles = []
    if do_prompt:
        tiles += [("prompt", si, xp, si * ST) for si in range(n_prompt_st)]
    if do_sample:
        tiles += [("s1", si, xs, si * ST) for si in range(n_s1)]
        tiles += [("s2", qi, s_x1, qi * ST) for qi in range(n_s2)]

    def prefetch(k):
        if k < len(tiles):
            ph, idx, src, off = tiles[k]
            load_x(src, off, k % 2)

    def sample_cache_prep():
        load_group(1)
        P.op("sp", lambda e: e.dma_start(out=R0, in_=cdk.rearrange("(t p) f -> p t f", p=128)), writes=R0K, dma="xl")
        for t in range(NT):
            P.op("act", (lambda e, t=t: e.activation(QB[:, t, :], R0[:, t, :], AF.Copy)), reads=R0K, writes=["QB"])
        transposes_bf(KT, "KT", lambda t, c: QB[:, t, c * 128:(c + 1) * 128], ["QB"], 4, 0)
        P.op("sp", lambda e: e.dma_start(out=R1, in_=cdv.rearrange("(t p) f -> p t f", p=128)), writes=R1K, dma="xl")
        for t in range(NT):
            P.op("dve", (lambda e, t=t: e.tensor_copy(VD[:, :, t, :], R1[:, t, :].rearrange("p (h d) -> p h d", d=128))),
                 reads=R1K, writes=["VD"])
        P.op("sp", lambda e: e.dma_start(out=R0[:, :, 0:128], in_=cgk.rearrange("(t p) f -> p t f", p=128)), writes=R0K, dma="xl")
        P.op("sp", lambda e: e.dma_start(out=R0[:, :, 128:256], in_=cgv.rearrange("(t p) f -> p t f", p=128)), writes=R0K, dma="xl")
        for t in range(NT):
            P.op("act", (lambda e, t=t: e.activation(QB[:, t, 0:128], R0[:, t, 0:128], AF.Copy)), reads=R0K, writes=["QB"])
            P.op("dve", (lambda e, t=t: e.tensor_copy(VG[:, :, t, :].rearrange("p n (r d) -> p n r d", r=2),
                                                      R0[:, t, 128:256].rearrange("p (h d) -> p h d", d=64).unsqueeze(2).broadcast_to([128, 2, 2, 64]))),
                 reads=R0K, writes=["VG"])
        transposes_bf(KT, "KT", lambda t, c: QB[:, t, 0:128], ["QB"], 1, 4)
        store_kv(0)

    prefetch(0)
    seen_phase = set()
    for k, (ph, idx, src, off) in enumerate(tiles):
        xcur[0] = k % 2
        if ph == "prompt" and "prompt" not in seen_phase:
            load_group(0)
        if ph == "s1" and "s1" not in seen_phase:
            if first[0]:
                precast_mix()
                precast_gu(1)
                precast_dn(1)
                first[0] = False
            sample_cache_prep()
        if ph == "s2" and "s1" not in seen_phase and "s2" not in seen_phase:
            load_group(1)
        seen_phase.add(ph)
        if ph == "prompt":
            tok0 = off
            steps = [lambda: norm_stage(0), lambda: (prefetch(k + 1), maybe(precast_mix)), lambda: ffn_stage(0, 0),
                     lambda: norm_stage(1), lambda: maybe(lambda: (precast_gu(1), precast_dn(1))), lambda: mix_q(False),
                     lambda: mix_kv(False, tok0, 0), attn_prompt, out_proj, lambda: norm_stage(2), lambda: ffn_stage(1, 2),
                     lambda: final_norm(y_p, tok0)]
            for kk, fn in enumerate(steps):
                if kk < stop_after:
                    fn()
            first[0] = False
        elif ph == "s1":
            load_rope(off)
            norm_stage(0)
            prefetch(k + 1)
            ffn_stage(0, 0)
            if idx >= 4:
                Xc = XB[xcur[0]]
                store(s_x1[(idx - 4) * ST:(idx - 3) * ST, :].rearrange("(t p) f -> p t f", p=128), Xc[:, :, :],
                      [("X", xcur[0], t) for t in range(NT)], [("s_x1", idx - 4)], q="sp")
            norm_stage(1)
            mix_kv(True, off, 4 + idx * NT)
        else:
            load_rope(OWN_TOK + off)
            norm_stage(1)
            prefetch(k + 1)
            mix_q(True)
            attn_sample()
            out_proj()
            norm_stage(2)
            ffn_stage(1, 2)
            final_norm(y_s, off)
    if first[0] and not tiles:
        precast_mix()
        precast_gu(1)
        precast_dn(1)

    P.finalize()
    with nc.Block() as block:
        P.emit(block, sems, dma_sems)
    es.close()
    return nc


def _rope_tables():
    rows = SAMPLE_TOK // 64
    row = np.repeat(np.arange(rows), 64)
    col = np.tile(np.arange(64), rows)
    half = 32
    inv = (10000.0 ** (-np.arange(0, half, 2, dtype=np.float32) / half)).astype(np.float32)
    ang = np.stack([row[:, None].astype(np.float32) * inv, col[:, None].astype(np.float32) * inv], axis=1)
    c = np.cos(ang).astype(np.float32)
    s = np.sin(ang).astype(np.float32)
    C2 = np.stack([c, c], axis=2).reshape(SAMPLE_TOK, 64)
    S2 = np.stack([-s, s], axis=2).reshape(SAMPLE_TOK, 64)
    return np.ascontiguousarray(C2), np.ascontiguousarray(S2)


def make_in_maps(inp):
    f = lambda a: np.ascontiguousarray(np.asarray(a, dtype=np.float32))
    C2, S2 = _rope_tables()
    ident_f = np.eye(128, dtype=np.float32)
    ident_b = np.eye(128, dtype=np.float32).astype(ml_dtypes.bfloat16)
    shared = {
        "w_ada": f(inp["w_ada"][0]), "b_ada": f(inp["b_ada"]).reshape(1, 9 * D),
        "w_ff1_gu": f(inp["w_ff1_gu"][0]), "w_ff2_gu": f(inp["w_ff2_gu"][0]),
        "w_ff1_down": f(inp["w_ff1_down"][0]), "w_ff2_down": f(inp["w_ff2_down"][0]),
        "w_in": f(inp["w_in"][0]), "w_out": f(inp["w_out"][0]),
        "qk_norm": np.concatenate([f(inp["q_norm"][0]), f(inp["k_norm"][0])]).reshape(1, 128),
        "lam_p": np.concatenate([f(inp["lambda_q1"][0]), f(inp["lambda_k1"][0]),
                                 f(inp["lambda_q2"][0]), f(inp["lambda_k2"][0])]).reshape(1, 256),
        "subln": f(inp["subln"]).reshape(1, 128), "final_norm": f(inp["final_norm"]).reshape(1, D),
        "ident_f": ident_f, "ident_b": ident_b,
    }
    norms = np.concatenate([f(inp["norm_ff1"][0]), f(inp["norm_mix"][0]), f(inp["norm_ff2"][0])]).reshape(24, 128)
    xp_all = f(inp["x_prompt"])
    xs_all = f(inp["x_sample"])
    maps = []
    for core in range(8):
        b, half = core // 2, core % 2
        order = np.concatenate([np.arange((1 - half) * OWN_TOK, (2 - half) * OWN_TOK),
                                np.arange(half * OWN_TOK, (half + 1) * OWN_TOK)])
        cv = np.stack([f(inp["c_ctx"]), f(inp["c"])[b]]).reshape(16, 128)
        m = dict(shared)
        m["xp"] = np.ascontiguousarray(xp_all[4 * core:4 * core + 4].reshape(PROMPT_TOK, D))
        m["xs"] = np.ascontiguousarray(xs_all[b][order])
        m["sparams"] = np.ascontiguousarray(np.concatenate([cv, norms], axis=0))
        m["cdk"] = f(inp["cache_diff_k"][b, 0]).reshape(512, 512)
        m["cdv"] = f(inp["cache_diff_v"][b, 0]).reshape(512, 512)
        m["cgk"] = f(inp["cache_gqa_k"][b, 0]).reshape(512, 128)
        m["cgv"] = f(inp["cache_gqa_v"][b, 0]).reshape(512, 128)
        m["rope_c"] = np.ascontiguousarray(C2[order])
        m["rope_s"] = np.ascontiguousarray(S2[order])
        maps.append(m)
    return maps


def kernel(**inp):
    nc = build_nc()
    maps = make_in_maps(inp)
    res = run_bass_kernel_spmd(nc, maps, core_ids=list(range(8)))
    R = res.results
    y_p = np.concatenate([r["y_p"].reshape(4, 256, D) for r in R], axis=0)
    y_s = np.zeros((4, SAMPLE_TOK, D), np.float32)
    for core, r in enumerate(R):
        b, half = core // 2, core % 2
        y_s[b, half * OWN_TOK:(half + 1) * OWN_TOK] = r["y_s"]
    ndk = np.concatenate([r["o_dk"].reshape(4, 1, 256, 4, 2, 64) for r in R], axis=0)
    ndv = np.concatenate([r["o_dv"].reshape(4, 1, 256, 4, 128) for r in R], axis=0)
    ngk = np.concatenate([r["o_gk"].reshape(4, 1, 256, 2, 64) for r in R], axis=0)
    ngv = np.concatenate([r["o_gv"].reshape(4, 1, 256, 2, 64) for r in R], axis=0)
    return (y_p.astype(np.float32), y_s, ndk.astype(np.float32), ndv.astype(np.float32),
            ngk.astype(np.float32), ngv.astype(np.float32))
```

```python
import math
import contextlib
import numpy as np
import ml_dtypes
import concourse.bass as bass
import concourse.mybir as mybir
from concourse.bass_utils import run_bass_kernel_spmd

F32 = mybir.dt.float32
BF16 = mybir.dt.bfloat16
AF = mybir.ActivationFunctionType
ALU = mybir.AluOpType
AX = mybir.AxisListType

D = 1024
DFF = 2816
NJ = 22
NJB = 11
DIN = 2304
EPS = 1e-6
LAM_INIT = 0.8 - 0.6 * math.exp(0.0)
ST = 512
NT = 4
NKT = 36
NKEYS = NKT * 128
PROMPT_TOK = 1024
SAMPLE_TOK = 4096
OWN_TOK = 2048
ENGS = ("pe", "act", "dve", "pool", "sp")
KVSTOP = [0]
STQ = ["act"]


class Op:
    __slots__ = ("eng", "fn", "deps", "dma", "dma_val", "dma_prev", "signal", "sigval", "order")

    def __init__(self, eng, fn, order):
        self.eng = eng
        self.fn = fn
        self.deps = []
        self.dma = None
        self.dma_val = 0
        self.dma_prev = 0
        self.signal = False
        self.sigval = 0
        self.order = order


def _okey(o):
    return ("d", o.dma) if o.dma is not None else ("e", o.eng)


class Prog:
    def __init__(self):
        self.ops = []
        self.last_w = {}
        self.readers = {}
        self.dma_cnt = {}

    def op(self, eng, fn, reads=(), writes=(), dma=None):
        o = Op(eng, fn, len(self.ops))
        cand = {}

        def add(d):
            if d is None:
                return
            k = _okey(d)
            c = cand.get(k)
            if c is None or d.order > c.order:
                cand[k] = d

        for k in reads:
            add(self.last_w.get(k))
        for k in writes:
            add(self.last_w.get(k))
            for r in self.readers.get(k, {}).values():
                add(r)
        if dma is not None:
            o.dma = dma
            o.dma_prev = self.dma_cnt.get(dma, 0)
            o.dma_val = o.dma_prev + 16
            self.dma_cnt[dma] = o.dma_val
        for k in writes:
            self.last_w[k] = o
            self.readers[k] = {}
        ok = _okey(o)
        for k in reads:
            self.readers.setdefault(k, {})[ok] = o
        o.deps = list(cand.values())
        self.ops.append(o)
        return o

    def finalize(self):
        for o in self.ops:
            for d in o.deps:
                if d.dma is None:
                    if d.eng == "pe" and o.eng == "pe" and o.dma is None:
                        continue
                    d.signal = True
        cnt = {e: 0 for e in ENGS}
        for o in self.ops:
            if o.dma is None and o.signal:
                cnt[o.eng] += 1
                o.sigval = cnt[o.eng]

    def emit(self, block, sems, dma_sems):
        per = {e: [o for o in self.ops if o.eng == e] for e in ENGS}
        reg = {"pe": block.tensor, "act": block.scalar, "dve": block.vector,
               "pool": block.gpsimd, "sp": block.sync}

        def make(ename):
            def body(eh):
                known = {}
                mine = set()
                for o in per[ename]:
                    need = {}
                    for d in o.deps:
                        if d.dma is not None:
                            key, val = ("d", d.dma), d.dma_val
                        else:
                            if d.eng == "pe" and ename == "pe" and o.dma is None:
                                continue
                            key, val = ("e", d.eng), d.sigval
                        if need.get(key, 0) < val:
                            need[key] = val
                    if o.dma is not None and o.dma_prev > 0:
                        key = ("d", o.dma)
                        if need.get(key, 0) < o.dma_prev:
                            need[key] = o.dma_prev
                    for key, val in need.items():
                        if known.get(key, 0) >= val:
                            continue
                        known[key] = val
                        s = dma_sems[key[1]] if key[0] == "d" else sems[key[1]]
                        eh.wait_ge(s, val)
                    ins = o.fn(eh)
                    if o.dma is not None:
                        ins.then_inc(dma_sems[o.dma], 16)
                        mine.add(o.dma)
                    elif o.signal:
                        ins.then_inc(sems[ename], 1)
                for name in sorted(mine):
                    total = self.dma_cnt[name]
                    if known.get(("d", name), 0) < total:
                        eh.wait_ge(dma_sems[name], total)
            return body

        for e in ENGS:
            if per[e]:
                reg[e](make(e))


def build_nc(do_prompt=True, do_sample=True, n_prompt_st=2, n_s1=8, n_s2=4, dbg=None, stop_after=99):
    nc = bass.Bass("TRN2", target_bir_lowering=False)
    P = Prog()
    es = contextlib.ExitStack()

    def din(name, shape, dt=F32):
        return nc.dram_tensor(name, list(shape), dt, kind="ExternalInput").ap()

    def dout(name, shape, dt=F32):
        return nc.dram_tensor(name, list(shape), dt, kind="ExternalOutput").ap()

    def dscr(name, shape, dt=BF16):
        return nc.dram_tensor(name, list(shape), dt, kind="Internal").ap()

    xp = din("xp", [PROMPT_TOK, D])
    xs = din("xs", [SAMPLE_TOK, D])
    sp_in = din("sparams", [40, 128])
    cdk = din("cdk", [512, 512])
    cdv = din("cdv", [512, 512])
    cgk = din("cgk", [512, 128])
    cgv = din("cgv", [512, 128])
    w_ada = din("w_ada", [D, 9 * D])
    b_ada = din("b_ada", [1, 9 * D])
    w_gu = [din("w_ff1_gu", [D, 2 * DFF]), din("w_ff2_gu", [D, 2 * DFF])]
    w_dn = [din("w_ff1_down", [DFF, D]), din("w_ff2_down", [DFF, D])]
    w_in = din("w_in", [D, DIN])
    w_out = din("w_out", [D, D])
    qk_norm = din("qk_norm", [1, 128])
    lam_p = din("lam_p", [1, 256])
    subln = din("subln", [1, 128])
    fnorm = din("final_norm", [1, D])
    rope_c = din("rope_c", [SAMPLE_TOK, 64])
    rope_s = din("rope_s", [SAMPLE_TOK, 64])
    ident_f = din("ident_f", [128, 128])
    ident_b = din("ident_b", [128, 128], BF16)

    y_p = dout("y_p", [PROMPT_TOK, D])
    y_s = dout("y_s", [OWN_TOK, D])
    o_dk = dout("o_dk", [PROMPT_TOK, 512])
    o_dv = dout("o_dv", [PROMPT_TOK, 512])
    o_gk = dout("o_gk", [PROMPT_TOK, 128])
    o_gv = dout("o_gv", [PROMPT_TOK, 128])

    s_gu = [dscr(f"s_gu{i}", [NJB, 128, 8, 512]) for i in (1, 2)]
    s_dn = [dscr(f"s_dn{i}", [DFF, D]) for i in (1, 2)]
    s_in = dscr("s_in", [5, 128, 8, 512])
    s_out = dscr("s_out", [2, 128, 8, 512])
    s_kt = dscr("s_kt", [5, 128, NKEYS])
    s_vd = dscr("s_vd", [4, 128, NKT, 128])
    s_vg = dscr("s_vg", [2, 128, NKT, 128])
    s_x1 = dscr("s_x1", [OWN_TOK, D], F32)
    s_mod = dscr("s_mod", [2, 9 * D], F32)
    dbg_out = dout("dbg", dbg) if dbg is not None else None

    def sb(name, shape, dt=F32):
        return es.enter_context(nc.sbuf_tensor(name, list(shape), dt))

    XB = [sb("X0", [128, NT, D]), sb("X1", [128, NT, D])]
    xcur = [0]
    XN = sb("XN", [128, NT, D])
    HT = sb("HT", [128, 8, ST], BF16)
    ARENA = sb("ARENA", [128, 13312], BF16)
    HID = ARENA[:, 0:NJ * ST].rearrange("p (j n) -> p j n", n=ST)
    SG = sb("SG", [128, 2, ST])
    AR = sb("AR", [128, 4, 8 * 512], BF16)
    BR = sb("BR", [128, 3, 2 * 512], BF16)
    QB = sb("QB", [128, NT, 512], BF16)
    QT = sb("QT", [128, 16, ST], BF16)
    KT = sb("KT", [128, 5, ST], BF16)
    VD = sb("VD", [128, 4, NT, 128], BF16)
    VG = sb("VG", [128, 2, NT, 128], BF16)
    ONESB = sb("ONESB", [128, 128], BF16)
    ONESF = sb("ONESF", [128, 128])
    SGC = sb("SGC", [128, 1])
    EPSC = sb("EPSC", [128, 1])
    ACCS = sb("ACCS", [128, 2, 512])
    ACCS_B = sb("ACCS_B", [128, 2, 512])
    OT = sb("OT", [128, 8, ST], BF16)
    GATE = sb("GATE", [128, 3, D])
    FGB = sb("FGB", [128, D])
    SGB = sb("SGB", [128, 128])
    QKNB = sb("QKNB", [128, 2, 64])
    LPB = sb("LPB", [128, 4, 64])
    IDF = sb("IDF", [128, 128])
    IDB = sb("IDB", [128, 128], BF16)
    SPR = sb("SPR", [40, 128])
    SPT = sb("SPT", [128, 40])
    MR = sb("MR", [72, 128])
    MODT = sb("MODT", [128, 9, 8])
    SC = sb("SC", [128, 8, 2])
    GS = sb("GS", [128, 3, 8])
    SM = sb("SM", [128, 64])
    SS = sb("SS", [128, 64])
    AS = sb("AS", [128, 64])
    NEGH = sb("NEGH", [128, 64])
    JUNK = sb("JUNK", [128, D], BF16)
    RC = sb("RC", [128, NT, 64])
    RS = sb("RS", [128, NT, 64])
    RT = sb("RT", [128, 2, 512])
    ON = sb("ON", [128, 3, 512])

    R0 = XN[:, 0:2, :].rearrange("p a (b n) -> p (a b) n", n=512)
    R1 = XN[:, 2:4, :].rearrange("p a (b n) -> p (a b) n", n=512)
    R0K = [("XN", 0), ("XN", 1)]
    R1K = [("XN", 2), ("XN", 3)]
    XNK = R0K + R1K

    KVS_K = [ARENA[:, i * 2304:(i + 1) * 2304] for i in range(2)]
    KVS_V = [ARENA[:, 4608 + i * 2304: 4608 + (i + 1) * 2304] for i in range(2)]
    PT2 = ARENA[:, 9216:9216 + 4096].rearrange("p (b j n) -> p b j n", j=2, n=512)
    ARENAK = [("HID", j) for j in range(NJ)]

    PSA = es.enter_context(nc.psum_tensor("PSA", [128, 8, 512], F32))
    PSB = [PSA[:, i, :] for i in range(8)]
    PSH = [PSA[:, i, :].bitcast(BF16) for i in range(8)]

    sems = {e: es.enter_context(nc.semaphore(f"sem_{e}")) for e in ENGS}
    dma_names = (["pc%d" % i for i in range(8)] + ["a%d" % i for i in range(4)] + ["b%d" % i for i in range(3)]
                 + ["wa0", "wa1", "wa2", "wa3", "wa4", "md0", "md1", "md2", "md3", "xl", "xl0", "xl1", "m0", "m1", "m2", "m3", "st0", "st1", "st2", "st3", "st4", "st5", "st6", "st7", "st8", "st9", "st10", "st11",
                    "kk0", "kk1", "kv0", "kv1", "kvw", "x1w", "tbl", "md"])
    dma_sems = {n: es.enter_context(nc.semaphore("dsem_" + n)) for n in dma_names}

    def mm(e, out, lhsT, rhs, start=True, stop=True, **kw):
        return e.matmul(out, lhsT, rhs, start=start, stop=stop, **kw)

    def PK(i):
        return ("PS", i)

    pc_i = [0]

    def precast(out_ap, in_ap, wkey):
        name = "pc%d" % (pc_i[0] % 8)
        pc_i[0] += 1
        P.op("pool", lambda e: e.dma_start(out=out_ap, in_=in_ap), writes=[wkey], dma=name)

    def precast_gu(i):
        for jb in range(NJB):
            for part in range(2):
                c0 = part * DFF + jb * 256
                precast(s_gu[i][jb, :, :, part * 256:(part + 1) * 256],
                        w_gu[i][:, c0:c0 + 256].rearrange("(kc p) c -> p kc c", p=128), ("s_gu", i, jb, part))

    def precast_dn(i):
        for h in range(2):
            precast(s_dn[i][h * 1408:(h + 1) * 1408, :], w_dn[i][h * 1408:(h + 1) * 1408, :], ("s_dn", i, h))

    def precast_mix():
        for b in range(5):
            n = 512 if b < 4 else 256
            precast(s_in[b, :, :, 0:n], w_in[:, b * 512:b * 512 + n].rearrange("(kc p) c -> p kc c", p=128), ("s_in", b))
        for b in range(2):
            precast(s_out[b], w_out[:, b * 512:(b + 1) * 512].rearrange("(kc p) c -> p kc c", p=128), ("s_out", b))

    m_i = [0]

    def ld(out_ap, in_ap, wkeys, rkeys=(), q="sp", dma=None):
        if dma is None:
            dma = "m%d" % (m_i[0] % 4)
            m_i[0] += 1
        return P.op(q, lambda e: e.dma_start(out=out_ap, in_=in_ap), reads=list(rkeys), writes=list(wkeys), dma=dma)

    ld(IDF[:, :], ident_f[:, :], ["IDF"])
    ld(IDB[:, :], ident_b[:, :], ["IDB"])
    ld(SPR[:, :], sp_in[:, :], ["SPR"])
    ld(LPB[:, :, :].rearrange("p a d -> p (a d)"), lam_p.rearrange("a d -> (a d)").partition_broadcast(128), ["LPB"])
    ld(QKNB[:, :, :].rearrange("p a d -> p (a d)"), qk_norm.rearrange("a d -> (a d)").partition_broadcast(128), ["QKNB"])
    ld(SGB[:, :], subln.rearrange("a d -> (a d)").partition_broadcast(128), ["SGB"])
    ld(FGB[:, :], fnorm.rearrange("a d -> (a d)").partition_broadcast(128), ["FGB"])

    precast_gu(0)
    precast_dn(0)

    P.op("pool", lambda e: e.memset(NEGH[:, :], -0.5), writes=["NEGH"])
    P.op("pool", lambda e: e.memset(ONESB[:, :], 1.0), writes=["ONESB"])
    P.op("dve", lambda e: e.memset(QT[:, :, :], 0.0), writes=[("QT", c) for c in range(8)])
    P.op("pool", lambda e: e.memset(ONESF[:, :], 1.0), writes=["ONESF"])
    P.op("pool", lambda e: e.memset(EPSC[:, :], EPS), writes=["EPSC"])
    ld(SGC[:, :], subln.rearrange("a d -> d a"), ["SGC"])
    P.op("dve", lambda e: e.tensor_scalar(SGC[:, :], SGC[:, :], 1.0 - LAM_INIT, None, ALU.mult), reads=["SGC"], writes=["SGC"])

    P.op("pe", lambda e: e.transpose(PSB[0][:, 0:40], SPR[:, :], IDF[0:40, 0:40]),
         reads=["SPR", "IDF"], writes=[PK(0)])
    P.op("dve", lambda e: e.tensor_copy(SPT[:, :], PSB[0][:, 0:40]), reads=[PK(0)], writes=["SPT"])
    P.op("act", lambda e: e.activation(SC[:, :, :].rearrange("p kc g -> p g kc"),
                                       SPT[:, 0:16].rearrange("p (g kc) -> p g kc", g=2), AF.Silu),
         reads=["SPT"], writes=["SC"])

    P.op("dve", lambda e: e.tensor_tensor(out=RT[:, 0, 0:128].rearrange("p (a d) -> p a d", a=2),
                                          in0=LPB[:, :, :].rearrange("p (a b) d -> p a b d", b=2)[:, :, 0, :],
                                          in1=LPB[:, :, :].rearrange("p (a b) d -> p a b d", b=2)[:, :, 1, :], op=ALU.mult),
         reads=["LPB"], writes=[("RT", 0)])
    P.op("dve", lambda e: e.tensor_reduce(out=SM[:, 2:4], in_=RT[:, 0, 0:128].rearrange("p (a d) -> p a d", a=2),
                                          axis=AX.X, op=ALU.add), reads=[("RT", 0)], writes=["SM_l"])
    P.op("act", lambda e: e.activation(SM[:, 4:6], SM[:, 2:4], AF.Exp), reads=["SM_l"], writes=["SM_e"])
    P.op("dve", lambda e: e.tensor_tensor(out=SM[:, 1:2], in0=SM[:, 5:6], in1=SM[:, 4:5], op=ALU.subtract),
         reads=["SM_e"], writes=["SM_d"])
    P.op("dve", lambda e: e.tensor_scalar(SM[:, 0:1], SM[:, 1:2], -LAM_INIT, None, ALU.add),
         reads=["SM_d"], writes=["NLAM"])
    P.op("dve", lambda e: e.tensor_scalar(SGB[:, :], SGB[:, :], 1.0 - LAM_INIT, None, ALU.mult),
         reads=["SGB"], writes=["SGB"])
    NLAM = SM[:, 0:1]

    ARF = ARENA[:, 0:12288].bitcast(F32)
    WAV = [XN[:, 0:2, :].rearrange("p a d -> p (a d)").rearrange("p (kc n) -> p kc n", n=256),
           XN[:, 2:4, :].rearrange("p a d -> p (a d)").rearrange("p (kc n) -> p kc n", n=256)]
    WAV += [ARF[:, i * 2048:(i + 1) * 2048].rearrange("p (kc n) -> p kc n", n=256) for i in range(3)]
    WAK = [R0K, R1K, [("HID", j) for j in range(0, 8)], [("HID", j) for j in range(8, 16)], [("HID", j) for j in range(16, 22)] + ["ARX"]]
    NWS = 4
    MB = ON[0:2, :, :].rearrange("p a n -> p (a n)")[:, 0:1024].rearrange("p (s n) -> p s n", n=256)
    BAB = ACCS[0:2, :, :].rearrange("p a n -> p (a n)").rearrange("p (s n) -> p s n", n=256)
    for blk in range(36):
        slot = blk % NWS
        c0 = blk * 256
        ld(WAV[slot], w_ada[:, c0:c0 + 256].rearrange("(kc p) c -> p kc c", p=128), WAK[slot], dma="wa%d" % slot)
        ld(BAB[:, slot, 0:256], b_ada[:, c0:c0 + 256].rearrange("a d -> (a d)").partition_broadcast(2), [("BAB", slot)])
        for kc in range(8):
            P.op("pe", (lambda e, slot=slot, kc=kc: mm(e, PSB[2 + slot][0:2, 0:256], SC[:, kc, :], WAV[slot][:, kc, :],
                                                     start=(kc == 0), stop=(kc == 7))),
                 reads=WAK[slot] + ["SC"], writes=[PK(2 + slot)])
        P.op("dve", (lambda e, slot=slot: e.tensor_tensor(out=MB[:, slot, 0:256], in0=PSB[2 + slot][0:2, 0:256],
                                                          in1=BAB[:, slot, 0:256], op=ALU.add)),
             reads=[PK(2 + slot), ("BAB", slot)], writes=[("MB", slot)])
        ld(s_mod[:, c0:c0 + 256], MB[:, slot, 0:256], ["s_mod"], rkeys=[("MB", slot)], q="act", dma="md%d" % (blk % 4))

    def load_group(g):
        for i, m in enumerate((2, 5, 8)):
            ld(GATE[:, i, :], s_mod[g, m * D:(m + 1) * D].partition_broadcast(128), [("GATE", i)], rkeys=["s_mod"])
            if m != 5:
                P.op("dve", (lambda e, i=i: e.tensor_scalar(GATE[:, i, :], GATE[:, i, :], 0.5, None, ALU.mult)),
                     reads=[("GATE", i)], writes=[("GATE", i)])
        ld(MR[:, :], s_mod[g, :].rearrange("(r p) -> r p", p=128), ["MR"], rkeys=["s_mod"])
        P.op("pe", lambda e: e.transpose(PSB[0][:, 0:72], MR[:, :], IDF[0:72, 0:72]), reads=["MR", "IDF"], writes=[PK(0)])
        P.op("dve", lambda e: e.tensor_copy(MODT[:, :, :].rearrange("p m c -> p (m c)"), PSB[0][:, 0:72]),
             reads=[PK(0)], writes=["MODT"])
        for i in range(3):
            P.op("dve", (lambda e, i=i: e.scalar_tensor_tensor(out=GS[:, i, :], in0=MODT[:, 3 * i + 1, :], scalar=1.0,
                                                                in1=SPT[:, 16 + 8 * i:24 + 8 * i], op0=ALU.add, op1=ALU.mult)),
                 reads=["MODT", "SPT"], writes=["GS"])

    a_i = [0]
    b_i = [0]

    def load_A(src_ap, rkeys, n=512):
        slot = a_i[0] % 4
        a_i[0] += 1
        P.op("sp", lambda e: e.dma_start(out=AR[:, slot, :].rearrange("p (k n) -> p k n", n=512)[:, :, 0:n], in_=src_ap[:, :, 0:n]),
             reads=list(rkeys), writes=[("A", slot)], dma="a%d" % slot)
        return AR[:, slot, :].rearrange("p (k n) -> p k n", n=512), ("A", slot)

    def load_B(i, jb, fh):
        slot = b_i[0] % 3
        b_i[0] += 1
        src = s_dn[i][jb * 256:(jb + 1) * 256, fh * 512:(fh + 1) * 512].rearrange("(jj p) f -> p jj f", p=128)
        P.op("sp", lambda e: e.dma_start(out=BR[:, slot, :].rearrange("p (jj f) -> p jj f", jj=2), in_=src),
             reads=[("s_dn", i, 0), ("s_dn", i, 1)], writes=[("B", slot)], dma="b%d" % slot)
        return BR[:, slot, :].rearrange("p (jj f) -> p jj f", jj=2), ("B", slot)

    st_i = [0]

    def store(out_ap, in_ap, rkeys, wkeys=(), q=None):
        name = "st%d" % (st_i[0] % 12)
        st_i[0] += 1
        P.op(q or STQ[0], lambda e: e.dma_start(out=out_ap, in_=in_ap), reads=list(rkeys), writes=list(wkeys), dma=name)

    def rstd_from_ss(n, scale, outcols):
        P.op("dve", lambda e: e.tensor_scalar(SS[:, 32:32 + n], SS[:, 0:n], scale, EPS, ALU.mult, ALU.add),
             reads=["SS"], writes=["VE"])
        P.op("pool", lambda e: e.tensor_tensor(out=SM[:, outcols:outcols + n], in0=SS[:, 32:32 + n], in1=NEGH[:, 0:n], op=ALU.pow),
             reads=["VE", "NEGH"], writes=["RSTD"])

    def sumsq(Xc, xk):
        for t in range(NT):
            P.op("act", (lambda e, t=t: e.activation(JUNK[:, :], Xc[:, t, :], AF.Square, accum_out=SS[:, t:t + 1])),
                 reads=[("X", xk, t)], writes=["SS"])

    def norm_stage(i):
        Xc = XB[xcur[0]]
        xk = xcur[0]
        sumsq(Xc, xk)
        rstd_from_ss(NT, 1.0 / D, 8)
        for t in range(NT):
            if t % 2 == 0:
                P.op("act", (lambda e, t=t: e.activation(XN[:, t, :], Xc[:, t, :], AF.Copy, scale=SM[:, 8 + t:9 + t])),
                     reads=[("X", xk, t), "RSTD"], writes=[("XN", t)])
            else:
                P.op("dve", (lambda e, t=t: e.tensor_scalar(XN[:, t, :], Xc[:, t, :], SM[:, 8 + t:9 + t], None, ALU.mult)),
                     reads=[("X", xk, t), "RSTD"], writes=[("XN", t)])
        for c in range(8):
            bank = c % 2
            for t in range(NT):
                P.op("pe", (lambda e, t=t, c=c, bank=bank: e.transpose(PSB[bank][:, t * 128:(t + 1) * 128],
                                                                      XN[:, t, c * 128:(c + 1) * 128], IDF[:, :])),
                     reads=[("XN", t), "IDF"], writes=[PK(bank)])
            if c % 2 == 0:
                P.op("dve", (lambda e, c=c, bank=bank, i=i: e.tensor_scalar(HT[:, c, :], PSB[bank], GS[:, i, c:c + 1],
                                                                            MODT[:, 3 * i, c:c + 1], ALU.mult, ALU.add)),
                     reads=[PK(bank), "GS", "MODT"], writes=[("HT", c)])
            else:
                P.op("act", (lambda e, c=c, bank=bank, i=i: e.activation(HT[:, c, :], PSB[bank], AF.Identity, scale=GS[:, i, c:c + 1],
                                                                         bias=MODT[:, 3 * i, c:c + 1])),
                     reads=[PK(bank), "GS", "MODT"], writes=[("HT", c)])

    def resid(gate_i, fh):
        Xc = XB[xcur[0]]
        xk = xcur[0]
        for t in range(NT):
            P.op("dve", (lambda e, t=t: e.tensor_tensor(out=RT[:, t % 2, :], in0=PSB[4 + t],
                                                        in1=GATE[:, gate_i, fh * 512:(fh + 1) * 512], op=ALU.mult)),
                 reads=[PK(4 + t), ("GATE", gate_i)], writes=[("RT", t % 2)])
            P.op("dve", (lambda e, t=t: e.tensor_tensor(out=Xc[:, t, fh * 512:(fh + 1) * 512], in0=Xc[:, t, fh * 512:(fh + 1) * 512],
                                                        in1=RT[:, t % 2, :], op=ALU.add)),
                 reads=[("RT", t % 2), ("X", xk, t)], writes=[("X", xk, t)])

    def ffn_stage(i, gate_i):
        for jb in range(NJB):
            A, akey = load_A(s_gu[i][jb], [("s_gu", i, jb, 0), ("s_gu", i, jb, 1)])
            for jj in range(2):
                j = 2 * jb + jj
                pg, pu = (j % 2) * 2, (j % 2) * 2 + 1
                for kc in range(8):
                    P.op("pe", (lambda e, kc=kc, jj=jj, pg=pg, A=A: mm(e, PSB[pg], A[:, kc, jj * 128:(jj + 1) * 128], HT[:, kc, :],
                                                                       start=(kc == 0), stop=(kc == 7))),
                         reads=[akey, ("HT", kc)], writes=[PK(pg)])
                for kc in range(8):
                    P.op("pe", (lambda e, kc=kc, jj=jj, pu=pu, A=A: mm(e, PSB[pu], A[:, kc, 256 + jj * 128:256 + (jj + 1) * 128], HT[:, kc, :],
                                                                       start=(kc == 0), stop=(kc == 7))),
                         reads=[akey, ("HT", kc)], writes=[PK(pu)])
                sgs = j % 2
                P.op("act", (lambda e, sgs=sgs, pg=pg: e.activation(SG[:, sgs, :], PSB[pg], AF.Silu)),
                     reads=[PK(pg)], writes=[("SG", sgs)])
                P.op("dve", (lambda e, sgs=sgs, pu=pu, j=j: e.tensor_tensor(out=HID[:, j, :], in0=SG[:, sgs, :], in1=PSB[pu], op=ALU.mult)),
                     reads=[("SG", sgs), PK(pu)], writes=[("HID", j)])
        for fh in range(2):
            for jb in range(NJB):
                B, bkey = load_B(i, jb, fh)
                for jj in range(2):
                    j = 2 * jb + jj
                    for t in range(NT):
                        P.op("pe", (lambda e, t=t, j=j, jj=jj, B=B: mm(e, PSB[4 + t], HID[:, j, t * 128:(t + 1) * 128], B[:, jj, :],
                                                                       start=(j == 0), stop=(j == NJ - 1))),
                             reads=[bkey, ("HID", j)], writes=[PK(4 + t)])
            resid(gate_i, fh)

    def transposes_bf(dst, dst_key, src_fn, src_keys, nch, c0=0, split=False):
        for c in range(nch):
            bank = c % 2
            for t in range(NT):
                P.op("pe", (lambda e, t=t, c=c, bank=bank: e.transpose(PSH[bank][:, t * 128:(t + 1) * 128], src_fn(t, c), IDB[:, :])),
                     reads=list(src_keys) + ["IDB"], writes=[PK(bank)])
            if split:
                for hf in range(2):
                    P.op("act", (lambda e, c=c, bank=bank, hf=hf: e.activation(dst[64 * hf:64 * hf + 64, 2 * (c0 + c) + hf, :],
                                                                              PSH[bank][64 * hf:64 * hf + 64, 0:ST], AF.Copy)),
                         reads=[PK(bank)], writes=[(dst_key, c0 + c)])
            else:
                P.op("act", (lambda e, c=c, bank=bank: e.activation(dst[:, c0 + c, :], PSH[bank][:, 0:ST], AF.Copy)),
                     reads=[PK(bank)], writes=[(dst_key, c0 + c)])

    def win_block(b):
        n = 512 if b < 4 else 256
        A, akey = load_A(s_in[b], [("s_in", b)], n)
        for t in range(NT):
            for kc in range(8):
                P.op("pe", (lambda e, t=t, kc=kc, A=A, n=n: mm(e, PSB[4 + t][:, 0:n], HT[:, kc, t * 128:(t + 1) * 128], A[:, kc, 0:n],
                                                               start=(kc == 0), stop=(kc == 7))),
                     reads=[akey, ("HT", kc)], writes=[PK(4 + t)])

    def rope(src_fn, src_keys, nh, dst_fn, dst_keys, t, addview=None):
        P.op("dve", (lambda e: e.tensor_tensor(out=RT[:, 0, 0:nh * 64].rearrange("p (h d) -> p h d", d=64),
                                               in0=src_fn().rearrange("p (h d) -> p h d", d=64),
                                               in1=RC[:, t, :].unsqueeze(1).broadcast_to([128, nh, 64]), op=ALU.mult)),
             reads=list(src_keys) + ["RC"], writes=[("RT", 0)])
        for b in range(2):
            P.op("dve", (lambda e, b=b: e.tensor_tensor(
                out=RT[:, 1, 0:nh * 64].rearrange("p (h a b i) -> p h a b i", a=2, b=2, i=16)[:, :, :, b, :],
                in0=src_fn().rearrange("p (h a b i) -> p h a b i", a=2, b=2, i=16)[:, :, :, 1 - b, :],
                in1=RS[:, t, :].rearrange("p (a b i) -> p a b i", a=2, b=2)[:, :, b, :].unsqueeze(1).broadcast_to([128, nh, 2, 16]),
                op=ALU.mult)),
                 reads=list(src_keys) + ["RS"], writes=[("RT", 1)])
        av = addview if addview is not None else (lambda a: a.rearrange("p (h d) -> p h d", d=64))
        P.op("dve", (lambda e: e.tensor_tensor(out=dst_fn(), in0=av(RT[:, 0, 0:nh * 64]),
                                                in1=av(RT[:, 1, 0:nh * 64]), op=ALU.add)),
             reads=[("RT", 0), ("RT", 1)], writes=list(dst_keys))

    def headnorm(R, Rk, col0, nh, gi):
        for t in range(NT):
            P.op("act", (lambda e, t=t: e.activation(RT[:, 0, 0:nh * 64], R[:, t, col0:col0 + nh * 64], AF.Square)),
                 reads=list(Rk), writes=[("RT", 0)])
            P.op("dve", (lambda e, t=t: e.tensor_reduce(out=SS[:, t * nh:(t + 1) * nh],
                                                        in_=RT[:, 0, 0:nh * 64].rearrange("p (h d) -> p h d", d=64),
                                                        axis=AX.X, op=ALU.add)),
                 reads=[("RT", 0)], writes=["SS"])
        rstd_from_ss(NT * nh, 1.0 / 64, 16)
        for t in range(NT):
            P.op("dve", (lambda e, t=t: e.tensor_tensor(
                out=R[:, t, col0:col0 + nh * 64].rearrange("p (h d) -> p h d", d=64),
                in0=R[:, t, col0:col0 + nh * 64].rearrange("p (h d) -> p h d", d=64),
                in1=SM[:, 16 + t * nh:16 + (t + 1) * nh].unsqueeze(2).broadcast_to([128, nh, 64]), op=ALU.mult)),
                 reads=list(Rk) + ["RSTD"], writes=list(Rk))
            P.op("dve", (lambda e, t=t: e.tensor_tensor(
                out=R[:, t, col0:col0 + nh * 64].rearrange("p (h d) -> p h d", d=64),
                in0=R[:, t, col0:col0 + nh * 64].rearrange("p (h d) -> p h d", d=64),
                in1=QKNB[:, gi, :].unsqueeze(1).broadcast_to([128, nh, 64]), op=ALU.mult)),
                 reads=list(Rk) + ["QKNB"], writes=list(Rk))

    def gq_dst(t):
        return QB[:, t, :].rearrange("p (g n d) -> p n g d", g=4, n=2)

    def mix_q(sample):
        win_block(0)
        for t in range(NT):
            if sample:
                rope(lambda t=t: PSB[4 + t], [PK(4 + t)], 8, lambda t=t: QB[:, t, :].rearrange("p (h d) -> p h d", d=64), ["QB"], t)
            else:
                P.op("act", (lambda e, t=t: e.activation(QB[:, t, :], PSB[4 + t], AF.Copy)), reads=[PK(4 + t)], writes=["QB"])
        transposes_bf(QT, "QT", lambda t, c: QB[:, t, c * 128:(c + 1) * 128], ["QB"], 4, 0, split=True)
        win_block(3)
        for t in range(NT):
            P.op("dve", (lambda e, t=t: e.tensor_copy(R1[:, t, :], PSB[4 + t])), reads=[PK(4 + t)], writes=R1K)
        headnorm(R1, R1K, 0, 8, 0)
        for t in range(NT):
            if sample:
                rope(lambda t=t: R1[:, t, :], R1K, 8, lambda t=t: gq_dst(t), ["QB"], t,
                     addview=lambda a: a.rearrange("p (n g d) -> p n g d", n=2, g=4))
            else:
                P.op("dve", (lambda e, t=t: e.tensor_copy(gq_dst(t), R1[:, t, :].rearrange("p (n g d) -> p n g d", n=2, g=4))),
                     reads=R1K, writes=["QB"])
        transposes_bf(QT, "QT", lambda t, c: QB[:, t, c * 128:(c + 1) * 128], ["QB"], 4, 4, split=True)

    def mix_kv(sample, tok0, kt0):
        win_block(1)
        for t in range(NT):
            if sample:
                rope(lambda t=t: PSB[4 + t], [PK(4 + t)], 8, lambda t=t: QB[:, t, :].rearrange("p (h d) -> p h d", d=64), ["QB"], t)
            else:
                P.op("act", (lambda e, t=t: e.activation(R0[:, t, :], PSB[4 + t], AF.Copy)), reads=[PK(4 + t)], writes=R0K)
                P.op("dve", (lambda e, t=t: e.tensor_copy(QB[:, t, :], R0[:, t, :])), reads=R0K, writes=["QB"])
        if not sample and KVSTOP[0] != 11:
            store(o_dk[tok0:tok0 + ST, :].rearrange("(t p) f -> p t f", p=128), R0, R0K)
        if KVSTOP[0] == 11:
            return
        transposes_bf(KT, "KT", lambda t, c: QB[:, t, c * 128:(c + 1) * 128], ["QB"], 4, 0)
        if KVSTOP[0] == 1:
            return
        win_block(2)
        for t in range(NT):
            if not sample:
                P.op("act", (lambda e, t=t: e.activation(R1[:, t, :], PSB[4 + t], AF.Copy)), reads=[PK(4 + t)], writes=R1K)
                P.op("dve", (lambda e, t=t: e.tensor_copy(VD[:, :, t, :], R1[:, t, :].rearrange("p (h d) -> p h d", d=128))),
                     reads=R1K, writes=["VD"])
            else:
                P.op("dve", (lambda e, t=t: e.tensor_copy(VD[:, :, t, :], PSB[4 + t].rearrange("p (h d) -> p h d", d=128))),
                     reads=[PK(4 + t)], writes=["VD"])
        if not sample:
            store(o_dv[tok0:tok0 + ST, :].rearrange("(t p) f -> p t f", p=128), R1, R1K)
        if KVSTOP[0] == 2:
            return
        win_block(4)
        for t in range(NT):
            P.op("dve", (lambda e, t=t: e.tensor_copy(R0[:, t, 0:256], PSB[4 + t][:, 0:256])), reads=[PK(4 + t)], writes=R0K)
        headnorm(R0, R0K, 0, 2, 1)
        for t in range(NT):
            if sample:
                rope(lambda t=t: R0[:, t, 0:128], R0K, 2, lambda t=t: QB[:, t, 0:128].rearrange("p (h d) -> p h d", d=64), ["QB"], t)
            else:
                P.op("act", (lambda e, t=t: e.activation(QB[:, t, 0:128], R0[:, t, 0:128], AF.Copy)), reads=R0K, writes=["QB"])
            P.op("dve", (lambda e, t=t: e.tensor_copy(VG[:, :, t, :].rearrange("p n (r d) -> p n r d", r=2),
                                                      R0[:, t, 128:256].rearrange("p (h d) -> p h d", d=64).unsqueeze(2).broadcast_to([128, 2, 2, 64]))),
                 reads=R0K, writes=["VG"])
        if not sample:
            store(o_gk[tok0:tok0 + ST, :].rearrange("(t p) f -> p t f", p=128), R0[:, :, 0:128], R0K)
            store(o_gv[tok0:tok0 + ST, :].rearrange("(t p) f -> p t f", p=128), R0[:, :, 128:256], R0K)
        transposes_bf(KT, "KT", lambda t, c: QB[:, t, 0:128], ["QB"], 1, 4)
        if sample:
            store_kv(kt0)

    def store_kv(kt0):
        store(s_kt[:, :, kt0 * 128:kt0 * 128 + ST].rearrange("c p n -> p c n"), KT[:, :, :], [("KT", c) for c in range(5)], [("s_kt", kt0)], q="sp")
        store(s_vd[:, :, kt0:kt0 + NT, :].rearrange("h p t e -> p h (t e)"), VD[:, :, :, :].rearrange("p h t e -> p h (t e)"),
              ["VD"], [("s_vd", h, kt0) for h in range(4)], q="sp")
        store(s_vg[:, :, kt0:kt0 + NT, :].rearrange("n p t e -> p n (t e)"), VG[:, :, :, :].rearrange("p n t e -> p n (t e)"),
              ["VG"], [("s_vg", n, kt0) for n in range(2)], q="sp")

    s_i = [0]
    p_i = [0]

    pending = [None]
    ab_i = [0]

    def attn_unit(kind, u, subs, q0, nq, ktiles):
        ab = ab_i[0] % 2
        ab_i[0] += 1
        attn_main(kind, u, subs, q0, nq, ktiles, ab)
        if pending[0] is not None:
            attn_norm(*pending[0])
        pending[0] = (kind, u, subs, q0, nq, ab)

    def attn_flush():
        if pending[0] is not None:
            attn_norm(*pending[0])
        pending[0] = None

    def attn_main(kind, u, subs, q0, nq, ktiles, ab):
        nk = len(ktiles)
        info = {}
        LA = 1
        ACC = ACCS if ab == 0 else ACCS_B
        ak = "ACCS%d" % ab
        ob = 4 + 2 * ab

        def emit_qk(ki):
            kfn, kkeys, v_ap, vkeys = ktiles[ki]
            r = s_i[0] % 2
            s_i[0] += 1
            for j in range(2):
                if kind == "diff":
                    r0, chunk = 64 * subs[j], u
                else:
                    r0, chunk = 64 * u, 4 + subs[j]
                P.op("pe", (lambda e, j=j, r0=r0, chunk=chunk: mm(e, PSB[2 * r + j][:, 0:nq], kfn(),
                                                                  QT[:, 2 * chunk + r0 // 64, q0:q0 + nq])),
                     reads=list(kkeys) + [("QT", chunk)], writes=[PK(2 * r + j)])
            slot = p_i[0] % 4
            p_i[0] += 1
            P.op("act", (lambda e: e.activation(PT2[:, slot, :, 0:nq], PSA[:, 2 * r:2 * r + 2, 0:nq], AF.Exp, scale=0.125)),
                 reads=[PK(2 * r), PK(2 * r + 1)],
                 writes=[("PT", slot)] + ([("HID", jj) for jj in range(18, NJ)] if ki == 0 else []))
            if ki == 0:
                P.op("dve", (lambda e: e.tensor_copy(ACC[:, :, 0:nq], PT2[:, slot, :, 0:nq])), reads=[("PT", slot)], writes=[ak, ak + "b"])
            else:
                cs = (nq * 3) // 4
                P.op("dve", (lambda e: e.tensor_tensor(out=ACC[:, :, 0:cs], in0=ACC[:, :, 0:cs], in1=PT2[:, slot, :, 0:cs], op=ALU.add)),
                     reads=[("PT", slot), ak], writes=[ak])
                P.op("pool", (lambda e: e.tensor_tensor(out=ACC[:, :, cs:nq], in0=ACC[:, :, cs:nq], in1=PT2[:, slot, :, cs:nq], op=ALU.add)),
                     reads=[("PT", slot), ak + "b"], writes=[ak + "b"])
            info[ki] = slot

        def emit_pv(ki):
            kfn, kkeys, v_ap, vkeys = ktiles[ki]
            slot = info[ki]
            for j in range(2):
                P.op("pe", (lambda e, j=j: mm(e, PSB[ob + j][:, 0:nq], v_ap, PT2[:, slot, j, 0:nq], start=(ki == 0), stop=(ki == nk - 1))),
                     reads=[("PT", slot)] + list(vkeys), writes=[PK(ob + j)])

        for kk in range(nk + LA):
            if kk < nk:
                emit_qk(kk)
            if kk - LA >= 0:
                emit_pv(kk - LA)

    def attn_norm(kind, u, subs, q0, nq, ab):
        ACC = ACCS if ab == 0 else ACCS_B
        ak = "ACCS%d" % ab
        ob = 4 + 2 * ab
        r = s_i[0] % 2
        s_i[0] += 1
        for j in range(2):
            P.op("pe", (lambda e, j=j: mm(e, PSB[2 * r + j][:, 0:nq], ONESF[:, :], ACC[:, j, 0:nq])), reads=[ak, ak + "b", "ONESF"], writes=[PK(2 * r + j)])
        for j in range(2):
            P.op("dve", (lambda e, j=j: e.reciprocal(ON[:, 2, 0:nq], PSB[2 * r + j][:, 0:nq])), reads=[PK(2 * r + j)], writes=[("ON", 2)])
            if kind == "diff":
                P.op("dve", (lambda e, j=j: e.tensor_tensor(out=ON[:, j, 0:nq], in0=PSB[ob + j][:, 0:nq], in1=ON[:, 2, 0:nq], op=ALU.mult)),
                     reads=[PK(ob + j), ("ON", 2)], writes=[("ON", j)])
            else:
                hq = 4 * u + subs[j]
                r0, chunk = 64 * (hq % 2), 4 + hq // 2
                P.op("dve", (lambda e, j=j, r0=r0, chunk=chunk: e.tensor_tensor(
                    out=OT[r0:r0 + 64, chunk, q0:q0 + nq], in0=PSB[ob + j][r0:r0 + 64, 0:nq], in1=ON[r0:r0 + 64, 2, 0:nq], op=ALU.mult)),
                     reads=[PK(ob + j), ("ON", 2)], writes=[("OT", chunk)])
        if kind == "gqa":
            return
        P.op("dve", lambda e: e.scalar_tensor_tensor(out=ON[:, 0, 0:nq], in0=ON[:, 1, 0:nq], scalar=NLAM, in1=ON[:, 0, 0:nq],
                                                     op0=ALU.mult, op1=ALU.add),
             reads=[("ON", 0), ("ON", 1), "NLAM"], writes=[("ON", 0)])
        P.op("act", lambda e: e.activation(ON[:, 2, 0:nq], ON[:, 0, 0:nq], AF.Square), reads=[("ON", 0)], writes=[("ON", 2)])
        r2 = s_i[0] % 2
        s_i[0] += 1
        P.op("pe", lambda e: mm(e, PSB[2 * r2][:, 0:nq], ONESF[:, :], ON[:, 2, 0:nq]), reads=[("ON", 2), "ONESF"], writes=[PK(2 * r2)])
        P.op("act", lambda e: e.activation(ON[:, 1, 0:nq], PSB[2 * r2][:, 0:nq], AF.Ln, scale=1.0 / 128, bias=EPSC[:, 0:1]),
             reads=[PK(2 * r2), "EPSC"], writes=[("ON", 1)])
        P.op("act", lambda e: e.activation(ON[:, 1, 0:nq], ON[:, 1, 0:nq], AF.Exp, scale=-0.5), reads=[("ON", 1)], writes=[("ON", 1)])
        P.op("dve", lambda e: e.scalar_tensor_tensor(out=OT[:, u, q0:q0 + nq], in0=ON[:, 0, 0:nq], scalar=SGC[:, 0:1], in1=ON[:, 1, 0:nq],
                                                     op0=ALU.mult, op1=ALU.mult),
             reads=[("ON", 0), ("ON", 1), "SGC"], writes=[("OT", u)])

    def attn_prompt():
        for bb in range(2):
            for h in range(4):
                kts = []
                for t in (2 * bb, 2 * bb + 1):
                    kts.append(((lambda t=t, h=h: KT[:, h, t * 128:(t + 1) * 128]), [("KT", h)],
                                VD[:, h, t, :], ["VD"]))
                attn_unit("diff", h, (0, 1), bb * 256, 256, kts)
            for n in range(2):
                for gp in range(2):
                    kts = []
                    for t in (2 * bb, 2 * bb + 1):
                        kts.append(((lambda t=t: KT[:, 4, t * 128:(t + 1) * 128]), [("KT", 4)],
                                    VG[:, n, t, :], ["VG"]))
                    attn_unit("gqa", n, (2 * gp, 2 * gp + 1), bb * 256, 256, kts)
        attn_flush()

    kv_i = [0]

    def attn_sample():
        units = [("diff", h, (0, 1)) for h in range(4)] + [("gqa", n, (2 * gp, 2 * gp + 1)) for n in range(2) for gp in range(2)]
        for kind, u, subs in units:
            kts = []
            for half in range(2):
                buf = kv_i[0] % 2
                kv_i[0] += 1
                chunk = u if kind == "diff" else 4
                P.op("sp", (lambda e, buf=buf, chunk=chunk, half=half: e.dma_start(
                    out=KVS_K[buf], in_=s_kt[chunk, :, half * 2304:(half + 1) * 2304])),
                     reads=[("s_kt", k0) for k0 in range(0, NKT, NT)], writes=[("KVK", buf)] + [("HID", j) for j in ((0, 1, 2, 3, 4) if buf == 0 else (4, 5, 6, 7, 8))],
                     dma="kk%d" % buf)
                vs = s_vd if kind == "diff" else s_vg
                vsrc = vs[u, :, half * 18:(half + 1) * 18, :].rearrange("p k e -> p (k e)")
                P.op("sp", (lambda e, buf=buf, vsrc=vsrc: e.dma_start(out=KVS_V[buf], in_=vsrc)),
                     reads=[("s_vd", hh, k0) for hh in range(4) for k0 in range(0, NKT, NT)] + [("s_vg", nn, k0) for nn in range(2) for k0 in range(0, NKT, NT)], writes=[("KVV", buf)] + [("HID", j) for j in ((9, 10, 11, 12, 13) if buf == 0 else (13, 14, 15, 16, 17))],
                     dma="kv%d" % buf)
                for k in range(18):
                    kts.append(((lambda buf=buf, k=k: KVS_K[buf][:, k * 128:(k + 1) * 128]), [("KVK", buf)],
                                KVS_V[buf][:, k * 128:(k + 1) * 128], [("KVV", buf)]))
            attn_unit(kind, u, subs, 0, ST, kts)
        attn_flush()

    def out_proj():
        for fh in range(2):
            A, akey = load_A(s_out[fh], [("s_out", fh)])
            for t in range(NT):
                for kc in range(8):
                    P.op("pe", (lambda e, t=t, kc=kc, A=A: mm(e, PSB[4 + t], OT[:, kc, t * 128:(t + 1) * 128], A[:, kc, :],
                                                              start=(kc == 0), stop=(kc == 7))),
                         reads=[akey, ("OT", kc)], writes=[PK(4 + t)])
            resid(1, fh)

    def final_norm(y, tok0):
        Xc = XB[xcur[0]]
        xk = xcur[0]
        sumsq(Xc, xk)
        rstd_from_ss(NT, 1.0 / D, 8)
        for t in range(NT):
            P.op("dve", (lambda e, t=t: e.scalar_tensor_tensor(out=XN[:, t, :], in0=Xc[:, t, :], scalar=SM[:, 8 + t:9 + t],
                                                                in1=FGB[:, :], op0=ALU.mult, op1=ALU.mult)),
                 reads=[("X", xk, t), "RSTD", "FGB"], writes=[("XN", t)])
        store(y[tok0:tok0 + ST, :].rearrange("(t p) f -> p t f", p=128), XN[:, :, :], XNK)

    def load_x(src, tok0, buf):
        Xc = XB[buf]
        P.op("sp", lambda e: e.dma_start(out=Xc[:, :, :], in_=src[tok0:tok0 + ST, :].rearrange("(t p) f -> p t f", p=128)),
             reads=[("s_x1", tok0 // ST)] if src is s_x1 else [], writes=[("X", buf, t) for t in range(NT)], dma="xl%d" % buf)

    def load_rope(tok0):
        P.op("sp", lambda e: e.dma_start(out=RC[:, :, :], in_=rope_c[tok0:tok0 + ST, :].rearrange("(t p) f -> p t f", p=128)),
             writes=["RC"], dma="tbl")
        P.op("sp", lambda e: e.dma_start(out=RS[:, :, :], in_=rope_s[tok0:tok0 + ST, :].rearrange("(t p) f -> p t f", p=128)),
             writes=["RS"], dma="tbl")

    first = [True]

    def maybe(fn):
        if first[0]:
            fn()

    tiles = []
    if do_prompt:
        tiles += [("prompt", si, xp, si * ST) for si in range(n_prompt_st)]
    if do_sample:
        tiles += [("s1", si, xs, si * ST) for si in range(n_s1)]
        tiles += [("s2", qi, s_x1, qi * ST) for qi in range(n_s2)]

    def prefetch(k):
        if k < len(tiles):
            ph, idx, src, off = tiles[k]
            load_x(src, off, k % 2)

    def sample_cache_prep():
        load_group(1)
        P.op("sp", lambda e: e.dma_start(out=R0, in_=cdk.rearrange("(t p) f -> p t f", p=128)), writes=R0K, dma="xl")
        for t in range(NT):
            P.op("act", (lambda e, t=t: e.activation(QB[:, t, :], R0[:, t, :], AF.Copy)), reads=R0K, writes=["QB"])
        transposes_bf(KT, "KT", lambda t, c: QB[:, t, c * 128:(c + 1) * 128], ["QB"], 4, 0)
        P.op("sp", lambda e: e.dma_start(out=R1, in_=cdv.rearrange("(t p) f -> p t f", p=128)), writes=R1K, dma="xl")
        for t in range(NT):
            P.op("dve", (lambda e, t=t: e.tensor_copy(VD[:, :, t, :], R1[:, t, :].rearrange("p (h d) -> p h d", d=128))),
                 reads=R1K, writes=["VD"])
        P.op("sp", lambda e: e.dma_start(out=R0[:, :, 0:128], in_=cgk.rearrange("(t p) f -> p t f", p=128)), writes=R0K, dma="xl")
        P.op("sp", lambda e: e.dma_start(out=R0[:, :, 128:256], in_=cgv.rearrange("(t p) f -> p t f", p=128)), writes=R0K, dma="xl")
        for t in range(NT):
            P.op("act", (lambda e, t=t: e.activation(QB[:, t, 0:128], R0[:, t, 0:128], AF.Copy)), reads=R0K, writes=["QB"])
            P.op("dve", (lambda e, t=t: e.tensor_copy(VG[:, :, t, :].rearrange("p n (r d) -> p n r d", r=2),
                                                      R0[:, t, 128:256].rearrange("p (h d) -> p h d", d=64).unsqueeze(2).broadcast_to([128, 2, 2, 64]))),
                 reads=R0K, writes=["VG"])
        transposes_bf(KT, "KT", lambda t, c: QB[:, t, 0:128], ["QB"], 1, 4)
        store_kv(0)

    prefetch(0)
    seen_phase = set()
    for k, (ph, idx, src, off) in enumerate(tiles):
        xcur[0] = k % 2
        if ph == "prompt" and "prompt" not in seen_phase:
            load_group(0)
        if ph == "s1" and "s1" not in seen_phase:
            if first[0]:
                precast_mix()
                precast_gu(1)
                precast_dn(1)
                first[0] = False
            sample_cache_prep()
        if ph == "s2" and "s1" not in seen_phase and "s2" not in seen_phase:
            load_group(1)
        seen_phase.add(ph)
        if ph == "prompt":
            tok0 = off
            steps = [lambda: norm_stage(0), lambda: (prefetch(k + 1), maybe(precast_mix)), lambda: ffn_stage(0, 0),
                     lambda: norm_stage(1), lambda: maybe(lambda: (precast_gu(1), precast_dn(1))), lambda: mix_q(False),
                     lambda: mix_kv(False, tok0, 0), attn_prompt, out_proj, lambda: norm_stage(2), lambda: ffn_stage(1, 2),
                     lambda: final_norm(y_p, tok0)]
            for kk, fn in enumerate(steps):
                if kk < stop_after:
                    fn()
            first[0] = False
        elif ph == "s1":
            load_rope(off)
            norm_stage(0)
            prefetch(k + 1)
            ffn_stage(0, 0)
            if idx >= 4:
                Xc = XB[xcur[0]]
                store(s_x1[(idx - 4) * ST:(idx - 3) * ST, :].rearrange("(t p) f -> p t f", p=128), Xc[:, :, :],
                      [("X", xcur[0], t) for t in range(NT)], [("s_x1", idx - 4)], q="sp")
            norm_stage(1)
            mix_kv(True, off, 4 + idx * NT)
        else:
            load_rope(OWN_TOK + off)
            norm_stage(1)
            prefetch(k + 1)
            mix_q(True)
            attn_sample()
            out_proj()
            norm_stage(2)
            ffn_stage(1, 2)
            final_norm(y_s, off)
    if first[0] and not tiles:
        precast_mix()
        precast_gu(1)
        precast_dn(1)

    P.finalize()
    with nc.Block() as block:
        P.emit(block, sems, dma_sems)
    es.close()
    return nc


def _rope_tables():
    rows = SAMPLE_TOK // 64
    row = np.repeat(np.arange(rows), 64)
    col = np.tile(np.arange(64), rows)
    half = 32
    inv = (10000.0 ** (-np.arange(0, half, 2, dtype=np.float32) / half)).astype(np.float32)
    ang = np.stack([row[:, None].astype(np.float32) * inv, col[:, None].astype(np.float32) * inv], axis=1)
    c = np.cos(ang).astype(np.float32)
    s = np.sin(ang).astype(np.float32)
    C2 = np.stack([c, c], axis=2).reshape(SAMPLE_TOK, 64)
    S2 = np.stack([-s, s], axis=2).reshape(SAMPLE_TOK, 64)
    return np.ascontiguousarray(C2), np.ascontiguousarray(S2)


def make_in_maps(inp):
    f = lambda a: np.ascontiguousarray(np.asarray(a, dtype=np.float32))
    C2, S2 = _rope_tables()
    ident_f = np.eye(128, dtype=np.float32)
    ident_b = np.eye(128, dtype=np.float32).astype(ml_dtypes.bfloat16)
    shared = {
        "w_ada": f(inp["w_ada"][0]), "b_ada": f(inp["b_ada"]).reshape(1, 9 * D),
        "w_ff1_gu": f(inp["w_ff1_gu"][0]), "w_ff2_gu": f(inp["w_ff2_gu"][0]),
        "w_ff1_down": f(inp["w_ff1_down"][0]), "w_ff2_down": f(inp["w_ff2_down"][0]),
        "w_in": f(inp["w_in"][0]), "w_out": f(inp["w_out"][0]),
        "qk_norm": np.concatenate([f(inp["q_norm"][0]), f(inp["k_norm"][0])]).reshape(1, 128),
        "lam_p": np.concatenate([f(inp["lambda_q1"][0]), f(inp["lambda_k1"][0]),
                                 f(inp["lambda_q2"][0]), f(inp["lambda_k2"][0])]).reshape(1, 256),
        "subln": f(inp["subln"]).reshape(1, 128), "final_norm": f(inp["final_norm"]).reshape(1, D),
        "ident_f": ident_f, "ident_b": ident_b,
    }
    norms = np.concatenate([f(inp["norm_ff1"][0]), f(inp["norm_mix"][0]), f(inp["norm_ff2"][0])]).reshape(24, 128)
    xp_all = f(inp["x_prompt"])
    xs_all = f(inp["x_sample"])
    maps = []
    for core in range(8):
        b, half = core // 2, core % 2
        order = np.concatenate([np.arange((1 - half) * OWN_TOK, (2 - half) * OWN_TOK),
                                np.arange(half * OWN_TOK, (half + 1) * OWN_TOK)])
        cv = np.stack([f(inp["c_ctx"]), f(inp["c"])[b]]).reshape(16, 128)
        m = dict(shared)
        m["xp"] = np.ascontiguousarray(xp_all[4 * core:4 * core + 4].reshape(PROMPT_TOK, D))
        m["xs"] = np.ascontiguousarray(xs_all[b][order])
        m["sparams"] = np.ascontiguousarray(np.concatenate([cv, norms], axis=0))
        m["cdk"] = f(inp["cache_diff_k"][b, 0]).reshape(512, 512)
        m["cdv"] = f(inp["cache_diff_v"][b, 0]).reshape(512, 512)
        m["cgk"] = f(inp["cache_gqa_k"][b, 0]).reshape(512, 128)
        m["cgv"] = f(inp["cache_gqa_v"][b, 0]).reshape(512, 128)
        m["rope_c"] = np.ascontiguousarray(C2[order])
        m["rope_s"] = np.ascontiguousarray(S2[order])
        maps.append(m)
    return maps


def kernel(**inp):
    nc = build_nc()
    maps = make_in_maps(inp)
    res = run_bass_kernel_spmd(nc, maps, core_ids=list(range(8)))
    R = res.results
    y_p = np.concatenate([r["y_p"].reshape(4, 256, D) for r in R], axis=0)
    y_s = np.zeros((4, SAMPLE_TOK, D), np.float32)
    for core, r in enumerate(R):
        b, half = core // 2, core % 2
        y_s[b, half * OWN_TOK:(half + 1) * OWN_TOK] = r["y_s"]
    ndk = np.concatenate([r["o_dk"].reshape(4, 1, 256, 4, 2, 64) for r in R], axis=0)
    ndv = np.concatenate([r["o_dv"].reshape(4, 1, 256, 4, 128) for r in R], axis=0)
    ngk = np.concatenate([r["o_gk"].reshape(4, 1, 256, 2, 64) for r in R], axis=0)
    ngv = np.concatenate([r["o_gv"].reshape(4, 1, 256, 2, 64) for r in R], axis=0)
    return (y_p.astype(np.float32), y_s, ndk.astype(np.float32), ndv.astype(np.float32),
            ngk.astype(np.float32), ngv.astype(np.float32))
```
